# Optimizing a Trainium2 kernel written in Bass

```python
import math
import jax, jax.numpy as jnp
from jax import lax
import numpy as np

D_MODEL = 1024
BATCH = 4
SEQ = 4096
DEPTH = 4
DEC_BATCH = 128
DEC_SEQ = 4
PAST_LEN = 2048
PAGE_SIZE = 128

A_HEADS = 4
A_DK = 128
A_DV = 128
A_CONV = 4
A_CHUNK = 64
B_GROUPS = ((128, 1), (512, 4), (2048, 16))
B_HEADS = 4
B_HD = 64
B_ROT = B_HD // 4
ROPE_THETA = 500000.0
C_WIDTH = 512
C_BLOCKS = 8
C_BW = C_WIDTH // C_BLOCKS
C_CONV = 4
C_POW = 8.0
D_FF = ((8 * D_MODEL // 3 + 255) // 256) * 256
F_CONV = 3
DN_ALPHA = (2 * DEPTH) ** 0.25
DN_BETA = (8 * DEPTH) ** -0.25
LN_EPS = 1e-5
NORM_EPS = 1e-6

A_QK = A_HEADS * A_DK
A_VW = A_HEADS * A_DV
A_QKV = 2 * A_QK + A_VW
B_W = len(B_GROUPS) * B_HEADS * B_HD
B_OUT = B_HEADS * B_HD
IN_SIZES = (A_QKV, A_HEADS, A_HEADS, A_VW, 3 * B_W, C_WIDTH, C_WIDTH, 3 * D_MODEL)
IN_SPLITS = tuple(int(s) for s in np.cumsum(IN_SIZES)[:-1])
N_IN = sum(IN_SIZES)

kernel_name = 'hybrid_gdn_dilated_rglru_deepnorm_step'

F32 = jnp.float32


def layer_norm(x, g, b):
    xf = x.astype(F32)
    mu = jnp.mean(xf, -1, keepdims=True)
    var = jnp.mean(jnp.square(xf - mu), -1, keepdims=True)
    return ((xf - mu) * lax.rsqrt(var + LN_EPS) * g + b).astype(x.dtype)


def l2norm(x):
    return x * lax.rsqrt(jnp.sum(x * x, -1, keepdims=True) + NORM_EPS)


def causal_dwconv(x, buf, w, b=None):
    K, L = w.shape[0], x.shape[1]
    xx = jnp.concatenate([buf.astype(x.dtype), x], axis=1)
    y = xx[:, 0:L] * w[0]
    for j in range(1, K):
        y = y + xx[:, j:j + L] * w[j]
    if b is not None:
        y = y + b
    return y, xx[:, L:]


def rope_partial(x, pos):
    half = B_ROT // 2
    inv = ROPE_THETA ** (-jnp.arange(half, dtype=F32) / half)
    ang = pos.astype(F32)[:, None] * inv
    shp = (1, pos.shape[0]) + (1,) * (x.ndim - 3) + (half,)
    cos, sin = jnp.cos(ang).reshape(shp), jnp.sin(ang).reshape(shp)
    x1 = x[..., :half].astype(F32)
    x2 = x[..., half:B_ROT].astype(F32)
    rot = jnp.concatenate([x1 * cos - x2 * sin, x2 * cos + x1 * sin], -1).astype(x.dtype)
    return jnp.concatenate([rot, x[..., B_ROT:]], -1)


def chunk_gated_delta(q, k, v, g, beta, S0):
    Bn, L, H, _ = q.shape
    Dv = v.shape[-1]
    C = A_CHUNK
    n = -(-L // C)
    P = n * C - L

    def chunks(t):
        t = jnp.pad(t, ((0, 0), (0, P)) + ((0, 0),) * (t.ndim - 2))
        t = t.reshape((Bn, n, C) + t.shape[2:])
        return jnp.moveaxis(t, 3, 1)

    q, k, v, g, beta = (chunks(t) for t in (q, k, v, g, beta))
    g = jnp.cumsum(g, axis=-1)
    tril = jnp.tril(jnp.ones((C, C), bool))
    strict = jnp.tril(jnp.ones((C, C), bool), -1)
    decay = jnp.where(tril, jnp.exp(jnp.where(tril, g[..., :, None] - g[..., None, :], 0.0)), 0.0)
    kb = k * beta[..., None]
    A = jnp.where(strict, jnp.einsum('bhncd,bhnjd->bhncj', kb, k) * decay, 0.0)
    eye = jnp.eye(C, dtype=F32)
    T = lax.linalg.triangular_solve(eye + A, jnp.broadcast_to(eye, A.shape),
                                    left_side=True, lower=True, unit_diagonal=True)
    u = T @ (v * beta[..., None])
    w = T @ (kb * jnp.exp(g)[..., None])
    qk = jnp.where(tril, jnp.einsum('bhncd,bhnjd->bhncj', q, k) * decay, 0.0)
    gl = g[..., -1]
    k_dec = k * jnp.exp(gl[..., None] - g)[..., None]
    q_dec = q * jnp.exp(g)[..., None]

    def step(S, xs):
        q_i, k_i, u_i, w_i, a_i, gl_i = xs
        v_new = u_i - w_i @ S
        o = q_i @ S + a_i @ v_new
        S = S * jnp.exp(gl_i)[..., None, None] + jnp.swapaxes(k_i, -1, -2) @ v_new
        return S, o

    xs = tuple(jnp.moveaxis(t, 2, 0) for t in (q_dec, k_dec, u, w, qk, gl))
    S, o = lax.scan(step, S0, xs)
    o = jnp.moveaxis(jnp.moveaxis(o, 0, 2), 1, 3).reshape(Bn, n * C, H, Dv)[:, :L]
    return o, S


def gated_delta_branch(qkv, b_logit, a_logit, z, conv_buf, S0, conv_w, A_log, dt_bias, norm_w):
    Bn, L, _ = qkv.shape
    y, new_buf = causal_dwconv(qkv, conv_buf, conv_w)
    y = jax.nn.silu(y.astype(F32))
    q, k, v = jnp.split(y, [A_QK, 2 * A_QK], axis=-1)
    q = l2norm(q.reshape(Bn, L, A_HEADS, A_DK)) * (A_DK ** -0.5)
    k = l2norm(k.reshape(Bn, L, A_HEADS, A_DK))
    v = v.reshape(Bn, L, A_HEADS, A_DV)
    beta = jax.nn.sigmoid(b_logit.astype(F32))
    g = -jnp.exp(A_log.astype(F32)) * jax.nn.softplus(a_logit.astype(F32) + dt_bias.astype(F32))
    o, S = chunk_gated_delta(q, k, v, g, beta, S0.astype(F32))
    o = o * lax.rsqrt(jnp.mean(o * o, -1, keepdims=True) + NORM_EPS) * norm_w.astype(F32)
    o = o * jax.nn.silu(z.astype(F32).reshape(Bn, L, A_HEADS, A_DV))
    return o.reshape(Bn, L, A_VW).astype(qkv.dtype), new_buf, S.astype(qkv.dtype)


def dilated_prompt(q, k, v, window, dil):
    Bn, S, H, D = q.shape
    nk = window // dil
    n = S // dil
    nblk = -(-n // nk)
    npad = nblk * nk - n

    def strided(t):
        t = t.reshape(Bn, n, dil, H, D).transpose(0, 2, 1, 3, 4).reshape(Bn * dil, n, H, D)
        t = jnp.pad(t.astype(F32), ((0, 0), (0, npad), (0, 0), (0, 0)))
        return t.reshape(Bn * dil, nblk, nk, H, D)

    qs, ks, vs = strided(q), strided(k), strided(v)
    prev = lambda t: jnp.pad(t, ((0, 0), (1, 0), (0, 0), (0, 0), (0, 0)))[:, :-1]
    kk = jnp.concatenate([prev(ks), ks], axis=2)
    vv = jnp.concatenate([prev(vs), vs], axis=2)
    s = jnp.einsum('bnqhd,bnkhd->bnhqk', qs, kk) * (D ** -0.5)
    qi = jnp.arange(nk)[:, None]
    kj = jnp.arange(2 * nk)[None, :]
    band = (kj >= qi) & (kj <= qi + nk)
    first = (jnp.arange(nblk) == 0)[:, None, None]
    valid = band[None] & ~(first & (kj < nk)[None])
    s = jnp.where(valid[None, :, None], s, -jnp.inf)
    lse = jax.nn.logsumexp(s, axis=-1)
    p = jnp.exp(s - lse[..., None])
    o = jnp.einsum('bnhqk,bnkhd->bnqhd', p, vv).reshape(Bn * dil, nblk * nk, H, D)
    lse = jnp.swapaxes(lse, 2, 3).reshape(Bn * dil, nblk * nk, H)

    def unstride(t):
        t = t[:, :n]
        t = t.reshape((Bn, dil, n) + t.shape[2:])
        return jnp.swapaxes(t, 1, 2).reshape((Bn, S) + t.shape[3:])

    return unstride(o), unstride(lse)


def dilated_sample(q, kv_new, buf, window, dil):
    Lb, T = buf.shape[1], q.shape[1]
    kv = jnp.concatenate([buf.astype(kv_new.dtype), kv_new], axis=1)
    m = jnp.arange(window // dil + 1)
    idx = Lb + jnp.arange(T)[:, None] - dil * m[None, :]
    valid = idx >= 0
    kvg = kv[:, jnp.maximum(idx, 0)].astype(F32)
    s = jnp.einsum('bthd,btmhd->bthm', q.astype(F32), kvg[:, :, :, 0]) * (B_HD ** -0.5)
    s = jnp.where(valid[None, :, None, :], s, -jnp.inf)
    lse = jax.nn.logsumexp(s, axis=-1)
    p = jnp.exp(s - lse[..., None])
    o = jnp.einsum('bthm,btmhd->bthd', p, kvg[:, :, :, 1])
    return o, lse


def dilated_branch(qkv_b, pos, bufs):
    Bn, L, _ = qkv_b.shape
    G = len(B_GROUPS)
    qb, kb, vb = jnp.split(qkv_b, 3, axis=-1)
    shp = (Bn, L, G, B_HEADS, B_HD)
    q = rope_partial(qb.reshape(shp), pos)
    k = rope_partial(kb.reshape(shp), pos)
    v = vb.reshape(shp)
    outs, lses, rows = [], [], []
    for gi, (win, dil) in enumerate(B_GROUPS):
        kv = jnp.stack([k[:, :, gi], v[:, :, gi]], axis=2)
        if bufs is None:
            o, lse = dilated_prompt(q[:, :, gi], k[:, :, gi], v[:, :, gi], win, dil)
            rows.append(kv[:, L - min(win, L):])
        else:
            o, lse = dilated_sample(q[:, :, gi], kv, bufs[gi], win, dil)
            rows.append(kv)
        outs.append(o)
        lses.append(lse)
    wts = jax.nn.softmax(jnp.stack(lses, 0), axis=0)
    o = jnp.sum(wts[..., None] * jnp.stack(outs, 0), axis=0)
    return o.reshape(Bn, L, B_OUT).astype(qkv_b.dtype), rows


def _lin_comb(l, r):
    return (l[0] * r[0], r[0] * l[1] + r[1])


def rglru_branch(xc, gc, conv_buf, h0, conv_w, conv_b, w_r, b_r, w_i, b_i, lam):
    Bn, L, _ = xc.shape
    y, new_buf = causal_dwconv(xc, conv_buf, conv_w, conv_b)
    yf = y.astype(F32)
    yb = yf.reshape(Bn, L, C_BLOCKS, C_BW)
    r = jax.nn.sigmoid(jnp.einsum('blnc,ncd->blnd', yb, w_r.astype(F32)).reshape(Bn, L, C_WIDTH) + b_r)
    i = jax.nn.sigmoid(jnp.einsum('blnc,ncd->blnd', yb, w_i.astype(F32)).reshape(Bn, L, C_WIDTH) + b_i)
    log_a = -C_POW * r * jax.nn.softplus(-lam.astype(F32))
    a = jnp.exp(log_a)
    bx = jnp.sqrt(-jnp.expm1(2.0 * log_a)) * (i * yf)
    a_cum, b_cum = lax.associative_scan(_lin_comb, (a, bx), axis=1)
    h = a_cum * h0.astype(F32)[:, None] + b_cum
    out = h * jax.nn.gelu(gc.astype(F32))
    return out.astype(xc.dtype), new_buf, h[:, -1].astype(xc.dtype)


def conv_ffn(x, buf, w_up, conv_w, conv_b, w_down):
    gate, up = jnp.split(x @ w_up, 2, axis=-1)
    gate, new_buf = causal_dwconv(gate, buf, conv_w, conv_b)
    return (jax.nn.silu(gate) * up) @ w_down, new_buf


def trunk_layer(x, pos, a_conv, a_S, b_bufs, c_conv, c_h, f_conv, p):
    proj = x @ p['w_in']
    qkv_a, b_a, a_a, z_a, qkv_b, x_c, g_c, gates = jnp.split(proj, IN_SPLITS, axis=-1)
    o_a, a_conv_n, a_S_n = gated_delta_branch(qkv_a, b_a, a_a, z_a, a_conv, a_S, p['a_conv_w'],
                                              p['a_A_log'], p['a_dt_bias'], p['a_norm_w'])
    o_b, b_rows = dilated_branch(qkv_b, pos, b_bufs)
    o_c, c_conv_n, c_h_n = rglru_branch(x_c, g_c, c_conv, c_h, p['c_conv_w'], p['c_conv_b'], p['c_w_r'],
                                        p['c_b_r'], p['c_w_i'], p['c_b_i'], p['c_lam'])
    s_a, s_b, s_c = jnp.split(jax.nn.sigmoid(gates.astype(F32)), 3, axis=-1)
    merged = s_a * (o_a @ p['w_pa']) + s_b * (o_b @ p['w_pb']) + s_c * (o_c @ p['w_pc'])
    mix = merged.astype(x.dtype) @ p['w_o']
    x = layer_norm(DN_ALPHA * x + mix, p['ln1_g'], p['ln1_b'])
    f, f_conv_n = conv_ffn(x, f_conv, p['f_up'], p['f_conv_w'], p['f_conv_b'], p['f_down'])
    x = layer_norm(DN_ALPHA * x + f, p['ln2_g'], p['ln2_b'])
    return x, (a_conv_n, a_S_n, b_rows[0], b_rows[1], b_rows[2], c_conv_n, c_h_n, f_conv_n)


def setup_inputs(seed: int = 0) -> dict:
    key = jax.random.key(seed)
    keys = jax.random.split(key, 40)
    cnt = [0]

    def nxt():
        cnt[0] += 1
        return keys[cnt[0] - 1]

    def nrm(shape, scale=1.0):
        return jax.random.normal(nxt(), shape, F32) * scale

    def unif(shape, lo, hi):
        return jax.random.uniform(nxt(), shape, F32, lo, hi)

    L = DEPTH
    lb = [min(w, PAST_LEN) for w, _ in B_GROUPS]
    dt = jnp.exp(unif((L, A_HEADS), math.log(1e-3), math.log(1e-1)))
    u = unif((L, C_WIDTH), 0.9, 0.999) ** (1.0 / C_POW)
    a_A = unif((L, A_HEADS), 1.0, 16.0)
    return {
        'x_prompt': nrm((BATCH, SEQ, D_MODEL)),
        'x_sample': nrm((DEC_BATCH, DEC_SEQ, D_MODEL)),
        'state_a_conv': nrm((L, DEC_BATCH, A_CONV - 1, A_QKV)),
        'state_a_rec': nrm((L, DEC_BATCH, A_HEADS, A_DK, A_DV), A_DK ** -0.5),
        'cache_b_w128': nrm((L, DEC_BATCH, lb[0], 2, B_HEADS, B_HD)),
        'cache_b_w512': nrm((L, DEC_BATCH, lb[1], 2, B_HEADS, B_HD)),
        'cache_b_w2048': nrm((L, DEC_BATCH, lb[2], 2, B_HEADS, B_HD)),
        'state_c_conv': nrm((L, DEC_BATCH, C_CONV - 1, C_WIDTH)),
        'state_c_h': nrm((L, DEC_BATCH, C_WIDTH), 0.5),
        'state_f_conv': nrm((L, DEC_BATCH, F_CONV - 1, D_FF)),
        'ln_in_g': 1.0 + nrm((D_MODEL,), 0.02),
        'ln_in_b': nrm((D_MODEL,), 0.02),
        'w_in': nrm((L, D_MODEL, N_IN), D_MODEL ** -0.5),
        'a_conv_w': nrm((L, A_CONV, A_QKV), A_CONV ** -0.5),
        'a_A_log': jnp.log(a_A),
        'a_dt_bias': dt + jnp.log(-jnp.expm1(-dt)),
        'a_norm_w': 1.0 + nrm((L, A_DV), 0.02),
        'c_conv_w': nrm((L, C_CONV, C_WIDTH), C_CONV ** -0.5),
        'c_conv_b': nrm((L, C_WIDTH), 0.02),
        'c_w_r': nrm((L, C_BLOCKS, C_BW, C_BW), C_BW ** -0.5),
        'c_b_r': nrm((L, C_WIDTH), 0.02),
        'c_w_i': nrm((L, C_BLOCKS, C_BW, C_BW), C_BW ** -0.5),
        'c_b_i': nrm((L, C_WIDTH), 0.02),
        'c_lam': jnp.log(u) - jnp.log1p(-u),
        'w_pa': nrm((L, A_VW, D_MODEL), A_VW ** -0.5),
        'w_pb': nrm((L, B_OUT, D_MODEL), B_OUT ** -0.5),
        'w_pc': nrm((L, C_WIDTH, D_MODEL), C_WIDTH ** -0.5),
        'w_o': nrm((L, D_MODEL, D_MODEL), D_MODEL ** -0.5 * DN_BETA),
        'ln1_g': 1.0 + nrm((L, D_MODEL), 0.02),
        'ln1_b': nrm((L, D_MODEL), 0.02),
        'f_up': nrm((L, D_MODEL, 2 * D_FF), D_MODEL ** -0.5),
        'f_conv_w': nrm((L, F_CONV, D_FF), F_CONV ** -0.5),
        'f_conv_b': nrm((L, D_FF), 0.02),
        'f_down': nrm((L, D_FF, D_MODEL), D_FF ** -0.5 * DN_BETA),
        'ln2_g': 1.0 + nrm((L, D_MODEL), 0.02),
        'ln2_b': nrm((L, D_MODEL), 0.02),
    }


def reference(x_prompt, x_sample, state_a_conv, state_a_rec, cache_b_w128, cache_b_w512, cache_b_w2048,
              state_c_conv, state_c_h, state_f_conv, ln_in_g, ln_in_b, w_in, a_conv_w, a_A_log, a_dt_bias,
              a_norm_w, c_conv_w, c_conv_b, c_w_r, c_b_r, c_w_i, c_b_i, c_lam, w_pa, w_pb, w_pc, w_o,
              ln1_g, ln1_b, f_up, f_conv_w, f_conv_b, f_down, ln2_g, ln2_b):
    Bp, Sp = x_prompt.shape[0], x_prompt.shape[1]
    dt = x_prompt.dtype
    pos_p = jnp.arange(Sp)
    pos_s = PAST_LEN + jnp.arange(x_sample.shape[1])
    zero_a_conv = jnp.zeros((Bp, A_CONV - 1, A_QKV), dt)
    zero_a_S = jnp.zeros((Bp, A_HEADS, A_DK, A_DV), dt)
    zero_c_conv = jnp.zeros((Bp, C_CONV - 1, C_WIDTH), dt)
    zero_c_h = jnp.zeros((Bp, C_WIDTH), dt)
    zero_f_conv = jnp.zeros((Bp, F_CONV - 1, D_FF), dt)

    hp = layer_norm(x_prompt, ln_in_g, ln_in_b)
    hs = layer_norm(x_sample, ln_in_g, ln_in_b)
    new_p, new_s = [], []
    for l in range(DEPTH):
        p = {'w_in': w_in[l], 'a_conv_w': a_conv_w[l], 'a_A_log': a_A_log[l], 'a_dt_bias': a_dt_bias[l],
             'a_norm_w': a_norm_w[l], 'c_conv_w': c_conv_w[l], 'c_conv_b': c_conv_b[l], 'c_w_r': c_w_r[l],
             'c_b_r': c_b_r[l], 'c_w_i': c_w_i[l], 'c_b_i': c_b_i[l], 'c_lam': c_lam[l], 'w_pa': w_pa[l],
             'w_pb': w_pb[l], 'w_pc': w_pc[l], 'w_o': w_o[l], 'ln1_g': ln1_g[l], 'ln1_b': ln1_b[l],
             'f_up': f_up[l], 'f_conv_w': f_conv_w[l], 'f_conv_b': f_conv_b[l], 'f_down': f_down[l],
             'ln2_g': ln2_g[l], 'ln2_b': ln2_b[l]}
        hp, st_p = trunk_layer(hp, pos_p, zero_a_conv, zero_a_S, None, zero_c_conv, zero_c_h, zero_f_conv, p)
        hs, st_s = trunk_layer(hs, pos_s, state_a_conv[l], state_a_rec[l],
                               (cache_b_w128[l], cache_b_w512[l], cache_b_w2048[l]),
                               state_c_conv[l], state_c_h[l], state_f_conv[l], p)
        new_p.append(st_p)
        new_s.append(st_s)

    (a_conv_p, a_rec_p, b128_p, b512_p, b2048_p, c_conv_p, c_h_p, f_conv_p) = [jnp.stack(s, 0) for s in zip(*new_p)]
    (a_conv_s, a_rec_s, b128_s, b512_s, b2048_s, c_conv_s, c_h_s, f_conv_s) = [jnp.stack(s, 0) for s in zip(*new_s)]
    return (hp, hs,
            a_conv_p, a_rec_p, b128_p, b512_p, b2048_p, c_conv_p, c_h_p, f_conv_p,
            a_conv_s, a_rec_s, b128_s, b512_s, b2048_s, c_conv_s, c_h_s, f_conv_s)
```

```python
import numpy as np
import concourse.bass as bass
import concourse.mybir as mybir
from concourse.bass_utils import run_bass_kernel_spmd

dt = mybir.dt
AF = mybir.ActivationFunctionType
ALU = mybir.AluOpType
F32, BF16 = dt.float32, dt.bfloat16

D = 1024
DEPTH = 4
A_QKV = 1536
B_W = 768
C_W = 512
D_FF = 2816
N_IN = 8456
ALPHA = (2 * DEPTH) ** 0.25
LN_EPS = 1e-5
NORM_EPS = 1e-6
O_QKVA, O_XC, O_GC, O_GATE, O_Z, O_QKVB, O_BA = 0, 1536, 2048, 2560, 5632, 6144, 8448
N_INP = 8456


class Buf:
    __slots__ = ("name", "w", "r", "subs")

    def __init__(self, name):
        self.name = name
        self.w = None
        self.r = {}
        self.subs = {}


class V:
    def __init__(self, ap, bufs):
        self.ap = ap
        self.bufs = bufs

    def __getitem__(self, idx):
        return V(self.ap[idx], self.bufs)

    def re(self, pat, **kw):
        return V(self.ap.rearrange(pat, **kw), self.bufs)

    def bc(self, d):
        return V(self.ap.bitcast(d), self.bufs)

    def bcast(self, shape):
        return V(self.ap.broadcast_to(shape), self.bufs)

    def pbcast(self, n):
        return V(self.ap.partition_broadcast(n), self.bufs)

    def subs(self, keys):
        out = []
        for key in keys:
            out.extend(self.sub(key).bufs)
        return V(self.ap, out)

    def sub(self, key):
        out = []
        for b in self.bufs:
            if key not in b.subs:
                b.subs[key] = Buf(f"{b.name}.{key}")
            out.append(b.subs[key])
        return V(self.ap, out)


class Stream:
    def __init__(self, sem):
        self.sem = sem
        self.count = 0


class KB:
    def __init__(self, nc, n_streams=24):
        self.nc = nc
        self.eng = {"pe": nc.tensor, "act": nc.scalar, "dve": nc.vector, "pool": nc.gpsimd, "sp": nc.sync}
        self.prog = {}
        self.cnt = {}
        self.waited = {e: {} for e in self.eng}
        for e in ("pe", "act", "dve", "pool"):
            self.prog[e] = nc.alloc_semaphore(f"prog_{e}")
            self.cnt[e] = 0
        self.streams = [Stream(nc.alloc_semaphore(f"dq{i}")) for i in range(n_streams)]
        self.rr = 0
        self.pool_streams = [Stream(nc.alloc_semaphore(f"pq{i}")) for i in range(6)]
        self.prr = 0
        self.semname = {}
        self.out_events = {}
        self.n_inst = 0

    def sb(self, name, shape, dtype):
        t = self.nc.alloc_sbuf_tensor("s_" + name, list(shape), dtype)
        return V(t[tuple(slice(None) for _ in shape)], [Buf(name)])

    def ps(self, name, shape, dtype=F32):
        t = self.nc.alloc_psum_tensor("p_" + name, list(shape), dtype)
        return V(t[tuple(slice(None) for _ in shape)], [Buf(name)])

    def dram(self, name, shape, dtype, kind):
        return self.nc.dram_tensor(name, list(shape), dtype, kind=kind).ap()

    def scratch(self, name, shape, dtype):
        ap = self.nc.dram_tensor(name, list(shape), dtype, kind="Internal").ap()
        return V(ap, [Buf(name)])

    def _deps(self, reads, writes):
        evs = {}

        def add(ev):
            if ev is None:
                return
            s, v = ev
            if evs.get(s, (None, 0))[1] < v:
                evs[s] = (s, v)

        for v in reads:
            for b in v.bufs:
                add(b.w)
        for v in writes:
            for b in v.bufs:
                add(b.w)
                for ev in b.r.values():
                    add(ev)
        return list(evs.values())

    def _wait(self, e, evs):
        eng = self.eng[e]
        w = self.waited[e]
        for s, v in evs:
            if e == "pe" and s is self.prog.get("pe"):
                continue
            key = id(s)
            if w.get(key, 0) >= v:
                continue
            eng.wait_ge(s, v)
            w[key] = v

    def _commit(self, ev, reads, writes):
        s, v = ev
        for x in reads:
            for b in x.bufs:
                b.r[id(s)] = ev
        for x in writes:
            for b in x.bufs:
                b.w = ev
                b.r = {}

    POOL_COMPUTE = False

    def emit(self, e, fn, reads, writes):
        reads = [r for r in reads if isinstance(r, V)]
        writes = [r for r in writes if isinstance(r, V)]
        self._wait(e, self._deps(reads, writes))
        inst = fn()
        self.cnt[e] += 1
        inst.then_inc(self.prog[e], 1)
        self._commit((self.prog[e], self.cnt[e]), reads, writes)
        self.n_inst += 1
        return inst

    def dma(self, q, out, in_, is_output=False, **kw):
        if q == "pool":
            st = self.pool_streams[self.prr % len(self.pool_streams)]
            self.prr += 1
        else:
            st = self.streams[self.rr % len(self.streams)]
            self.rr += 1
        reads = [in_] if isinstance(in_, V) else []
        writes = [out] if isinstance(out, V) else []
        evs = self._deps(reads, writes)
        if st.count:
            evs.append((st.sem, st.count))
        self._wait(q, evs)
        o = out.ap if isinstance(out, V) else out
        i = in_.ap if isinstance(in_, V) else in_
        inst = self.eng[q].dma_start(out=o, in_=i, **kw)
        st.count += 16
        inst.then_inc(st.sem, 16)
        self._commit((st.sem, st.count), reads, writes)
        if is_output:
            self.out_events[id(st.sem)] = (st.sem, st.count)
        self.n_inst += 1

    def finish(self):
        evs = [(st.sem, st.count) for st in self.streams + self.pool_streams if st.count]
        self._wait("sp", evs)
        self._wait("sp", [(self.prog[e], self.cnt[e]) for e in self.prog if self.cnt[e]])

    @staticmethod
    def _a(x):
        return x.ap if isinstance(x, V) else x

    def mm(self, out, lhsT, rhs, start=True, stop=True):
        return self.emit("pe", lambda: self.nc.tensor.matmul(out.ap, lhsT.ap, rhs.ap, start=start, stop=stop),
                         [lhsT, rhs] + ([] if start else [out]), [out])

    def tr(self, out, in_, ident):
        return self.emit("pe", lambda: self.nc.tensor.transpose(out.ap, in_.ap, ident.ap), [in_, ident], [out])

    def act(self, out, in_, func, bias=None, scale=None, accum_out=None):
        kw = {}
        rd = [in_]
        if bias is not None:
            kw["bias"] = self._a(bias)
            rd.append(bias)
        if scale is not None:
            kw["scale"] = self._a(scale)
            rd.append(scale)
        wr = [out]
        if accum_out is not None:
            kw["accum_out"] = accum_out.ap
            wr.append(accum_out)
        return self.emit("act", lambda: self.nc.scalar.activation(out.ap, in_.ap, func, **kw), rd, wr)

    def ts(self, e, out, in0, s1, s2=None, op0=ALU.mult, op1=None):
        if e == "pool" and not self.POOL_COMPUTE:
            e = "dve"
        kw = {}
        if op1 is not None:
            kw["op1"] = op1
        return self.emit(e, lambda: self.eng[e].tensor_scalar(out.ap, in0.ap, self._a(s1), self._a(s2), op0, **kw),
                         [in0, s1, s2], [out])

    def tt(self, e, out, in0, in1, op):
        if e == "pool!":
            e = "pool"
        elif e == "pool" and not self.POOL_COMPUTE:
            e = "dve"
        return self.emit(e, lambda: self.eng[e].tensor_tensor(out.ap, in0.ap, in1.ap, op), [in0, in1], [out])

    def stt(self, out, in0, scalar, in1, op0, op1):
        return self.emit("dve", lambda: self.nc.vector.scalar_tensor_tensor(out.ap, in0.ap, self._a(scalar), in1.ap, op0, op1),
                         [in0, scalar, in1], [out])

    def copy(self, e, out, in_):
        if e == "pool" and not self.POOL_COMPUTE:
            e = "act"
        if e == "act":
            return self.emit("act", lambda: self.nc.scalar.copy(out.ap, in_.ap), [in_], [out])
        return self.emit(e, lambda: self.eng[e].tensor_copy(out.ap, in_.ap), [in_], [out])

    def memset(self, e, out, val):
        if e == "pool" and not self.POOL_COMPUTE:
            e = "dve"
        return self.emit(e, lambda: self.eng[e].memset(out.ap, val), [], [out])

    def recip(self, out, in_):
        return self.emit("dve", lambda: self.nc.vector.reciprocal(out.ap, in_.ap), [in_], [out])

    def reduce(self, e, out, in_, op, axis=mybir.AxisListType.X):
        return self.emit(e, lambda: self.eng[e].tensor_reduce(out.ap, in_.ap, axis, op), [in_], [out])

    def bn_stats(self, out, in_):
        return self.emit("dve", lambda: self.nc.vector.bn_stats(out.ap, in_.ap), [in_], [out])

    def bn_aggr(self, out, in_):
        return self.emit("dve", lambda: self.nc.vector.bn_aggr(out.ap, in_.ap), [in_], [out])

    def scan(self, out, d0, d1, init, op0, op1):
        return self.emit("dve", lambda: self.nc.vector.tensor_tensor_scan(out.ap, d0.ap, d1.ap, self._a(init), op0, op1),
                         [d0, d1, init], [out])


B_GROUPS = ((128, 1), (512, 4), (2048, 16))
NRV = 4 * 1024 + 128 + 8
PV_ACW, PV_CCW, PV_CCB, PV_CBR, PV_CBI, PV_LAM, PV_FCW, PV_FCB = 0, 48, 64, 68, 72, 76, 80, 146
NPV = 168
TM_SLOTS = [("z", 5632, 512), ("q01", 6144, 512), ("q2ba", 6656, 264), ("kv0", 6920, 512), ("kv1", 7432, 512), ("kv2", 7944, 512)]


class Model:
    def __init__(self, T, NSEQ, L, with_samples=True, debug=False):
        self.debug = debug
        self.T, self.NSEQ, self.L = T, NSEQ, L
        self.CW = 512 if T >= 512 else T
        assert T % self.CW == 0 and self.CW % 128 == 0
        self.NCH = T // self.CW
        self.NT = self.CW // 128
        self.with_samples = with_samples and NSEQ > 0
        self.TS = 4 * NSEQ
        nc = bass.Bass("TRN2", target_bir_lowering=False)
        self.nc = nc
        self.k = KB(nc, n_streams=32)
        self.Weff = [min(w, T) for w, _ in B_GROUPS]
        self.ring = [8, 8, 20]
        self.build()

    def declare(self):
        k, T, L, CW = self.k, self.T, self.L, self.CW
        I, O = "ExternalInput", "ExternalOutput"
        d = {}
        d["xp"] = k.dram("xp", [T, D], F32, I)
        d["win"] = k.dram("win", [L, D, N_INP], F32, I)
        d["wpa"] = k.dram("wpa", [L, 512, D], F32, I)
        d["wpb"] = k.dram("wpb", [L, 256, D], F32, I)
        d["wpc"] = k.dram("wpc", [L, 512, D], F32, I)
        d["wo"] = k.dram("wo", [L, D, D], F32, I)
        d["fup"] = k.dram("fup", [L, D, 2 * D_FF], F32, I)
        d["fdown"] = k.dram("fdown", [L, D_FF, D], F32, I)
        d["wbd"] = k.dram("wbd", [L, 128, 8, 128], F32, I)
        d["rvec"] = k.dram("rvec", [L, NRV], F32, I)
        d["rvec0"] = k.dram("rvec0", [2048], F32, I)
        d["pvec"] = k.dram("pvec", [L, 128, NPV], F32, I)
        d["csp"] = k.dram("csp", [T, 16], F32, I)
        d["consts"] = k.dram("consts", [128, 6, 128], F32, I)
        d["mw0"] = k.dram("mw0", [128, 256], F32, I)
        d["mw1"] = k.dram("mw1", [128, 640], F32, I)
        d["mw2"] = k.dram("mw2", [128, 2176], F32, I)
        d["yp"] = k.dram("yp", [T, D], F32, O)
        d["a_conv_p"] = k.dram("a_conv_p", [L, 12, 128, 3], F32, O)
        d["a_rec_p"] = k.dram("a_rec_p", [L, 4, 128, 128], F32, O)
        for g in range(3):
            d[f"b{g}_p"] = k.dram(f"b{g}_p", [L, self.Weff[g], 512], F32, O)
        d["c_conv_p"] = k.dram("c_conv_p", [L, 4, 128, 3], F32, O)
        d["c_h_p"] = k.dram("c_h_p", [L, 4, 128], F32, O)
        d["f_conv_p"] = k.dram("f_conv_p", [L, 22, 128, 2], F32, O)
        self.xs_scr = k.scratch("xs_scr", [T, D], F32)
        self.wscr = {}
        for nm, shp in (("win", [D, N_INP]), ("wpa", [512, D]), ("wpb", [256, D]), ("wpc", [512, D]), ("wo", [D, D]),
                        ("fup", [D, 2 * D_FF]), ("fdown", [D_FF, D])):
            for l_ in range(L):
                self.wscr[(nm, l_)] = k.scratch(f"wb_{nm}_{l_}", shp, BF16)
        if self.with_samples:
            NS, TS = self.NSEQ, self.TS
            assert NS == 16
            d["xs"] = k.dram("xs", [TS, D], F32, I)
            d["css"] = k.dram("css", [TS, 16], F32, I)
            d["consts_s"] = k.dram("consts_s", [128, 6, 128], F32, I)
            d["smask"] = k.dram("smask", [TS, 3, TS], F32, I)
            d["m0"] = k.dram("m0", [128, 4], F32, I)
            d["zsel"] = k.dram("zsel", [128, 127], F32, I)
            d["segm"] = k.dram("segm", [128, 16, TS], F32, I)
            d["sel"] = k.dram("sel", [128, 16], F32, I)
            d["a_conv_s_in"] = k.dram("a_conv_s_in", [L, 128, 12, NS, 3], F32, I)
            d["a_rec_s_in"] = k.dram("a_rec_s_in", [L, NS, 4, 128, 128], F32, I)
            d["c_conv_s_in"] = k.dram("c_conv_s_in", [L, 128, 4, NS, 3], F32, I)
            d["c_h_s_in"] = k.dram("c_h_s_in", [L, 128, 4, NS], F32, I)
            d["f_conv_s_in"] = k.dram("f_conv_s_in", [L, 128, 22, NS, 2], F32, I)
            for g, (W, dil) in enumerate(B_GROUPS):
                d[f"cb{g}"] = k.dram(f"cb{g}", [L, NS, W, 512], F32, I)
                d[f"b{g}_s"] = k.dram(f"b{g}_s", [L, TS, 512], F32, O)
            d["ys"] = k.dram("ys", [TS, D], F32, O)
            d["a_conv_s"] = k.dram("a_conv_s", [L, 12, 128, NS, 3], F32, O)
            d["a_rec_s"] = k.dram("a_rec_s", [L, NS, 4, 128, 128], F32, O)
            d["c_conv_s"] = k.dram("c_conv_s", [L, 4, 128, NS, 3], F32, O)
            d["c_h_s"] = k.dram("c_h_s", [L, 4, 128, NS], F32, O)
            d["f_conv_s"] = k.dram("f_conv_s", [L, 22, 128, NS, 2], F32, O)
            self.xs_scr_s = k.scratch("xs_scr_s", [TS, D], F32)
            self.qs_scr = k.scratch("qs_scr", [TS, 768], F32)
        if getattr(self, "debug", False):
            d["dbg_oT"] = k.dram("dbg_oT", [L, 128, 10, T], F32, O)
            d["dbg_sm"] = k.dram("dbg_sm", [T // 128, 128, 64], F32, O)
            d["dbg_gb"] = k.dram("dbg_gb", [T // 128, 128, 512], F32, O)
            d["dbg_vn"] = k.dram("dbg_vn", [64, 4, 128], F32, O)
            d["dbg_sms"] = k.dram("dbg_sms", [128, 64], F32, O)
            d["dbg_kd"] = k.dram("dbg_kd", [64, 4, 128], F32, O)
        self.d = d

    ARENA = 17920

    def alloc(self):
        k, CW, NT = self.k, self.CW, self.NT
        s = {}
        s["consts"] = k.sb("consts", [128, 6, 128], F32)
        s["ident_b"] = k.sb("ident_b", [128, 128], BF16)
        s["ones_b"] = k.sb("ones_b", [128, 128], BF16)
        s["zeros_b"] = k.sb("zeros_b", [128, 128], BF16)
        s["mw"] = [k.sb("mw0", [128, 256], BF16), k.sb("mw1", [128, 640], BF16), k.sb("mw2", [128, 2176], BF16)]
        s["lnbuf"] = k.sb("lnbuf", [128, 2048], F32)
        s["rsm"] = k.sb("rsm", [128, 136], F32)
        s["pvec"] = k.sb("pvec", [128, NPV], F32)
        s["lay"] = k.sb("lay", [128, 16], F32)
        s["wbd"] = k.sb("wbd", [128, 8, 128], BF16)
        s["xT"] = k.sb("xT", [128, 8, CW], BF16)
        s["xres"] = [k.sb(f"xres{i}", [128, D], F32) for i in range(NT)]
        s["wslot"] = [k.sb(f"wslot{i}", [128, 8, 512], BF16) for i in range(3)]
        s["oT"] = k.sb("oT", [128, 10, CW], BF16)
        s["zs"] = [k.sb(f"zs{i}", [128, 512], BF16) for i in range(NT)]
        s["ba"] = [k.sb(f"ba{i}", [128, 8], F32) for i in range(NT)]
        s["cs"] = [k.sb(f"cs{i}", [128, 16], F32) for i in range(NT)]
        s["qT"] = k.sb("qT", [128, 6, CW], BF16)
        s["kT"] = [k.sb(f"kT{g}", [128, 2, self.ring[g] * 128], BF16) for g in range(3)]
        s["vr"] = [k.sb(f"vr{g}", [128, self.ring[g], 256], BF16) for g in range(3)]
        s["S"] = k.sb("S", [128, 4, 128], F32)
        s["Sb"] = k.sb("Sb", [128, 4, 128], BF16)
        s["hlast"] = k.sb("hlast", [128, 4], F32)
        s["fh"] = k.sb("fh", [128, 22, 2], F32)
        s["ahalo"] = k.sb("ahalo", [128, 12, 3], F32)
        s["chalo"] = k.sb("chalo", [128, 4, 3], F32)
        s["psum"] = k.ps("psum", [128, 8, 512])
        if self.with_samples:
            s["consts_s"] = k.sb("consts_s", [128, 6, 128], F32)
            s["smask"] = k.sb("smask", [128, 3, 64], BF16)
            s["m0"] = k.sb("m0", [128, 4], BF16)
            s["zsel"] = k.sb("zsel", [128, 127], BF16)
            s["segm"] = k.sb("segm", [128, 16, 64], BF16)
            s["sel"] = k.sb("sel", [128, 16], F32)
            s["kTn"] = k.sb("kTn", [128, 6, 64], BF16)
            s["vns"] = k.sb("vns", [128, 3, 256], BF16)
        s["arena"] = k.sb("arena", [128, self.ARENA], F32)
        self.s = s
        print('SBUF bytes remaining after alloc:', self.nc.sbuf_bytes_remaining() if callable(getattr(self.nc, 'sbuf_bytes_remaining', None)) else self.nc.sbuf_bytes_remaining)
        self.f_v3 = None
        self.use_f32r = False
        self.bank_i = 0
        self.pinned = set()
        self.slot_i = 0
        self.tmp_i = {}
        self.phase_name = None
        self.phase_off = 0
        self.marks = []
        self.phase_bufs = {}
        self.phase_offs = {}
        self.arena_front = {}

    def phase(self, name):
        for bl in self.phase_bufs.values():
            for b in bl:
                for ev in ([b.w] if b.w else []) + list(b.r.values()):
                    key = id(ev[0])
                    if self.arena_front.get(key, (None, 0))[1] < ev[1]:
                        self.arena_front[key] = ev
        self.marks.append((name, dict(self.k.cnt)))
        self.phase_name = name
        self.phase_off = self.phase_offs.get(name, 0)
        for b in self.phase_bufs.get(name, []):
            for key, ev in self.arena_front.items():
                if b.r.get(key, (None, 0))[1] < ev[1]:
                    b.r[key] = ev

    def carve(self, name, shape, dtype):
        n = 1
        for d_ in shape[1:]:
            n *= d_
        nb = n * (4 if dtype == F32 else 2)
        ne = (nb + 3) // 4
        off = self.phase_off
        self.phase_off += (ne + 7) // 8 * 8
        self.phase_offs[self.phase_name] = self.phase_off
        assert self.phase_off <= self.ARENA, (self.phase_name, name, self.phase_off)
        ap = self.s["arena"].ap[:, off:off + ne]
        if dtype != F32:
            ap = ap.bitcast(dtype)[:, 0:n]
        if len(shape) > 2:
            names = " ".join(f"d{i}" for i in range(1, len(shape)))
            kw = {f"d{i}": shape[i] for i in range(1, len(shape))}
            ap = ap.rearrange(f"p ({names}) -> p {names}", **kw)
        b = Buf(f"ar_{self.phase_name}_{name}")
        for key, ev in self.arena_front.items():
            b.r[key] = ev
        self.phase_bufs.setdefault(self.phase_name, []).append(b)
        return V(ap, [b])

    def bank(self, pin=False):
        if pin:
            i = min(j for j in range(8) if j not in self.pinned)
            self.pinned.add(i)
        else:
            while self.bank_i % 8 in self.pinned:
                self.bank_i += 1
            i = self.bank_i % 8
            self.bank_i += 1
        p = self.s["psum"]
        v = V(p.ap[:, i, :], [p.sub(i).bufs[0]])
        v.bi = i
        return v

    def unpin(self, v):
        self.pinned.discard(v.bi)

    def bank2(self):
        i = (self.bank_i + 1) // 2 * 2 % 8
        n = 0
        while i in self.pinned or (i + 1) in self.pinned:
            i = (i + 2) % 8
            n += 1
            assert n < 8, "no free psum bank pair"
        self.bank_i = i + 2
        p = self.s["psum"]
        return V(p.ap[:, i:i + 2, :], [p.sub(i).bufs[0], p.sub(i + 1).bufs[0]])

    @staticmethod
    def run_gens(gens, weights=None):
        gens = list(gens)
        weights = list(weights) if weights else [1] * len(gens)
        live = list(range(len(gens)))
        while live:
            for gi in list(live):
                for _ in range(weights[gi]):
                    try:
                        next(gens[gi])
                    except StopIteration:
                        live.remove(gi)
                        break

    def wslot(self):
        v = self.s["wslot"][self.slot_i % 3]
        self.slot_i += 1
        return v

    def wload(self, src, kc, ncols):
        sl = self.wslot()
        dst = sl[:, 0:kc, 0:ncols]
        nm, l, rs, cs = src
        w = self.wscr[(nm, l)].sub((rs.start, rs.stop, cs.start, cs.stop))
        self.k.dma("sp", dst, V(w.ap[rs, cs].rearrange("(kc p) n -> p kc n", p=128), w.bufs))
        return dst

    def weight_blocks(self):
        blks = []
        R = slice(0, D)
        for name, off, ncols in TM_SLOTS:
            blks.append(("win", R, slice(off, off + ncols)))
        for si in (3, 4, 0, 1, 2):
            blks.append(("win", R, slice(si * 512, (si + 1) * 512)))
        for q in range(2):
            for br, (nk, wname) in enumerate(((4, "wpa"), (2, "wpb"), (4, "wpc"))):
                blks.append(("win", R, slice(O_GATE + br * 1024 + q * 512, O_GATE + br * 1024 + (q + 1) * 512)))
                blks.append((wname, slice(0, nk * 128), slice(q * 512, (q + 1) * 512)))
        for half in range(2):
            blks.append(("wo", R, slice(half * 512, (half + 1) * 512)))
        for grp in range(6):
            nb = 4 if grp < 5 else 2
            blks.append(("fup", R, slice(grp * 512, grp * 512 + nb * 128)))
            blks.append(("fup", R, slice(D_FF + grp * 512, D_FF + grp * 512 + nb * 128)))
        for half in range(2):
            for (k0, nk) in ((0, 8), (8, 8), (16, 6)):
                blks.append(("fdown", slice(k0 * 128, (k0 + nk) * 128), slice(half * 512, (half + 1) * 512)))
        return blks

    def cast_weights(self, l):
        d = self.d
        for nm, rs, cs in self.weight_blocks():
            w = self.wscr[(nm, l)].sub((rs.start, rs.stop, cs.start, cs.stop))
            self.k.dma("pool", V(w.ap[rs, cs], w.bufs), d[nm][l][rs, cs])

    def tmp(self, name, shape, dtype, n=2):
        key = (self.phase_name, name, tuple(shape), dtype)
        if key not in self.tmp_i:
            if self.phase_name is None:
                bufs = [self.k.sb(f"{name}_{j}", shape, dtype) for j in range(n)]
            else:
                bufs = [self.carve(f"{name}_{j}", shape, dtype) for j in range(n)]
            self.tmp_i[key] = [0, bufs]
        elif self.phase_name is not None:
            pass
        ent = self.tmp_i[key]
        v = ent[1][ent[0] % n]
        ent[0] += 1
        return v

    def const(self, i):
        return self.s["consts"][:, i, :]

    def layernorm(self, x, tp):
        k = self.k
        g_b, b_b = self.s["lnbuf"][:, 0:1024], self.s["lnbuf"][:, 1024:2048]
        st = self.tmp("lnst", [128, 2, 6], F32)
        mv = self.tmp("lnmv", [128, 4], F32)
        for h in range(2):
            k.bn_stats(st[0:tp, h, :], x[0:tp, h * 512:(h + 1) * 512])
        k.bn_aggr(mv[0:tp, 0:2], st[0:tp].re("p a b -> p (a b)"))
        k.ts("dve", mv[0:tp, 2:3], mv[0:tp, 1:2], LN_EPS, None, op0=ALU.add)
        k.act(mv[0:tp, 2:3], mv[0:tp, 2:3], AF.Sqrt)
        k.recip(mv[0:tp, 3:4], mv[0:tp, 2:3])
        k.ts("dve", x[0:tp], x[0:tp], mv[0:tp, 0:1], mv[0:tp, 3:4], op0=ALU.subtract, op1=ALU.mult)
        k.tt("pool", x[0:tp], x[0:tp], g_b[0:tp], ALU.mult)
        k.tt("pool", x[0:tp], x[0:tp], b_b[0:tp], ALU.add)

    def to_xT(self, x, tp, i):
        k = self.k
        for h in range(2):
            bk = self.bank()
            for j in range(4):
                kc = h * 4 + j
                k.tr(bk[:, j * 128:j * 128 + tp], x[0:tp, kc * 128:(kc + 1) * 128], self.const(0)[0:tp, 0:tp])
            k.copy("act" if h else "dve", self.s["xT"][:, h * 4:h * 4 + 4, i * 128:i * 128 + tp],
                   bk.re("p (a b) -> p a b", a=4)[:, :, 0:tp])

    def layer_setup(self, l):
        k, s, d = self.k, self.s, self.d
        k.dma("sp", s["rsm"], d["rvec"][l][4096:4232].partition_broadcast(128))
        k.dma("sp", s["pvec"], d["pvec"][l])
        k.dma("pool", s["wbd"], d["wbd"][l])
        rsm, lay = s["rsm"], s["lay"]
        k.act(lay[:, 0:4], rsm[:, 128:132], AF.Exp)
        k.ts("dve", lay[:, 0:4], lay[:, 0:4], -1.0, None, op0=ALU.mult)
        k.act(lay[:, 4:8], s["pvec"][:, PV_LAM:PV_LAM + 4], AF.Exp, scale=-1.0)
        k.act(lay[:, 4:8], lay[:, 4:8], AF.Ln, bias=1.0)
        k.ts("dve", lay[:, 4:8], lay[:, 4:8], -8.0, None, op0=ALU.mult)
        k.memset("pool", s["S"], 0.0)
        k.memset("pool", s["Sb"], 0.0)
        k.memset("pool", s["hlast"], 0.0)
        k.memset("pool", s["fh"], 0.0)
        k.memset("pool", s["ahalo"], 0.0)
        k.memset("pool", s["chalo"], 0.0)

    def load_ln(self, src):
        self.k.dma("sp", self.s["lnbuf"], src.partition_broadcast(128))

    def load_chunk(self, l, c):
        k, s, d, CW, NT = self.k, self.s, self.d, self.CW, self.NT
        if l == 0:
            self.load_ln(d["rvec0"])
        for i in range(NT):
            r0 = c * CW + i * 128
            x = s["xres"][i]
            if l == 0:
                k.dma("sp", x, d["xp"][r0:r0 + 128, :])
                self.layernorm(x, 128)
            else:
                k.dma("sp", x, self.xs_scr[r0:r0 + 128, :])
            k.dma("sp", s["cs"][i], d["csp"][r0:r0 + 128, :])
            self.to_xT(x, 128, i)

    def proj_fm(self, l, n, slots, dst_of, first, srcv=None):
        k, s, d = self.k, self.s, self.d
        for si in slots:
            w = self.wload(("win", l, slice(0, D), slice(si * 512, (si + 1) * 512)), 8, 512)
            for j in range(4):
                blk = si * 4 + j
                bk = self.bank()
                for kc in range(8):
                    k.mm(bk[:, 0:n], w[:, kc, j * 128:(j + 1) * 128], s["xT"][:, kc, 0:n], start=(kc == 0), stop=(kc == 7))
                k.copy("act" if blk % 2 else "dve", dst_of(blk), bk[:, 0:n] if srcv is None else srcv(bk[:, 0:n]))
                yield

    def rope(self, x, tp, nh, cs):
        k = self.k
        x3 = x.re("p (h e) -> p h e", e=64)
        x1, x2 = x3[0:tp, :, 0:8], x3[0:tp, :, 8:16]
        cosb = cs[0:tp, None, 0:8].bcast([tp, nh, 8])
        sinb = cs[0:tp, None, 8:16].bcast([tp, nh, 8])
        t = self.tmp("ropet", [128, 4, 12, 8], F32)
        t1, t2, t3, t4 = (t[0:tp, j, 0:nh, :] for j in range(4))
        k.tt("pool", t1, x1, cosb, ALU.mult)
        k.tt("pool", t2, x2, sinb, ALU.mult)
        k.tt("pool", t3, x2, cosb, ALU.mult)
        k.tt("pool", t4, x1, sinb, ALU.mult)
        k.tt("pool", x1, t1, t2, ALU.subtract)
        k.tt("pool", x2, t3, t4, ALU.add)

    def proj_tm(self, l, c):
        k, s, d, CW, NT, T = self.k, self.s, self.d, self.CW, self.NT, self.T
        pend = []
        for name, off, ncols in TM_SLOTS:
            w = self.wload(("win", l, slice(0, D), slice(off, off + ncols)), 8, ncols)
            for i in range(NT):
                a = c * NT + i
                bk = self.bank()
                for kc in range(8):
                    k.mm(bk[:, 0:ncols], s["xT"][:, kc, i * 128:(i + 1) * 128], w[:, kc, :], start=(kc == 0), stop=(kc == 7))
                while pend:
                    pend.pop(0)()
                if name == "z":
                    k.act(s["zs"][i], bk, AF.Silu)
                    yield
                    continue
                t = self.tmp("tmq", [128, 512], F32, n=3)
                k.copy("act", t[:, 0:ncols], bk[:, 0:ncols])
                if name == "q01":
                    self.rope(t, 128, 8, s["cs"][i])
                    pend.append(lambda t=t, i=i: self.tm_to_T(t, 4, s["qT"][:, 0:4, i * 128:(i + 1) * 128]))
                elif name == "q2ba":
                    k.copy("act", s["ba"][i], t[:, 256:264])
                    self.rope(t[:, 0:256], 128, 4, s["cs"][i])
                    pend.append(lambda t=t, i=i: self.tm_to_T(t, 2, s["qT"][:, 4:6, i * 128:(i + 1) * 128]))
                else:
                    g = int(name[2])
                    self.rope(t[:, 0:256], 128, 4, s["cs"][i])
                    W = self.Weff[g]
                    r = a * 128 - (T - W)
                    if r >= 0:
                        k.dma("sp", d[f"b{g}_p"][l][r:r + 128, :], t, is_output=True)
                    slot = a % self.ring[g]
                    pend.append(lambda t=t, g=g, slot=slot: self.tm_to_T(t, 2, s["kT"][g][:, :, slot * 128:(slot + 1) * 128]))
                    k.copy("act", s["vr"][g][:, slot, :], t[:, 256:512])
                yield
        while pend:
            pend.pop(0)()

    def tm_to_T(self, t, nblk, dst):
        k = self.k
        bk = self.bank()
        for j in range(nblk):
            k.tr(bk[:, j * 128:(j + 1) * 128], t[:, j * 128:(j + 1) * 128], self.const(0))
        k.copy("act", dst, bk.re("p (a b) -> p a b", a=4)[:, 0:nblk, :])

    def deltanet_tile(self, pre_cols, tp, sl, consts, ba, zs, o_dst, state, nlev, post=None):
        k, s = self.k, self.s
        pv, rsm, lay = s["pvec"], s["rsm"], s["lay"]
        ident, U, Lst, ones_f, same = consts[:, 0, :], consts[:, 1, :], consts[:, 3, :], consts[:, 4, :], consts[:, 5, :]
        qkvf = self.tmp("qkvf", [128, 12, 128], F32, n=1)
        for blk in range(12 if post is None else 0):
            acc = self.tmp("cacc", [128, 128], F32, n=3)
            accv = acc[:, 0:tp] if sl == tp else acc[:, 0:tp].re("p (s t) -> p s t", t=sl)
            k.ts("dve", accv, pre_cols(blk, 0), pv[:, PV_ACW + blk * 4:PV_ACW + blk * 4 + 1], None, op0=ALU.mult)
            for j in range(1, 4):
                k.stt(accv, pre_cols(blk, j), pv[:, PV_ACW + blk * 4 + j:PV_ACW + blk * 4 + j + 1], accv, ALU.mult, ALU.add)
            k.act(qkvf[:, blk, 0:tp], acc[:, 0:tp], AF.Silu)
            if blk % 4 == 3:
                yield
        yield
        qkvt = self.tmp("qkvt", [128, 12, 128], F32, n=1)
        for g3 in range(3):
            bk = self.bank()
            for j in range(4):
                k.tr(bk[0:tp, j * 128:(j + 1) * 128], qkvf[:, g3 * 4 + j, 0:tp] if post is None else post(g3 * 4 + j), ident)
            k.copy("act", qkvt[0:tp, g3 * 4:g3 * 4 + 4, :], bk.re("p (a b) -> p a b", a=4)[0:tp])
        yield
        sm = self.tmp("dsm", [128, 64], F32, n=2)
        sq = self.tmp("dsq", [128, 8, 128], F32, n=1)
        k.tt("dve", sq[0:tp], qkvt[0:tp, 0:8, :], qkvt[0:tp, 0:8, :], ALU.mult)
        k.reduce("dve", sm[0:tp, 0:8], sq[0:tp], ALU.add)
        k.ts("dve", sm[0:tp, 0:8], sm[0:tp, 0:8], NORM_EPS, None, op0=ALU.add)
        k.act(sm[0:tp, 0:8], sm[0:tp, 0:8], AF.Sqrt)
        k.recip(sm[0:tp, 8:16], sm[0:tp, 0:8])
        k.ts("dve", sm[0:tp, 8:12], sm[0:tp, 8:12], 128.0 ** -0.5, None, op0=ALU.mult)
        yield
        k.act(sm[0:tp, 16:20], ba[0:tp, 0:4], AF.Sigmoid)
        k.tt("pool", sm[0:tp, 20:24], ba[0:tp, 4:8], rsm[0:tp, 132:136], ALU.add)
        k.act(sm[0:tp, 20:24], sm[0:tp, 20:24], AF.Exp)
        k.act(sm[0:tp, 20:24], sm[0:tp, 20:24], AF.Ln, bias=1.0)
        k.tt("dve", sm[0:tp, 20:24], sm[0:tp, 20:24], lay[0:tp, 0:4], ALU.mult)
        g = sm[0:tp, 20:24]
        yield
        bk = self.bank()
        k.mm(bk[0:tp, 0:4], U[0:tp, 0:tp], g)
        k.mm(bk[0:tp, 4:8], same[0:tp, 0:tp], g)
        k.copy("dve", sm[0:tp, 24:32], bk[0:tp, 0:8])
        G, GL = sm[0:tp, 24:28], sm[0:tp, 28:32]
        ug = qkvf[:, 0:4, :]
        k.tt("dve", ug[0:tp, :, 0:tp], U[0:tp, None, 0:tp].bcast([tp, 4, tp]), g[:, :, None].bcast([tp, 4, tp]), ALU.mult)
        gbk = self.bank(pin=True)
        gb = gbk.re("p (h i) -> p h i", h=4)
        for h in range(4):
            k.mm(gb[:, h, 0:tp], ones_f[0:tp, :], ug[0:tp, h, 0:tp])
        yield
        X = qkvf[:, 4:8, :]
        k.tt("dve", X[0:tp, :, 0:tp], gb[0:tp, :, 0:tp], G[:, :, None].bcast([tp, 4, tp]), ALU.subtract)
        DT = self.tmp("dDT", [128, 4, 128], F32, n=1)
        D2 = self.tmp("dD2", [128, 4, 128], F32, n=1)
        k.ts("dve", DT[0:tp, :, 0:tp], X[0:tp, :, 0:tp], 0.0, None, op0=ALU.min)
        k.act(DT[0:tp, :, 0:tp], DT[0:tp, :, 0:tp], AF.Exp)
        k.ts("dve", D2[0:tp, :, 0:tp], X[0:tp, :, 0:tp], -1.0, 0.0, op0=ALU.mult, op1=ALU.min)
        k.act(D2[0:tp, :, 0:tp], D2[0:tp, :, 0:tp], AF.Exp)
        k.tt("pool", DT[0:tp, :, 0:tp], DT[0:tp, :, 0:tp], U[0:tp, None, 0:tp].bcast([tp, 4, tp]), ALU.mult)
        k.tt("pool", D2[0:tp, :, 0:tp], D2[0:tp, :, 0:tp], Lst[0:tp, None, 0:tp].bcast([tp, 4, tp]), ALU.mult)
        k.act(sm[0:tp, 32:36], G, AF.Exp)
        k.tt("pool", sm[0:tp, 36:40], GL, G, ALU.subtract)
        k.act(sm[0:tp, 36:40], sm[0:tp, 36:40], AF.Exp)
        eGb = qkvf[:, 8:12, :]
        k.act(eGb[:, :, 0:tp], gb[:, :, 0:tp], AF.Exp)
        if self.debug and self.cur_l == 0 and tp == 128:
            gbd = sq[:, 0:4, :].re("p a b -> p (a b)")
            k.copy("dve", gbd, gbk)
            k.dma("sp", self.d["dbg_gb"][self.dbg_tile], gbd, is_output=True)
        ncol = tp // sl
        eGL = self.tmp("deGL", [128, 4, 16], F32, n=2)
        k.act(eGL[:, :, 0:ncol], gb[:, :, sl - 1:tp:sl], AF.Exp)
        self.unpin(gbk)
        yield
        k.tt("pool", sm[0:tp, 40:44], sm[0:tp, 12:16], sm[0:tp, 16:20], ALU.mult)
        k.tt("pool", sm[0:tp, 40:44], sm[0:tp, 40:44], sm[0:tp, 32:36], ALU.mult)
        k.tt("pool", sm[0:tp, 44:48], sm[0:tp, 12:16], sm[0:tp, 36:40], ALU.mult)
        k.ts("dve", sm[0:tp, 48:52], sm[0:tp, 16:20], -1.0, None, op0=ALU.mult)

        def bc4(col):
            return sm[0:tp, col:col + 4, None].bcast([tp, 4, 128])

        knq = self.tmp("dknq", [128, 8, 128], BF16, n=1)
        kw = self.tmp("dkw", [128, 4, 128], BF16, n=1)
        kdec = self.tmp("dkdec", [128, 4, 128], BF16, n=1)
        vb = self.tmp("dvb", [128, 4, 128], BF16, n=1)
        k.tt("dve", knq[0:tp, 0:4, :], qkvt[0:tp, 4:8, :], bc4(12), ALU.mult)
        k.tt("dve", knq[0:tp, 4:8, :], qkvt[0:tp, 0:4, :], bc4(8), ALU.mult)
        k.tt("dve", kw[0:tp], qkvt[0:tp, 4:8, :], bc4(40), ALU.mult)
        k.tt("dve", kdec[0:tp], qkvt[0:tp, 4:8, :], bc4(44), ALU.mult)
        k.tt("dve", vb[0:tp], qkvt[0:tp, 8:12, :], bc4(16), ALU.mult)
        bkb = self.bank().bc(BF16)
        for j in range(8):
            k.tr(bkb[:, j * 128:j * 128 + tp], knq[0:tp, j, :], s["ident_b"][0:tp, 0:tp])
        knqT = self.tmp("dknqT", [128, 8, 128], BF16, n=1)
        k.copy("act", knqT[:, :, 0:tp], bkb.re("p (a b) -> p a b", a=8)[:, :, 0:tp])
        qdT = self.tmp("dqdT", [128, 4, 128], BF16, n=1)
        self.dn_scratch = knqT
        self.dn_sq = sq
        k.tt("dve", qdT[:, :, 0:tp], knqT[:, 4:8, 0:tp], eGb[:, :, 0:tp], ALU.mult)
        yield
        kk = self.bank().re("p (h i) -> p h i", h=4)
        qk = self.bank().re("p (h i) -> p h i", h=4)
        for h in range(4):
            k.mm(kk[0:tp, h, 0:tp], knqT[:, h, 0:tp], knqT[:, h, 0:tp])
        for h in range(4):
            k.mm(qk[0:tp, h, 0:tp], knqT[:, h, 0:tp], knqT[:, 4 + h, 0:tp])
        k.tt("pool", D2[0:tp, :, 0:tp], D2[0:tp, :, 0:tp], sm[0:tp, 48:52, None].bcast([tp, 4, tp]), ALU.mult)
        A = [qkvt[:, 0:4, :], qkvt[:, 4:8, :]]
        qb16 = qkvt[:, 8:12, :].bc(BF16).re("p a (b c) -> p (a b) c", c=128)
        BP = [self.tmp("dBP", [128, 4, 2, 128], F32, n=2) for _ in range(2)]
        rr = (lambda v_: v_.bc(dt.float32r)) if (tp == 128 and self.use_f32r) else (lambda v_: v_)
        k.tt("dve", rr(A[0][0:tp, :, 0:tp]), kk[0:tp, :, 0:tp], D2[0:tp, :, 0:tp], ALU.mult)
        QKm = qb16[:, 0:4, :]
        k.tt("dve", QKm[0:tp, :, 0:tp], qk[0:tp, :, 0:tp], DT[0:tp, :, 0:tp], ALU.mult)
        bkt = self.bank().re("p (a b) -> p a b", a=4)
        for h in range(4):
            k.tr(bkt[0:tp, h, 0:tp], A[0][0:tp, h, 0:tp], ident[0:tp, 0:tp])
        k.copy("act", rr(BP[0][0:tp, :, 0, 0:tp]), bkt[0:tp, :, 0:tp])
        k.copy("pool", rr(BP[0][0:tp, :, 1, 0:tp]), ident[0:tp, None, 0:tp].bcast([tp, 4, tp]))
        yield
        for n in range(nlev):
            a_n, bp_n = A[n % 2], BP[n % 2]
            a_x, bp_x = A[(n + 1) % 2], BP[(n + 1) % 2]
            last = n == nlev - 1
            p1 = self.bank2().re("p a (h t i) -> p (a h) t i", h=2, t=2)
            for h in range(4):
                if tp == 128 and self.use_f32r:
                    k.mm(p1[0:tp, h, :, 0:tp], a_n[0:tp, h, 0:tp].bc(dt.float32r), bp_n[0:tp, h, :, 0:tp].bc(dt.float32r))
                else:
                    k.mm(p1[0:tp, h, :, 0:tp], a_n[0:tp, h, 0:tp], bp_n[0:tp, h, :, 0:tp])
            if not last:
                p2 = self.bank().re("p (h i) -> p h i", h=4)
                for h in range(4):
                    k.mm(p2[0:tp, h, 0:tp], bp_n[0:tp, h, 0, 0:tp], a_n[0:tp, h, 0:tp])
                k.copy("act", rr(bp_x[0:tp, :, 0, 0:tp]), p1[0:tp, :, 0, 0:tp])
                k.copy("act", rr(a_x[0:tp, :, 0:tp]), p2[0:tp, :, 0:tp])
            k.tt("dve", rr(bp_x[0:tp, :, 1, 0:tp]), p1[0:tp, :, 1, 0:tp], bp_n[0:tp, :, 1, 0:tp], ALU.add)
            yield
        P = BP[(nlev + 1) % 2][:, :, 0, :].bc(BF16)[:, :, 0:128]
        k.copy("act", P[0:tp, :, 0:tp], BP[nlev % 2][0:tp, :, 1, 0:tp])
        yield
        wb = self.bank().re("p (h i) -> p h i", h=4)
        for h in range(4):
            k.mm(wb[:, h, 0:tp], kw[0:tp, h, :], P[0:tp, h, 0:tp])
        nwT = qb16[:, 4:8, :]
        k.ts("dve", nwT[:, :, 0:tp], wb[:, :, 0:tp], -1.0, None, op0=ALU.mult)
        yield
        ob = state(self, tp, sl, P, vb, nwT, qdT, QKm, kdec, eGL, consts)
        if self.debug and self.cur_l == 0 and tp == 128:
            k.dma("sp", self.d["dbg_sm"][self.dbg_tile], sm, is_output=True)
            self.dbg_tile += 1
        if self.debug and self.cur_l == 0 and tp == 64:
            k.dma("sp", self.d["dbg_sms"], sm, is_output=True)
        yield
        oe = sq[:, 4:8, :]
        k.copy("act", oe[0:tp], ob[0:tp])
        self.unpin(self.ob_bank)
        k.tt("dve", sq[0:tp, 0:4, :], oe[0:tp], oe[0:tp], ALU.mult)
        k.reduce("dve", sm[0:tp, 52:56], sq[0:tp, 0:4, :], ALU.add)
        k.ts("dve", sm[0:tp, 52:56], sm[0:tp, 52:56], 1.0 / 128.0, NORM_EPS, op0=ALU.mult, op1=ALU.add)
        k.act(sm[0:tp, 52:56], sm[0:tp, 52:56], AF.Sqrt)
        k.recip(sm[0:tp, 56:60], sm[0:tp, 52:56])
        k.tt("dve", oe[0:tp], oe[0:tp], bc4(56), ALU.mult)
        k.tt("pool", oe[0:tp], oe[0:tp], rsm[0:tp, None, 0:128].bcast([tp, 4, 128]), ALU.mult)
        oab = knq[:, 0:4, :]
        k.tt("dve", oab[0:tp], oe[0:tp], zs[0:tp].re("p (h e) -> p h e", h=4), ALU.mult)
        bkb = self.bank().bc(BF16)
        for h in range(4):
            k.tr(bkb[:, h * 128:h * 128 + tp], oab[0:tp, h, :], s["ident_b"][0:tp, 0:tp])
        k.copy("act", o_dst, bkb.re("p (a b) -> p a b", a=8)[:, 0:4, 0:tp])
        yield

    def deltanet_heads(self, tag, h0, nh, ba, zs, o_dst, post, nlev=7):
        k, s = self.k, self.s
        rsm, lay, consts = s["rsm"], s["lay"], self.cp
        ident, U, Lst, ones_f, same = consts[:, 0, :], consts[:, 1, :], consts[:, 3, :], consts[:, 4, :], consts[:, 5, :]
        tp = 128
        T_ = lambda nm, shape, dtp, n=1: self.tmp(nm + tag, shape, dtp, n=n)
        hs_ = slice(h0, h0 + nh)
        scr = T_("hscr", [128, 3 * nh, 128], F32)
        qkvt = T_("hqkvt", [128, 3, nh, 128], F32)
        for part in range(3):
            bk = self.bank()
            for j in range(nh):
                k.tr(bk[:, j * 128:(j + 1) * 128], post(part * 4 + h0 + j), ident)
            k.copy("act", qkvt[:, part, :, :], bk.re("p (a b) -> p a b", a=4)[:, 0:nh, :])
        yield
        sm = T_("hsm", [128, 64], F32)
        sq = T_("hsq", [128, 2, nh, 128], F32)
        k.tt("dve", sq, qkvt[:, 0:2], qkvt[:, 0:2], ALU.mult)
        k.reduce("dve", sm[:, 0:2 * nh], sq.re("p a h e -> p (a h) e"), ALU.add)
        k.ts("dve", sm[:, 0:2 * nh], sm[:, 0:2 * nh], NORM_EPS, None, op0=ALU.add)
        k.act(sm[:, 0:2 * nh], sm[:, 0:2 * nh], AF.Sqrt)
        k.recip(sm[:, 8:8 + 2 * nh], sm[:, 0:2 * nh])
        k.ts("dve", sm[:, 8:8 + nh], sm[:, 8:8 + nh], 128.0 ** -0.5, None, op0=ALU.mult)
        RNQ, RNK = 8, 8 + nh
        yield
        BETA, GG = 16, 20
        k.act(sm[:, BETA:BETA + nh], ba[:, h0:h0 + nh], AF.Sigmoid)
        k.tt("dve", sm[:, GG:GG + nh], ba[:, 4 + h0:4 + h0 + nh], rsm[:, 132 + h0:132 + h0 + nh], ALU.add)
        k.act(sm[:, GG:GG + nh], sm[:, GG:GG + nh], AF.Exp)
        k.act(sm[:, GG:GG + nh], sm[:, GG:GG + nh], AF.Ln, bias=1.0)
        k.tt("dve", sm[:, GG:GG + nh], sm[:, GG:GG + nh], lay[:, h0:h0 + nh], ALU.mult)
        g = sm[:, GG:GG + nh]
        yield
        bk = self.bank()
        k.mm(bk[:, 0:nh], U, g)
        k.mm(bk[:, 4:4 + nh], same, g)
        k.copy("dve", sm[:, 24:32], bk[:, 0:8])
        G, GL = sm[:, 24:24 + nh], sm[:, 28:28 + nh]
        ug, X, eGb = scr[:, 0:nh, :], scr[:, nh:2 * nh, :], scr[:, 2 * nh:3 * nh, :]
        k.tt("dve", ug, U[:, None, :].bcast([128, nh, 128]), g[:, :, None].bcast([128, nh, 128]), ALU.mult)
        gbk = self.bank(pin=True)
        gb = gbk.re("p (h i) -> p h i", h=4)[:, 0:nh, :]
        for h in range(nh):
            k.mm(gb[:, h, :], ones_f, ug[:, h, :])
        yield
        k.tt("dve", X, gb, G[:, :, None].bcast([128, nh, 128]), ALU.subtract)
        DT = T_("hDT", [128, nh, 128], F32)
        D2 = T_("hD2", [128, nh, 128], F32)
        k.ts("dve", DT, X, 0.0, None, op0=ALU.min)
        k.ts("dve", D2, X, -1.0, 0.0, op0=ALU.mult, op1=ALU.min)
        k.tt("dve", sm[:, 36:36 + nh], GL, G, ALU.subtract)
        k.act(DT, DT, AF.Exp)
        k.act(D2, D2, AF.Exp)
        k.act(sm[:, 32:32 + nh], G, AF.Exp)
        k.act(sm[:, 36:36 + nh], sm[:, 36:36 + nh], AF.Exp)
        k.act(eGb, gb, AF.Exp)
        eGL = T_("heGL", [128, 4], F32, n=2)
        k.act(eGL[:, 0:nh], gb[:, :, 127], AF.Exp)
        self.unpin(gbk)
        k.tt("dve", DT, DT, U[:, None, :].bcast([128, nh, 128]), ALU.mult)
        k.tt("dve", D2, D2, Lst[:, None, :].bcast([128, nh, 128]), ALU.mult)
        yield
        k.tt("dve", sm[:, 40:40 + nh], sm[:, RNK:RNK + nh], sm[:, BETA:BETA + nh], ALU.mult)
        k.tt("dve", sm[:, 40:40 + nh], sm[:, 40:40 + nh], sm[:, 32:32 + nh], ALU.mult)
        k.tt("dve", sm[:, 44:44 + nh], sm[:, RNK:RNK + nh], sm[:, 36:36 + nh], ALU.mult)
        k.ts("dve", sm[:, 48:48 + nh], sm[:, BETA:BETA + nh], -1.0, None, op0=ALU.mult)
        bcn = lambda col: sm[:, col:col + nh, None].bcast([128, nh, 128])
        knq = T_("hknq", [128, 2, nh, 128], BF16)
        kw = T_("hkw", [128, nh, 128], BF16)
        kdec = T_("hkdec", [128, nh, 128], BF16)
        vb = T_("hvb", [128, nh, 128], BF16)
        k.tt("dve", knq[:, 0], qkvt[:, 1], bcn(RNK), ALU.mult)
        k.tt("dve", knq[:, 1], qkvt[:, 0], bcn(RNQ), ALU.mult)
        k.tt("dve", kw, qkvt[:, 1], bcn(40), ALU.mult)
        k.tt("dve", kdec, qkvt[:, 1], bcn(44), ALU.mult)
        k.tt("dve", vb, qkvt[:, 2], bcn(BETA), ALU.mult)
        bkb = self.bank().bc(BF16).re("p (a b) -> p a b", a=8)
        for j in range(2 * nh):
            k.tr(bkb[:, j, :], knq[:, j // nh, j % nh, :], s["ident_b"])
        knqT = T_("hknqT", [128, 2, nh, 128], BF16)
        k.copy("act", knqT.re("p a h e -> p (a h) e"), bkb[:, 0:2 * nh, :])
        qdT = T_("hqdT", [128, nh, 128], BF16)
        k.tt("dve", qdT, knqT[:, 1], eGb, ALU.mult)
        yield
        kk = self.bank().re("p (h i) -> p h i", h=4)[:, 0:nh, :]
        qk = self.bank().re("p (h i) -> p h i", h=4)[:, 0:nh, :]
        for h in range(nh):
            k.mm(kk[:, h, :], knqT[:, 0, h, :], knqT[:, 0, h, :])
        for h in range(nh):
            k.mm(qk[:, h, :], knqT[:, 0, h, :], knqT[:, 1, h, :])
        k.tt("dve", D2, D2, sm[:, 48:48 + nh, None].bcast([128, nh, 128]), ALU.mult)
        A = [qkvt[:, 0], qkvt[:, 1]]
        qb16 = qkvt[:, 2].bc(BF16).re("p h (b c) -> p (h b) c", c=128)
        BP = [T_("hBP0", [128, nh, 2, 128], F32), T_("hBP1", [128, nh, 2, 128], F32)]
        k.tt("dve", A[0], kk, D2, ALU.mult)
        QKm = qb16[:, 0:nh, :]
        k.tt("dve", QKm, qk, DT, ALU.mult)
        bkt = self.bank().re("p (a b) -> p a b", a=4)[:, 0:nh, :]
        for h in range(nh):
            k.tr(bkt[:, h, :], A[0][:, h, :], ident)
        k.copy("act", BP[0][:, :, 0, :], bkt)
        k.copy("act", BP[0][:, :, 1, :], ident[:, None, :].bcast([128, nh, 128]))
        yield
        for n in range(nlev):
            a_n, bp_n = A[n % 2], BP[n % 2]
            a_x, bp_x = A[(n + 1) % 2], BP[(n + 1) % 2]
            last = n == nlev - 1
            if nh <= 2:
                p1 = self.bank().re("p (h t i) -> p h t i", h=2, t=2)[:, 0:nh]
            else:
                p1 = self.bank2().re("p a (h t i) -> p (a h) t i", h=2, t=2)
            for h in range(nh):
                k.mm(p1[:, h, :, :], a_n[:, h, :], bp_n[:, h, :, :])
            if not last:
                p2 = self.bank().re("p (h i) -> p h i", h=4)[:, 0:nh, :]
                for h in range(nh):
                    k.mm(p2[:, h, :], bp_n[:, h, 0, :], a_n[:, h, :])
                k.copy("act", bp_x[:, :, 0, :], p1[:, :, 0, :])
                k.copy("act", a_x, p2)
            k.tt("dve", bp_x[:, :, 1, :], p1[:, :, 1, :], bp_n[:, :, 1, :], ALU.add)
            yield
        P = BP[(nlev + 1) % 2][:, :, 0, :].bc(BF16)[:, :, 0:128]
        k.copy("act", P, BP[nlev % 2][:, :, 1, :])
        wb = self.bank().re("p (h i) -> p h i", h=4)[:, 0:nh, :]
        for h in range(nh):
            k.mm(wb[:, h, :], kw[:, h, :], P[:, h, :])
        nwT = qb16[:, nh:2 * nh, :]
        k.ts("dve", nwT, wb, -1.0, None, op0=ALU.mult)
        yield
        S, Sb = s["S"], s["Sb"]
        vn = self.bank().re("p (h e) -> p h e", h=4)[:, 0:nh, :]
        for h in range(nh):
            k.mm(vn[:, h, :], P[:, h, :], vb[:, h, :], start=True, stop=False)
            k.mm(vn[:, h, :], nwT[:, h, :], Sb[:, h0 + h, :], start=False, stop=True)
        vnb = knqT[:, 0]
        k.copy("act", vnb, vn)
        obk = self.bank(pin=True)
        ob = obk.re("p (h e) -> p h e", h=4)[:, 0:nh, :]
        for h in range(nh):
            k.mm(ob[:, h, :], qdT[:, h, :], Sb[:, h0 + h, :], start=True, stop=False)
            k.mm(ob[:, h, :], QKm[:, h, :], vnb[:, h, :], start=False, stop=True)
        dS = self.bank().re("p (h e) -> p h e", h=4)[:, 0:nh, :]
        for h in range(nh):
            k.mm(dS[:, h, :], kdec[:, h, :], vnb[:, h, :])
        Ssub, Sbsub = s["S"].sub(tag)[:, hs_, :], s["Sb"].sub(tag)[:, hs_, :]
        k.tt("dve", Ssub, Ssub, eGL[:, 0:nh, None].bcast([128, nh, 128]), ALU.mult)
        k.tt("dve", Ssub, Ssub, dS, ALU.add)
        k.copy("act", Sbsub, Ssub)
        yield
        oe = sq[:, 1]
        k.copy("act", oe, ob)
        self.unpin(obk)
        k.tt("dve", sq[:, 0], oe, oe, ALU.mult)
        k.reduce("dve", sm[:, 52:52 + nh], sq[:, 0], ALU.add)
        k.ts("dve", sm[:, 52:52 + nh], sm[:, 52:52 + nh], 1.0 / 128.0, NORM_EPS, op0=ALU.mult, op1=ALU.add)
        k.act(sm[:, 52:52 + nh], sm[:, 52:52 + nh], AF.Sqrt)
        k.recip(sm[:, 56:56 + nh], sm[:, 52:52 + nh])
        k.tt("dve", oe, oe, bcn(56), ALU.mult)
        k.tt("dve", oe, oe, rsm[:, None, 0:128].bcast([128, nh, 128]), ALU.mult)
        oab = knq[:, 0]
        k.tt("dve", oab, oe, zs[:, h0 * 128:(h0 + nh) * 128].re("p (h e) -> p h e", h=nh), ALU.mult)
        bkb = self.bank().bc(BF16).re("p (a b) -> p a b", a=8)
        for h in range(nh):
            k.tr(bkb[:, h, :], oab[:, h, :], s["ident_b"])
        k.copy("act", o_dst, bkb[:, 0:nh, :])
        yield

    @staticmethod
    def state_prompt(self, tp, sl, P, vb, nwT, qdT, QKm, kdec, eGL, consts):
        k, s = self.k, self.s
        S, Sb = s["S"], s["Sb"]
        vn = self.bank().re("p (h e) -> p h e", h=4)
        for h in range(4):
            k.mm(vn[0:tp, h, :], P[0:tp, h, 0:tp], vb[0:tp, h, :], start=True, stop=False)
            k.mm(vn[0:tp, h, :], nwT[:, h, 0:tp], Sb[:, h, :], start=False, stop=True)
        vnb = self.dn_scratch[:, 0:4, :]
        k.copy("act", vnb[0:tp], vn[0:tp])
        obk = self.bank(pin=True)
        self.ob_bank = obk
        ob = obk.re("p (h e) -> p h e", h=4)
        for h in range(4):
            k.mm(ob[0:tp, h, :], qdT[:, h, 0:tp], Sb[:, h, :], start=True, stop=False)
            k.mm(ob[0:tp, h, :], QKm[0:tp, h, 0:tp], vnb[0:tp, h, :], start=False, stop=True)
        dS = self.bank().re("p (h e) -> p h e", h=4)
        for h in range(4):
            k.mm(dS[:, h, :], kdec[0:tp, h, :], vnb[0:tp, h, :])
        k.tt("pool", S, S, eGL[:, :, 0:1].bcast([128, 4, 128]), ALU.mult)
        k.tt("dve", S, S, dS, ALU.add)
        k.copy("act", Sb, S)
        return ob

    def attention_chunk(self, c, LAG=2):
        k, s, CW, NT = self.k, self.s, self.CW, self.NT
        for pair in range(2):
            num, den = self.bank(pin=True), self.bank(pin=True)
            zr = s["mw"][0][:, 0:CW] if CW <= 256 else s["mw"][2][:, 0:CW]
            k.mm(num[:, 0:CW], s["zeros_b"], zr)
            k.mm(den[:, 0:CW], s["zeros_b"], zr)
            units = []
            for g, (W, dil) in enumerate(B_GROUPS):
                for hh in (2 * pair, 2 * pair + 1):
                    for a in range(max(0, c * NT - W // 128), c * NT + NT):
                        rho = a - c * NT
                        q_lo = max(0, 128 * rho)
                        q_hi = min(CW, 128 * rho + W + 128)
                        if q_hi - q_lo > 0:
                            units.append((g, hh, a, rho, q_lo, q_hi))
            pend = []
            cnt = 0
            for u in units + [None] * LAG:
                if u is not None:
                    g, hh, a, rho, q_lo, q_hi = u
                    n = q_hi - q_lo
                    prt = slice(64 * (hh % 2), 64 * (hh % 2) + 64)
                    slot = a % self.ring[g]
                    st = self.bank()
                    k.mm(st[:, 0:n], s["kT"][g][prt, pair, slot * 128:(slot + 1) * 128], s["qT"][prt, 2 * g + pair, q_lo:q_hi])
                    p = self.tmp("attp", [128, 512], BF16, n=LAG + 2)
                    k.act(p[:, 0:n], st[:, 0:n], AF.Exp, scale=0.125)
                    k.tt("pool" if cnt % 2 else "dve", p[:, 0:n], p[:, 0:n], s["mw"][g][:, q_lo - 128 * rho:q_hi - 128 * rho], ALU.mult)
                    cnt += 1
                    pend.append((p, n, prt, q_lo, q_hi, g, slot, hh))
                if len(pend) > LAG or (u is None and pend):
                    p, n, prt, q_lo, q_hi, g, slot, hh = pend.pop(0)
                    k.mm(num[prt, q_lo:q_hi], s["vr"][g][:, slot, hh * 64:(hh + 1) * 64], p[:, 0:n], start=False, stop=False)
                    k.mm(den[prt, q_lo:q_hi], s["ones_b"][:, 0:64], p[:, 0:n], start=False, stop=False)
                yield
            rd = self.tmp("attrd", [128, 512], F32, n=1)
            k.recip(rd[:, 0:CW], den[:, 0:CW])
            k.tt("dve", s["oT"][:, 4 + pair, :], num[:, 0:CW], rd[:, 0:CW], ALU.mult)
            self.unpin(num)
            self.unpin(den)
            yield

    def rglru(self, n, xwin, gview, hinit, o_dst, scan_fix=None, v3=None, hs=None):
        k, s = self.k, self.s
        pv, lay = s["pvec"], s["lay"]
        hs = [] if hs is None else hs
        for blk in range(4):
            def pc(off):
                return pv[:, off:off + 1]
            y = self.tmp("ry", [128, 512], F32, n=2)[:, 0:n]
            yv = y if v3 is None else v3(y)
            k.ts("dve", yv, xwin(blk, 0), pc(PV_CCW + blk * 4), pc(PV_CCB + blk), op0=ALU.mult, op1=ALU.add)
            for j in range(1, 4):
                k.stt(yv, xwin(blk, j), pc(PV_CCW + blk * 4 + j), yv, ALU.mult, ALU.add)
            yb = self.tmp("ryb", [128, 512], BF16, n=2)[:, 0:n]
            k.copy("act", yb, y)
            yield
            rp, ip = self.bank(), self.bank()
            k.mm(rp[:, 0:n], s["wbd"][:, blk, :], yb)
            k.mm(ip[:, 0:n], s["wbd"][:, 4 + blk, :], yb)
            r = self.tmp("rr", [128, 512], F32, n=2)[:, 0:n]
            ig = self.tmp("ri", [128, 512], F32, n=2)[:, 0:n]
            k.act(r, rp[:, 0:n], AF.Sigmoid, bias=pc(PV_CBR + blk))
            k.act(ig, ip[:, 0:n], AF.Sigmoid, bias=pc(PV_CBI + blk))
            yield
            a = self.tmp("ra", [128, 512], F32, n=2)[:, 0:n]
            k.act(a, r, AF.Exp, scale=lay[:, 4 + blk:5 + blk])
            k.tt("pool", r, a, a, ALU.mult)
            k.act(r, r, AF.Sqrt, scale=-1.0, bias=1.0)
            k.tt("pool", ig, ig, y, ALU.mult)
            k.tt("pool", ig, ig, r, ALU.mult)
            if scan_fix is not None:
                scan_fix(blk, a, ig)
            h = self.tmp("rh", [128, 512], F32, n=4)[:, 0:n]
            k.scan(h, a, ig, hinit(blk) if hinit is not None else 0.0, ALU.mult, ALU.add)
            hs.append(h)
            yield
            u = self.tmp("ru", [128, 512], F32, n=2)[:, 0:n]
            gv = self.tmp("rgv", [128, 512], F32, n=2)[:, 0:n]
            k.copy("pool", gv if v3 is None else v3(gv), gview(blk))
            k.tt("pool", u, gv, gv, ALU.mult)
            k.ts("dve", u, u, 0.044715, 1.0, op0=ALU.mult, op1=ALU.add)
            k.tt("pool", u, u, gv, ALU.mult)
            k.act(u, u, AF.Sigmoid, scale=1.5957691216057308)
            k.tt("pool", u, u, gv, ALU.mult)
            k.tt("dve", o_dst(blk), u, h, ALU.mult)
            yield

    def merge(self, l, n, tiles):
        k, s, d = self.k, self.s, self.d
        oT, xT = s["oT"], s["xT"]
        accs = [self.tmp(f"macc{j}", [128, 512], F32, n=1)[:, 0:n] for j in range(4)]
        for q in range(2):
            for br, (o0, nk, wname) in enumerate(((0, 4, "wpa"), (4, 2, "wpb"), (6, 4, "wpc"))):
                gw = self.wload(("win", l, slice(0, D), slice(O_GATE + br * 1024 + q * 512, O_GATE + br * 1024 + (q + 1) * 512)), 8, 512)
                pw = self.wload((wname, l, slice(0, nk * 128), slice(q * 512, (q + 1) * 512)), nk, 512)
                for j in range(4):
                    fb = q * 4 + j
                    bk = self.bank()
                    for kc in range(8):
                        k.mm(bk[:, 0:n], gw[:, kc, j * 128:(j + 1) * 128], xT[:, kc, 0:n], start=(kc == 0), stop=(kc == 7))
                    sg = self.tmp("msig", [128, 512], F32, n=3)[:, 0:n]
                    k.act(sg, bk[:, 0:n], AF.Sigmoid)
                    bk = self.bank()
                    for kc in range(nk):
                        k.mm(bk[:, 0:n], pw[:, kc, j * 128:(j + 1) * 128], oT[:, o0 + kc, 0:n], start=(kc == 0), stop=(kc == nk - 1))
                    if br == 0:
                        k.tt("dve", accs[j], bk[:, 0:n], sg, ALU.mult)
                    else:
                        k.tt("dve", sg, bk[:, 0:n], sg, ALU.mult)
                        if br == 1:
                            k.tt("pool", accs[j], accs[j], sg, ALU.add)
                        else:
                            k.tt("pool", s["mergedT"][:, fb, 0:n], accs[j], sg, ALU.add)
        for half in range(2):
            w = self.wload(("wo", l, slice(0, D), slice(half * 512, (half + 1) * 512)), 8, 512)
            for (x, tp, c0) in tiles:
                bk = self.bank()
                for kc in range(8):
                    k.mm(bk[0:tp, :], s["mergedT"][:, kc, c0:c0 + tp], w[:, kc, :], start=(kc == 0), stop=(kc == 7))
                xs = x[0:tp, half * 512:(half + 1) * 512]
                k.stt(xs, xs, ALPHA, bk[0:tp, :], ALU.mult, ALU.add)
        self.load_ln(d["rvec"][l][0:2048])
        for ti, (x, tp, c0) in enumerate(tiles):
            self.layernorm(x, tp)
            self.to_xT(x, tp, c0 // 128)

    def ffn(self, l, n, tiles, gwin, halo_update):
        k, s, d = self.k, self.s, self.d
        pv, xT, hT = s["pvec"], s["xT"], s["hT"]
        for grp in range(6):
            nb = 4 if grp < 5 else 2
            gwt = self.wload(("fup", l, slice(0, D), slice(grp * 512, grp * 512 + nb * 128)), 8, nb * 128)
            uwt = self.wload(("fup", l, slice(0, D), slice(D_FF + grp * 512, D_FF + grp * 512 + nb * 128)), 8, nb * 128)
            for j in range(nb):
                blk = grp * 4 + j
                gb_, ub = self.bank(), self.bank()
                for kc in range(8):
                    k.mm(gb_[:, 0:n], gwt[:, kc, j * 128:(j + 1) * 128], xT[:, kc, 0:n], start=(kc == 0), stop=(kc == 7))
                for kc in range(8):
                    k.mm(ub[:, 0:n], uwt[:, kc, j * 128:(j + 1) * 128], xT[:, kc, 0:n], start=(kc == 0), stop=(kc == 7))
                gp = self.tmp("fgp", [128, 3 * 512 // 2], F32, n=2)
                win = gwin(blk, gp, gb_)
                acc = self.tmp("facc", [128, 512], F32, n=2)[:, 0:n]
                accv = acc if self.f_v3 is None else self.f_v3(acc)
                if True:
                    k.ts("dve", accv, win(0), pv[:, PV_FCW + blk * 3:PV_FCW + blk * 3 + 1], pv[:, PV_FCB + blk:PV_FCB + blk + 1], op0=ALU.mult, op1=ALU.add)
                    for jj in (1, 2):
                        k.stt(accv, win(jj), pv[:, PV_FCW + blk * 3 + jj:PV_FCW + blk * 3 + jj + 1], accv, ALU.mult, ALU.add)
                else:
                    k.ts("pool", accv, win(0), pv[:, PV_FCW + blk * 3:PV_FCW + blk * 3 + 1], pv[:, PV_FCB + blk:PV_FCB + blk + 1], op0=ALU.mult, op1=ALU.add)
                    tq = self.tmp("fconvt", [128, 512], F32, n=1)[:, 0:n]
                    tqv = tq if self.f_v3 is None else self.f_v3(tq)
                    for jj in (1, 2):
                        k.ts("pool", tqv, win(jj), pv[:, PV_FCW + blk * 3 + jj:PV_FCW + blk * 3 + jj + 1], None, op0=ALU.mult)
                        k.tt("pool", accv, accv, tqv, ALU.add)
                halo_update(blk, gp)
                k.act(acc, acc, AF.Silu)
                k.tt("dve", hT[:, blk, 0:n], ub[:, 0:n], acc, ALU.mult)
        for half in range(2):
            banks = [self.bank() for _ in tiles]
            for kg, (k0, nk) in enumerate(((0, 8), (8, 8), (16, 6))):
                w = self.wload(("fdown", l, slice(k0 * 128, (k0 + nk) * 128), slice(half * 512, (half + 1) * 512)), nk, 512)
                for ti, (x, tp, c0) in enumerate(tiles):
                    for kc in range(nk):
                        k.mm(banks[ti][0:tp, :], hT[:, k0 + kc, c0:c0 + tp], w[:, kc, :], start=(k0 + kc == 0), stop=(k0 + kc == 21))
            for ti, (x, tp, c0) in enumerate(tiles):
                xs = x[0:tp, half * 512:(half + 1) * 512]
                k.stt(xs, xs, ALPHA, banks[ti][0:tp, :], ALU.mult, ALU.add)
        self.load_ln(d["rvec"][l][2048:4096])
        for (x, tp, c0) in tiles:
            self.layernorm(x, tp)

    def build(self):
        k = self.k
        self.declare()
        self.alloc()
        s, d, CW, NT, T, L = self.s, self.d, self.CW, self.NT, self.T, self.L
        k.dma("sp", s["consts"], d["consts"])
        for g in range(3):
            k.dma("pool", s["mw"][g], d[f"mw{g}"])
        k.copy("dve", s["ident_b"], s["consts"][:, 0, :])
        k.copy("dve", s["ones_b"], s["consts"][:, 4, :])
        k.memset("pool", s["zeros_b"], 0.0)
        self.cp = s["consts"]
        if self.with_samples:
            k.dma("sp", s["consts_s"], d["consts_s"])
            k.dma("sp", s["sel"], d["sel"])
            k.dma("pool", s["smask"][0:64], d["smask"])
            k.dma("pool", s["m0"], d["m0"])
            k.dma("pool", s["zsel"], d["zsel"])
            k.dma("pool", s["segm"], d["segm"])
        self.dbg_tile = 0
        self.cast_weights(0)
        for l in range(L):
            self.cur_l = l
            if l + 1 < L:
                self.cast_weights(l + 1)
            self.layer_setup(l)
            for c in range(self.NCH):
                self.cur_c = c
                self.prompt_chunk(l, c)
            self.prompt_layer_outputs(l)
            if self.with_samples:
                self.sample_chunk(l)
        k.finish()

    def prompt_chunk(self, l, c):
        k, s, d, CW, NT, T, L = self.k, self.s, self.d, self.CW, self.NT, self.T, self.L
        self.phase("TMC")
        self.load_chunk(l, c)
        cpre = self.tmp("cpre", [128, 8, 3 + CW], F32, n=1)
        hs = []

        def gen_c():
            k.copy("act", cpre[:, 0:4, 0:3], s["chalo"])
            yield from self.proj_fm(l, CW, range(3, 5), lambda blk: cpre[:, blk - 12, 3:3 + CW], c == 0)
            k.copy("act", s["chalo"], cpre[:, 0:4, CW:CW + 3])
            yield from self.rglru(CW, lambda blk, j: cpre[:, blk, j:j + CW], lambda blk: cpre[:, 4 + blk, 3:3 + CW],
                                  lambda blk: s["hlast"][:, blk:blk + 1], lambda blk: s["oT"][:, 6 + blk, :], hs=hs)
            for blk in range(4):
                k.copy("act", s["hlast"][:, blk:blk + 1], hs[blk][:, CW - 1:CW])

        self.run_gens([self.proj_tm(l, c), gen_c()])
        self.phase("AB")
        apre0 = self.tmp("apre", [128, 12, 3 + CW], F32, n=1)
        apre_all = apre0.subs(range(12))
        apre = lambda blk: apre0.sub(blk)
        pv = s["pvec"]
        k.copy("act", apre_all[:, :, 0:3], s["ahalo"])
        for _ in self.proj_fm(l, CW, range(0, 3), lambda blk: apre(blk)[:, blk, 3:3 + CW], c == 0):
            pass
        k.copy("act", s["ahalo"], apre_all[:, :, CW:CW + 3])

        def gen_a():
            qf = self.tmp("qkvf", [128, 12, 128], F32, n=1).re("p a b -> p (a b)")
            for blk in range(12):
                acc = qf[:, (blk % 2) * 512:(blk % 2) * 512 + CW]
                ab = apre(blk)
                k.ts("dve", acc, ab[:, blk, 0:CW], pv[:, PV_ACW + blk * 4:PV_ACW + blk * 4 + 1], None, op0=ALU.mult)
                for j in range(1, 4):
                    k.stt(acc, ab[:, blk, j:j + CW], pv[:, PV_ACW + blk * 4 + j:PV_ACW + blk * 4 + j + 1], acc, ALU.mult, ALU.add)
                k.act(ab[:, blk, 3:3 + CW], acc, AF.Silu)
                yield
            for i in range(NT):
                yield from self.deltanet_tile(None, 128, 128, self.cp, s["ba"][i], s["zs"][i], s["oT"][:, 0:4, i * 128:(i + 1) * 128],
                                              Model.state_prompt, 7, post=lambda blk, i=i: apre(blk)[:, blk, 3 + i * 128:3 + (i + 1) * 128])

        self.run_gens([gen_a(), self.attention_chunk(c)])
        tiles = [(s["xres"][i], 128, i * 128) for i in range(NT)]
        if getattr(self, "debug", False):
            self.phase("DBG")
            dbg = self.tmp("dbgo", [128, 10, CW], F32, n=1)
            k.copy("act", dbg, s["oT"])
            k.dma("sp", d["dbg_oT"][l][:, :, c * CW:(c + 1) * CW], dbg, is_output=True)
        self.phase("M")
        s["mergedT"] = self.tmp("mergedT", [128, 8, CW], BF16, n=1)
        self.merge(l, CW, tiles)
        self.phase("F")
        s["hT"] = self.tmp("hT", [128, 22, CW], BF16, n=1)

        def gwin(blk, gp, gbank):
            k.copy("act", gp[:, 0:2], s["fh"][:, blk, :])
            k.copy("act", gp[:, 2:2 + CW], gbank[:, 0:CW])
            return lambda j: gp[:, j:j + CW]

        def halo_update(blk, gp):
            k.copy("act", s["fh"][:, blk, :], gp[:, CW:CW + 2])

        self.ffn(l, CW, tiles, gwin, halo_update)
        for i in range(NT):
            r0 = c * CW + i * 128
            if l == L - 1:
                k.dma("sp", d["yp"][r0:r0 + 128, :], s["xres"][i], is_output=True)
            else:
                k.dma("sp", self.xs_scr[r0:r0 + 128, :], s["xres"][i])

    def prompt_layer_outputs(self, l):
        k, s, d, CW = self.k, self.s, self.d, self.CW
        with self.nc.allow_non_contiguous_dma(reason="small state outputs"):
            k.dma("sp", d["a_conv_p"][l].rearrange("b p j -> p b j"), s["ahalo"], is_output=True)
            k.dma("sp", d["c_conv_p"][l].rearrange("b p j -> p b j"), s["chalo"], is_output=True)
            k.dma("sp", d["c_h_p"][l].rearrange("b p -> p b"), s["hlast"], is_output=True)
            k.dma("sp", d["f_conv_p"][l].rearrange("b p j -> p b j"), s["fh"], is_output=True)
        k.dma("sp", d["a_rec_p"][l].rearrange("h k v -> k h v"), s["S"], is_output=True)


    def sample_chunk(self, l):
        k, s, d, L, NS, TS = self.k, self.s, self.d, self.L, self.NSEQ, self.TS
        x = s["xres"][0]
        v3 = lambda a: a.re("p (s t) -> p s t", t=4)
        self.phase("STM")
        if l == 0:
            self.load_ln(d["rvec0"])
            k.dma("sp", x[0:TS], d["xs"])
            self.layernorm(x, TS)
        else:
            k.dma("sp", x[0:TS], self.xs_scr_s)
        k.dma("sp", s["cs"][0][0:TS], d["css"])
        self.to_xT(x, TS, 0)
        for name, off, ncols in TM_SLOTS:
            w = self.wload(("win", l, slice(0, D), slice(off, off + ncols)), 8, ncols)
            bk = self.bank()
            for kc in range(8):
                k.mm(bk[0:TS, 0:ncols], s["xT"][:, kc, 0:TS], w[:, kc, :], start=(kc == 0), stop=(kc == 7))
            if name == "z":
                k.act(s["zs"][0][0:TS], bk[0:TS], AF.Silu)
                continue
            t = self.tmp("tmq", [128, 512], F32, n=3)
            k.copy("act", t[0:TS, 0:ncols], bk[0:TS, 0:ncols])
            if name == "q01":
                self.rope(t, TS, 8, s["cs"][0])
                k.dma("sp", self.qs_scr[:, 0:512], t[0:TS, :])
                self.tm_to_T_s(t, 4, s["qT"][:, 0:4, 0:TS])
            elif name == "q2ba":
                k.copy("pool", s["ba"][0][0:TS], t[0:TS, 256:264])
                self.rope(t[:, 0:256], TS, 4, s["cs"][0])
                k.dma("sp", self.qs_scr[:, 512:768], t[0:TS, 0:256])
                self.tm_to_T_s(t, 2, s["qT"][:, 4:6, 0:TS])
            else:
                g = int(name[2])
                self.rope(t[:, 0:256], TS, 4, s["cs"][0])
                k.dma("sp", d[f"b{g}_s"][l], t[0:TS, :], is_output=True)
                self.tm_to_T_s(t, 2, s["kTn"][:, 2 * g:2 * g + 2, :])
                k.copy("pool", s["vns"][0:TS, g, :], t[0:TS, 256:512])
        self.phase("SA")
        apre = self.tmp("apres", [128, 12, NS, 7], F32, n=1)
        st3 = self.tmp("ast3", [128, 12, NS, 3], F32, n=1)
        k.dma("sp", st3, d["a_conv_s_in"][l])
        k.copy("pool", apre[:, :, :, 0:3], st3)
        for _ in self.proj_fm(l, TS, range(0, 3), lambda blk: apre[:, blk, :, 3:7], True, srcv=v3):
            pass
        k.copy("pool", st3, apre[:, :, :, 4:7])
        k.dma("sp", d["a_conv_s"][l].rearrange("b p s j -> p b s j"), st3, is_output=True)
        self.cur_l = l
        for _ in self.deltanet_tile(lambda blk, j: apre[:, blk, :, j:j + 4], TS, 4, s["consts_s"], s["ba"][0], s["zs"][0],
                                    s["oT"][:, 0:4, 0:TS], Model.state_sample, 2):
            pass
        self.phase("SB")
        self.sample_attention(l)
        self.phase("SC")
        cpre = self.tmp("cpres", [128, 8, NS, 7], F32, n=1)
        ct3 = self.tmp("cst3", [128, 4, NS, 3], F32, n=1)
        h0 = self.tmp("ch0", [128, 4, NS], F32, n=1)
        k.dma("sp", ct3, d["c_conv_s_in"][l])
        k.dma("sp", h0, d["c_h_s_in"][l])
        k.copy("pool", cpre[:, 0:4, :, 0:3], ct3)
        for _ in self.proj_fm(l, TS, range(3, 5), lambda blk: cpre[:, blk - 12, :, 3:7], True, srcv=v3):
            pass
        k.copy("pool", ct3, cpre[:, 0:4, :, 4:7])
        k.dma("sp", d["c_conv_s"][l].rearrange("b p s j -> p b s j"), ct3, is_output=True)

        def scan_fix(blk, a, bx):
            a3, b3 = v3(a), v3(bx)
            tmpf = self.tmp("sfix", [128, NS], F32, n=2)
            k.tt("pool", tmpf, a3[:, :, 0], h0[:, blk, :], ALU.mult)
            k.tt("pool", b3[:, :, 0], b3[:, :, 0], tmpf, ALU.add)
            k.memset("pool", a3[:, :, 0], 0.0)

        hs = []
        for _ in self.rglru(TS, lambda blk, j: cpre[:, blk, :, j:j + 4], lambda blk: cpre[:, 4 + blk, :, 3:7], None,
                            lambda blk: s["oT"][:, 6 + blk, 0:TS], scan_fix=scan_fix, v3=v3, hs=hs):
            pass
        hout = self.tmp("chout", [128, 4, NS], F32, n=1)
        for blk in range(4):
            k.copy("pool", hout[:, blk, :], v3(hs[blk])[:, :, 3])
        k.dma("sp", d["c_h_s"][l].rearrange("b p s -> p b s"), hout, is_output=True)
        tiles = [(x, TS, 0)]
        self.phase("M")
        s["mergedT"] = self.tmp("mergedT", [128, 8, self.CW], BF16, n=1)
        self.merge(l, TS, tiles)
        self.phase("F")
        s["hT"] = self.tmp("hT", [128, 22, self.CW], BF16, n=1)
        fhs = self.tmp("fhs", [128, 22, NS, 2], F32, n=1)
        fho = self.tmp("fho", [128, 22, NS, 2], F32, n=1)
        k.dma("sp", fhs, d["f_conv_s_in"][l])

        def gwin(blk, gp, gbank):
            gp3 = gp[:, 0:NS * 6].re("p (s c) -> p s c", c=6)
            k.copy("pool", gp3[:, :, 0:2], fhs[:, blk, :, :])
            k.copy("act", gp3[:, :, 2:6], v3(gbank[:, 0:TS]))
            self._gp3 = gp3
            return lambda j: gp3[:, :, j:j + 4]

        def halo_update(blk, gp):
            k.copy("pool", fho[:, blk, :, :], self._gp3[:, :, 4:6])

        self.f_v3 = v3
        self.ffn(l, TS, tiles, gwin, halo_update)
        self.f_v3 = None
        k.dma("sp", d["f_conv_s"][l].rearrange("b p s j -> p b s j"), fho, is_output=True)
        if l == L - 1:
            k.dma("sp", d["ys"], x[0:TS], is_output=True)
        else:
            k.dma("sp", self.xs_scr_s, x[0:TS])

    def tm_to_T_s(self, t, nblk, dst):
        k, TS = self.k, self.TS
        bk = self.bank()
        for j in range(nblk):
            k.tr(bk[:, j * 128:j * 128 + TS], t[0:TS, j * 128:(j + 1) * 128], self.const(0)[0:TS, 0:TS])
        k.copy("act", dst, bk.re("p (a b) -> p a b", a=4)[:, 0:nblk, 0:TS])

    @staticmethod
    def state_sample(self, tp, sl, P, vb, nwT, qdT, QKm, kdec, eGL, consts):
        k, s, d, l, NS = self.k, self.s, self.d, self.cur_l, self.NSEQ
        segb = s["segm"][:, None, :, :].bcast([128, 4, NS, tp])
        msk = self.tmp("dmsk", [128, 4, NS, 64], BF16, n=1)
        k.tt("dve", msk, nwT[:, :, None, 0:tp].bcast([128, 4, NS, tp]), segb, ALU.mult)
        zrhs = s["mw"][2][:, 0:512]
        vnk = self.bank(pin=True)
        vn = vnk.re("p (h e) -> p h e", h=4)
        k.mm(vnk[0:tp, :], s["zeros_b"][:, 0:tp], zrhs)
        for h in range(4):
            k.mm(vn[0:tp, h, :], P[0:tp, h, 0:tp], vb[0:tp, h, :], start=False, stop=False)
        src = lambda q: d["a_rec_s_in"][l][q].rearrange("h k v -> k h v")
        for q in range(NS):
            s32 = self.tmp("ds32", [128, 4, 128], F32, n=2)
            sb = self.tmp("dsb", [128, 4, 128], BF16, n=2)
            k.dma("sp", s32, src(q))
            k.copy("act" if q % 2 else "pool", sb, s32)
            for h in range(4):
                k.mm(vn[0:tp, h, :], msk[:, h, q, :], sb[:, h, :], start=False, stop=False)
        vnb = self.dn_scratch[:, 0:4, :]
        k.copy("act", vnb[0:tp], vn[0:tp])
        if self.debug and l == 0:
            dv = self.dn_sq[:, 0:4, :]
            k.copy("act", dv[0:tp], vn[0:tp])
            k.dma("sp", d["dbg_vn"], dv[0:tp], is_output=True)
            dv2 = self.dn_sq[:, 4:8, :]
            k.copy("act", dv2[0:tp], kdec[0:tp])
            k.dma("sp", d["dbg_kd"], dv2[0:tp], is_output=True)
        self.unpin(vnk)
        k.tt("dve", msk, qdT[:, :, None, 0:tp].bcast([128, 4, NS, tp]), segb, ALU.mult)
        obk = self.bank(pin=True)
        self.ob_bank = obk
        ob = obk.re("p (h e) -> p h e", h=4)
        k.mm(obk[0:tp, :], s["zeros_b"][:, 0:tp], zrhs)
        for h in range(4):
            k.mm(ob[0:tp, h, :], QKm[0:tp, h, 0:tp], vnb[0:tp, h, :], start=False, stop=False)
        for q in range(NS):
            s32 = self.tmp("ds32", [128, 4, 128], F32, n=2)
            sb = self.tmp("dsb", [128, 4, 128], BF16, n=2)
            k.dma("sp", s32, src(q))
            k.copy("act" if q % 2 else "pool", sb, s32)
            for h in range(4):
                k.mm(ob[0:tp, h, :], msk[:, h, q, :], sb[:, h, :], start=False, stop=False)
            kdm = self.tmp("dkdm", [128, 4, 128], BF16, n=2)
            k.ts("dve", kdm[0:tp], kdec[0:tp], s["sel"][0:tp, q:q + 1], None, op0=ALU.mult)
            dS = self.bank().re("p (h e) -> p h e", h=4)
            for h in range(4):
                k.mm(dS[:, h, :], kdm[0:tp, h, :], vnb[0:tp, h, :])
            sn = self.dn_sq[:, 4 * (q % 2):4 * (q % 2) + 4, :]
            k.tt("pool", sn, s32, eGL[:, :, q:q + 1].bcast([128, 4, 128]), ALU.mult)
            k.tt("dve", sn, sn, dS, ALU.add)
            k.dma("sp", d["a_rec_s"][l][q].rearrange("h k v -> k h v"), sn, is_output=True)
        return ob

    def sample_attention(self, l):
        k, s, d, NS, TS = self.k, self.s, self.d, self.NSEQ, self.TS
        acck = self.bank(pin=True)
        k.mm(acck[0:TS, 0:260], s["zeros_b"][:, 0:TS], s["mw"][2][:, 0:260])
        for q in range(NS):
            for g, (W, dil) in enumerate(B_GROUPS):
                kv = self.tmp("skv", [128, 4, 512], F32, n=3)
                qb = self.tmp("sqb", [128, 4, 256], F32, n=3)
                if g == 0:
                    k.dma("sp", kv[:, 0, :], d["cb0"][l][q])
                    kk_ = kv[:, 0:1, 0:256].bcast([128, 4, 256])
                    vv_ = kv[:, 0:1, 256:512].bcast([128, 4, 256])
                else:
                    k.dma("sp", kv, d[f"cb{g}"][l][q].rearrange("(m j) c -> m j c", j=dil)[:, 0:4, :])
                    kk_, vv_ = kv[:, :, 0:256], kv[:, :, 256:512]
                k.dma("sp", qb, self.qs_scr[4 * q:4 * q + 4, g * 256:(g + 1) * 256].pbcast(128))
                prod = self.tmp("sprod", [128, 4, 4, 64], F32, n=2)
                sp4 = lambda a: a.re("p t (h e) -> p t h e", e=64)
                k.tt("dve", prod, sp4(kk_), sp4(qb), ALU.mult)
                sc = self.tmp("ssc", [128, 16], F32, n=3)
                k.reduce("dve", sc, prod.re("p t h e -> p (t h) e"), ALU.add)
                pb = self.tmp("spb", [128, 16], BF16, n=3)
                k.act(pb, sc, AF.Exp, scale=0.125)
                if g == 0:
                    pb3 = pb.re("p (t h) -> p t h", h=4)
                    k.tt("pool", pb3, pb3, s["m0"][:, :, None].bcast([128, 4, 4]), ALU.mult)
                Wt = self.tmp("sW", [128, 4, 4, 64], BF16, n=3)
                k.tt("pool!" if (3 * q + g) % 2 else "dve", Wt, sp4(vv_), pb.re("p (t h) -> p t h", h=4)[:, :, :, None].bcast([128, 4, 4, 64]), ALU.mult)
                for t in range(4):
                    tok = 4 * q + t
                    E = s["zsel"][:, 63 - tok:127 - tok]
                    k.mm(acck[0:TS, 0:256], E, Wt[:, t, :, :].re("p h e -> p (h e)"), start=False, stop=False)
                    k.mm(acck[0:TS, 256:260], E, pb[:, 4 * t:4 * t + 4], start=False, stop=False)
        for g in range(3):
            for hh in range(4):
                pair, ph = hh // 2, hh % 2
                prt = slice(64 * ph, 64 * ph + 64)
                st = self.bank()
                k.mm(st[0:TS, 0:TS], s["kTn"][prt, 2 * g + pair, :], s["qT"][prt, 2 * g + pair, 0:TS])
                pn = self.tmp("spn", [128, 64], BF16, n=2)
                k.act(pn[0:TS], st[0:TS, 0:TS], AF.Exp, scale=0.125)
                k.tt("dve", pn[0:TS], pn[0:TS], s["smask"][0:TS, g, :], ALU.mult)
                k.mm(acck[0:TS, hh * 64:(hh + 1) * 64], pn[0:TS], s["vns"][0:TS, g, hh * 64:(hh + 1) * 64], start=False, stop=False)
                k.mm(acck[0:TS, 256 + hh:257 + hh], pn[0:TS], s["ones_b"][0:TS, 0:1], start=False, stop=False)
        rd = self.tmp("srd", [128, 4], F32, n=1)
        k.recip(rd[0:TS], acck[0:TS, 256:260])
        ob = self.tmp("sob", [128, 4, 64], F32, n=1)
        k.tt("dve", ob[0:TS], acck[0:TS, 0:256].re("p (h e) -> p h e", e=64), rd[0:TS, :, None].bcast([TS, 4, 64]), ALU.mult)
        self.unpin(acck)
        ob2 = ob.re("p h e -> p (h e)")
        bk = self.bank()
        for j in range(2):
            k.tr(bk[:, j * 128:j * 128 + TS], ob2[0:TS, j * 128:(j + 1) * 128], self.const(0)[0:TS, 0:TS])
        k.copy("act", s["oT"][:, 4:6, 0:TS], bk.re("p (a b) -> p a b", a=4)[:, 0:2, 0:TS])


def _rope_table(pos):
    half = 8
    inv = (500000.0 ** (-np.arange(half, dtype=np.float32) / half)).astype(np.float32)
    ang = pos.astype(np.float32)[:, None] * inv[None, :]
    return np.concatenate([np.cos(ang), np.sin(ang)], axis=1).astype(np.float32)


def _mask_w(W, dil):
    kk = np.arange(128)[:, None]
    u = np.arange(W + 128)[None, :]
    dd = u - kk
    return ((dd >= 0) & (dd <= W) & (dd % dil == 0)).astype(np.float32)


def _consts():
    i = np.arange(128)
    c = np.zeros((128, 6, 128), np.float32)
    c[:, 0] = np.eye(128)
    c[:, 1] = (i[:, None] <= i[None, :])
    c[:, 2] = (i[:, None] < i[None, :])
    c[:, 3] = (i[:, None] > i[None, :])
    c[:, 4] = 1.0
    c[:, 5] = 1.0
    return c


def prep_shared(inp, L):
    f = lambda a: np.ascontiguousarray(np.asarray(a, dtype=np.float32))
    w_in = f(inp["w_in"])[:L]
    sp = np.cumsum([0, 1536, 4, 4, 512, 2304, 512, 512, 3072])
    qkv_a, b_a, a_a, z_a, qkv_b, x_c, g_c, gates = [w_in[:, :, sp[i]:sp[i + 1]] for i in range(8)]
    qb, kb, vb = qkv_b[..., 0:768], qkv_b[..., 768:1536], qkv_b[..., 1536:2304]
    gsl = lambda a, g: a[..., g * 256:(g + 1) * 256]
    win = np.concatenate([qkv_a, x_c, g_c, gates, z_a, gsl(qb, 0), gsl(qb, 1), gsl(qb, 2), b_a, a_a,
                          gsl(kb, 0), gsl(vb, 0), gsl(kb, 1), gsl(vb, 1), gsl(kb, 2), gsl(vb, 2)], axis=-1)
    assert win.shape[-1] == N_INP
    out = {"win": f(win)}
    out["wpa"], out["wpb"], out["wpc"], out["wo"] = f(inp["w_pa"])[:L], f(inp["w_pb"])[:L], f(inp["w_pc"])[:L], f(inp["w_o"])[:L]
    out["fup"], out["fdown"] = f(inp["f_up"])[:L], f(inp["f_down"])[:L]
    wbd = np.zeros((L, 128, 8, 128), np.float32)
    for ri, nm in enumerate(("c_w_r", "c_w_i")):
        w = f(inp[nm])[:L]
        for blk in range(4):
            for hb in range(2):
                wbd[:, hb * 64:(hb + 1) * 64, ri * 4 + blk, hb * 64:(hb + 1) * 64] = w[:, blk * 2 + hb]
    out["wbd"] = wbd
    out["rvec"] = f(np.concatenate([inp["ln1_g"][:L], inp["ln1_b"][:L], inp["ln2_g"][:L], inp["ln2_b"][:L],
                                    inp["a_norm_w"][:L], inp["a_A_log"][:L], inp["a_dt_bias"][:L]], axis=1))
    out["rvec0"] = f(np.concatenate([inp["ln_in_g"], inp["ln_in_b"]]))
    pm = lambda a, nb: np.moveaxis(f(a)[:L].reshape(L, -1, nb, 128), 3, 1)

    def pp(a, nb):
        a = f(a)[:L]
        J = a.shape[1]
        return a.reshape(L, J, nb, 128).transpose(0, 3, 2, 1).reshape(L, 128, nb * J)

    def p1(a, nb):
        return f(a)[:L].reshape(L, nb, 128).transpose(0, 2, 1)

    out["pvec"] = f(np.concatenate([pp(inp["a_conv_w"], 12), pp(inp["c_conv_w"], 4), p1(inp["c_conv_b"], 4), p1(inp["c_b_r"], 4),
                                    p1(inp["c_b_i"], 4), p1(inp["c_lam"], 4), pp(inp["f_conv_w"], 22), p1(inp["f_conv_b"], 22)], axis=2))
    assert out["pvec"].shape[2] == NPV
    out["consts"] = _consts()
    for g, (W, dil) in enumerate(B_GROUPS):
        out[f"mw{g}"] = _mask_w(W, dil)
    return out


_PROG_CACHE = {}


def run_model(inp, T, NSEQ, L, n_cores, xp_list, with_samples=False, debug=False):
    key = (T, NSEQ, L, with_samples, debug)
    if key not in _PROG_CACHE:
        _PROG_CACHE[key] = Model(T, NSEQ, L, with_samples, debug)
    m = _PROG_CACHE[key]
    sh = prep_shared(inp, L)
    sh["csp"] = _rope_table(np.arange(T))
    in_maps = []
    for c in range(n_cores):
        im = dict(sh)
        im["xp"] = np.ascontiguousarray(xp_list[c], dtype=np.float32)
        in_maps.append(im)
    res = run_bass_kernel_spmd(m.nc, in_maps, core_ids=list(range(n_cores)))
    return m, res.results


def assemble_prompt(m, results, cores):
    L, T = m.L, m.T
    R = [results[c] for c in cores]
    st = lambda nm: np.stack([r[nm] for r in R], axis=1)
    yp = np.stack([r["yp"] for r in R], axis=0)
    a_conv = st("a_conv_p").transpose(0, 1, 4, 2, 3).reshape(L, len(R), 3, 1536)
    a_rec = st("a_rec_p")
    bs = [st(f"b{g}_p").reshape(L, len(R), m.Weff[g], 2, 4, 64) for g in range(3)]
    c_conv = st("c_conv_p").transpose(0, 1, 4, 2, 3).reshape(L, len(R), 3, 512)
    c_h = st("c_h_p").reshape(L, len(R), 512)
    f_conv = st("f_conv_p").transpose(0, 1, 4, 2, 3).reshape(L, len(R), 2, D_FF)
    return [yp, a_conv, a_rec, bs[0], bs[1], bs[2], c_conv, c_h, f_conv]


PAST_LEN = 2048


def sample_consts():
    i = np.arange(128)
    valid = (i[:, None] < 64) & (i[None, :] < 64)
    same = ((i[:, None] // 4) == (i[None, :] // 4)) & valid
    c = np.zeros((128, 6, 128), np.float32)
    c[:, 0] = np.eye(128)
    c[:, 1] = same & (i[:, None] <= i[None, :])
    c[:, 2] = same & (i[:, None] < i[None, :])
    c[:, 3] = same & (i[:, None] > i[None, :])
    c[:, 4] = 1.0
    c[:, 5] = same
    out = {"consts_s": c}
    j = np.arange(64)
    sm = np.zeros((64, 3, 64), np.float32)
    ss = (j[:, None] // 4) == (j[None, :] // 4)
    sm[:, 0] = ss & ((j[:, None] % 4) <= (j[None, :] % 4))
    sm[:, 1] = ss & ((j[:, None] % 4) == (j[None, :] % 4))
    sm[:, 2] = sm[:, 1]
    out["smask"] = sm
    out["m0"] = (i[:, None] >= np.arange(4)[None, :]).astype(np.float32)
    z = np.zeros((128, 127), np.float32)
    z[:, 63] = 1.0
    out["zsel"] = z
    out["segm"] = np.broadcast_to(((j[None, :] // 4) == np.arange(16)[:, None]).astype(np.float32)[None], (128, 16, 64)).copy()
    out["sel"] = (((i[:, None] // 4) == np.arange(16)[None, :]) & (i[:, None] < 64)).astype(np.float32)
    out["css"] = _rope_table(PAST_LEN + (j % 4))
    return out


def prep_samples(inp, L, seqs):
    f = lambda a: np.ascontiguousarray(a, dtype=np.float32)
    NS = len(seqs)
    o = {}
    o["xs"] = f(np.asarray(inp["x_sample"])[seqs].reshape(NS * 4, D))

    def fm(a, nb):
        a = np.asarray(a)[:L][:, seqs]
        J = a.shape[2]
        return f(a.reshape(L, NS, J, nb, 128).transpose(0, 4, 3, 1, 2))

    o["a_conv_s_in"] = fm(inp["state_a_conv"], 12)
    o["a_rec_s_in"] = f(np.asarray(inp["state_a_rec"])[:L][:, seqs])
    o["c_conv_s_in"] = fm(inp["state_c_conv"], 4)
    o["c_h_s_in"] = f(np.asarray(inp["state_c_h"])[:L][:, seqs].reshape(L, NS, 4, 128).transpose(0, 3, 2, 1))
    o["f_conv_s_in"] = fm(inp["state_f_conv"], 22)
    for g, nm in enumerate(("cache_b_w128", "cache_b_w512", "cache_b_w2048")):
        a = np.asarray(inp[nm])[:L][:, seqs]
        o[f"cb{g}"] = f(a.reshape(L, NS, a.shape[2], 512))
    return o


def assemble_samples(m, results, cores):
    L = m.L
    R = [results[c] for c in cores]
    cat = lambda nm, ax: np.concatenate([r[nm] for r in R], axis=ax)
    NS = 16 * len(R)
    ys = cat("ys", 0).reshape(NS, 4, D)
    a_conv = cat("a_conv_s", 3).transpose(0, 3, 4, 1, 2).reshape(L, NS, 3, 1536)
    a_rec = cat("a_rec_s", 1)
    bs = [cat(f"b{g}_s", 1).reshape(L, NS, 4, 2, 4, 64) for g in range(3)]
    c_conv = cat("c_conv_s", 3).transpose(0, 3, 4, 1, 2).reshape(L, NS, 3, 512)
    c_h = cat("c_h_s", 3).transpose(0, 3, 1, 2).reshape(L, NS, 512)
    f_conv = cat("f_conv_s", 3).transpose(0, 3, 4, 1, 2).reshape(L, NS, 2, D_FF)
    return [ys, a_conv, a_rec, bs[0], bs[1], bs[2], c_conv, c_h, f_conv]


def run_full(inp, T, L, n_cores, prompt_of_core, seqs_of_core, debug=False):
    key = (T, 16, L, True, debug)
    if key not in _PROG_CACHE:
        _PROG_CACHE[key] = Model(T, 16, L, True, debug)
    m = _PROG_CACHE[key]
    sh = prep_shared(inp, L)
    sh["csp"] = _rope_table(np.arange(T))
    sh.update(sample_consts())
    xp = np.asarray(inp["x_prompt"])
    in_maps = []
    for c in range(n_cores):
        im = dict(sh)
        im["xp"] = np.ascontiguousarray(xp[prompt_of_core[c]], dtype=np.float32)
        im.update(prep_samples(inp, L, seqs_of_core[c]))
        in_maps.append(im)
    res = run_bass_kernel_spmd(m.nc, in_maps, core_ids=list(range(n_cores)))
    return m, res.results


def kernel(**inputs):
    n = 8
    prompt_of_core = [c % 4 for c in range(n)]
    seqs_of_core = [list(range(16 * c, 16 * c + 16)) for c in range(n)]
    m, results = run_full(inputs, 4096, DEPTH, n, prompt_of_core, seqs_of_core)
    po = assemble_prompt(m, results, [0, 1, 2, 3])
    so = assemble_samples(m, results, list(range(n)))
    outs = [po[0], so[0]] + po[1:] + so[1:]
    return tuple(np.ascontiguousarray(o, dtype=np.float32) for o in outs)
```

```python
import numpy as np
import concourse.bass as bass
import concourse.mybir as mybir
from concourse.bass_utils import run_bass_kernel_spmd

dt = mybir.dt
AF = mybir.ActivationFunctionType
ALU = mybir.AluOpType
F32, BF16 = dt.float32, dt.bfloat16

D = 1024
DEPTH = 4
A_QKV = 1536
B_W = 768
C_W = 512
D_FF = 2816
N_IN = 8456
ALPHA = (2 * DEPTH) ** 0.25
LN_EPS = 1e-5
NORM_EPS = 1e-6
O_QKVA, O_XC, O_GC, O_GATE, O_Z, O_QKVB, O_BA = 0, 1536, 2048, 2560, 5632, 6144, 8448
N_INP = 8456


class Buf:
    __slots__ = ("name", "w", "r", "subs")

    def __init__(self, name):
        self.name = name
        self.w = None
        self.r = {}
        self.subs = {}


class V:
    def __init__(self, ap, bufs):
        self.ap = ap
        self.bufs = bufs

    def __getitem__(self, idx):
        return V(self.ap[idx], self.bufs)

    def re(self, pat, **kw):
        return V(self.ap.rearrange(pat, **kw), self.bufs)

    def bc(self, d):
        return V(self.ap.bitcast(d), self.bufs)

    def bcast(self, shape):
        return V(self.ap.broadcast_to(shape), self.bufs)

    def pbcast(self, n):
        return V(self.ap.partition_broadcast(n), self.bufs)

    def subs(self, keys):
        out = []
        for key in keys:
            out.extend(self.sub(key).bufs)
        return V(self.ap, out)

    def sub(self, key):
        out = []
        for b in self.bufs:
            if key not in b.subs:
                b.subs[key] = Buf(f"{b.name}.{key}")
            out.append(b.subs[key])
        return V(self.ap, out)


class Stream:
    def __init__(self, sem):
        self.sem = sem
        self.count = 0


class KB:
    def __init__(self, nc, n_streams=24):
        self.nc = nc
        self.eng = {"pe": nc.tensor, "act": nc.scalar, "dve": nc.vector, "pool": nc.gpsimd, "sp": nc.sync}
        self.prog = {}
        self.cnt = {}
        self.waited = {e: {} for e in self.eng}
        for e in ("pe", "act", "dve", "pool"):
            self.prog[e] = nc.alloc_semaphore(f"prog_{e}")
            self.cnt[e] = 0
        self.streams = [Stream(nc.alloc_semaphore(f"dq{i}")) for i in range(n_streams)]
        self.rr = 0
        self.pool_streams = [Stream(nc.alloc_semaphore(f"pq{i}")) for i in range(6)]
        self.prr = 0
        self.semname = {}
        self.out_events = {}
        self.n_inst = 0

    def sb(self, name, shape, dtype):
        t = self.nc.alloc_sbuf_tensor("s_" + name, list(shape), dtype)
        return V(t[tuple(slice(None) for _ in shape)], [Buf(name)])

    def ps(self, name, shape, dtype=F32):
        t = self.nc.alloc_psum_tensor("p_" + name, list(shape), dtype)
        return V(t[tuple(slice(None) for _ in shape)], [Buf(name)])

    def dram(self, name, shape, dtype, kind):
        return self.nc.dram_tensor(name, list(shape), dtype, kind=kind).ap()

    def scratch(self, name, shape, dtype):
        ap = self.nc.dram_tensor(name, list(shape), dtype, kind="Internal").ap()
        return V(ap, [Buf(name)])

    def _deps(self, reads, writes):
        evs = {}

        def add(ev):
            if ev is None:
                return
            s, v = ev
            if evs.get(s, (None, 0))[1] < v:
                evs[s] = (s, v)

        for v in reads:
            for b in v.bufs:
                add(b.w)
        for v in writes:
            for b in v.bufs:
                add(b.w)
                for ev in b.r.values():
                    add(ev)
        return list(evs.values())

    def _wait(self, e, evs):
        eng = self.eng[e]
        w = self.waited[e]
        for s, v in evs:
            if e == "pe" and s is self.prog.get("pe"):
                continue
            key = id(s)
            if w.get(key, 0) >= v:
                continue
            eng.wait_ge(s, v)
            w[key] = v

    def _commit(self, ev, reads, writes):
        s, v = ev
        for x in reads:
            for b in x.bufs:
                b.r[id(s)] = ev
        for x in writes:
            for b in x.bufs:
                b.w = ev
                b.r = {}

    POOL_COMPUTE = False

    def emit(self, e, fn, reads, writes):
        reads = [r for r in reads if isinstance(r, V)]
        writes = [r for r in writes if isinstance(r, V)]
        self._wait(e, self._deps(reads, writes))
        inst = fn()
        self.cnt[e] += 1
        inst.then_inc(self.prog[e], 1)
        self._commit((self.prog[e], self.cnt[e]), reads, writes)
        self.n_inst += 1
        return inst

    def dma(self, q, out, in_, is_output=False, **kw):
        if q == "pool":
            st = self.pool_streams[self.prr % len(self.pool_streams)]
            self.prr += 1
        else:
            st = self.streams[self.rr % len(self.streams)]
            self.rr += 1
        reads = [in_] if isinstance(in_, V) else []
        writes = [out] if isinstance(out, V) else []
        evs = self._deps(reads, writes)
        if st.count:
            evs.append((st.sem, st.count))
        self._wait(q, evs)
        o = out.ap if isinstance(out, V) else out
        i = in_.ap if isinstance(in_, V) else in_
        inst = self.eng[q].dma_start(out=o, in_=i, **kw)
        st.count += 16
        inst.then_inc(st.sem, 16)
        self._commit((st.sem, st.count), reads, writes)
        if is_output:
            self.out_events[id(st.sem)] = (st.sem, st.count)
        self.n_inst += 1

    def finish(self):
        evs = [(st.sem, st.count) for st in self.streams + self.pool_streams if st.count]
        self._wait("sp", evs)
        self._wait("sp", [(self.prog[e], self.cnt[e]) for e in self.prog if self.cnt[e]])

    @staticmethod
    def _a(x):
        return x.ap if isinstance(x, V) else x

    def mm(self, out, lhsT, rhs, start=True, stop=True):
        return self.emit("pe", lambda: self.nc.tensor.matmul(out.ap, lhsT.ap, rhs.ap, start=start, stop=stop),
                         [lhsT, rhs] + ([] if start else [out]), [out])

    def tr(self, out, in_, ident):
        return self.emit("pe", lambda: self.nc.tensor.transpose(out.ap, in_.ap, ident.ap), [in_, ident], [out])

    def act(self, out, in_, func, bias=None, scale=None, accum_out=None):
        kw = {}
        rd = [in_]
        if bias is not None:
            kw["bias"] = self._a(bias)
            rd.append(bias)
        if scale is not None:
            kw["scale"] = self._a(scale)
            rd.append(scale)
        wr = [out]
        if accum_out is not None:
            kw["accum_out"] = accum_out.ap
            wr.append(accum_out)
        return self.emit("act", lambda: self.nc.scalar.activation(out.ap, in_.ap, func, **kw), rd, wr)

    def ts(self, e, out, in0, s1, s2=None, op0=ALU.mult, op1=None):
        if e == "pool" and not self.POOL_COMPUTE:
            e = "dve"
        kw = {}
        if op1 is not None:
            kw["op1"] = op1
        return self.emit(e, lambda: self.eng[e].tensor_scalar(out.ap, in0.ap, self._a(s1), self._a(s2), op0, **kw),
                         [in0, s1, s2], [out])

    def tt(self, e, out, in0, in1, op):
        if e == "pool!":
            e = "pool"
        elif e == "pool" and not self.POOL_COMPUTE:
            e = "dve"
        return self.emit(e, lambda: self.eng[e].tensor_tensor(out.ap, in0.ap, in1.ap, op), [in0, in1], [out])

    def stt(self, out, in0, scalar, in1, op0, op1):
        return self.emit("dve", lambda: self.nc.vector.scalar_tensor_tensor(out.ap, in0.ap, self._a(scalar), in1.ap, op0, op1),
                         [in0, scalar, in1], [out])

    def copy(self, e, out, in_):
        if e == "pool" and not self.POOL_COMPUTE:
            e = "act"
        if e == "act":
            return self.emit("act", lambda: self.nc.scalar.copy(out.ap, in_.ap), [in_], [out])
        return self.emit(e, lambda: self.eng[e].tensor_copy(out.ap, in_.ap), [in_], [out])

    def memset(self, e, out, val):
        if e == "pool" and not self.POOL_COMPUTE:
            e = "dve"
        return self.emit(e, lambda: self.eng[e].memset(out.ap, val), [], [out])

    def recip(self, out, in_):
        return self.emit("dve", lambda: self.nc.vector.reciprocal(out.ap, in_.ap), [in_], [out])

    def reduce(self, e, out, in_, op, axis=mybir.AxisListType.X):
        return self.emit(e, lambda: self.eng[e].tensor_reduce(out.ap, in_.ap, axis, op), [in_], [out])

    def bn_stats(self, out, in_):
        return self.emit("dve", lambda: self.nc.vector.bn_stats(out.ap, in_.ap), [in_], [out])

    def bn_aggr(self, out, in_):
        return self.emit("dve", lambda: self.nc.vector.bn_aggr(out.ap, in_.ap), [in_], [out])

    def scan(self, out, d0, d1, init, op0, op1):
        return self.emit("dve", lambda: self.nc.vector.tensor_tensor_scan(out.ap, d0.ap, d1.ap, self._a(init), op0, op1),
                         [d0, d1, init], [out])


B_GROUPS = ((128, 1), (512, 4), (2048, 16))
NRV = 4 * 1024 + 128 + 8
PV_ACW, PV_CCW, PV_CCB, PV_CBR, PV_CBI, PV_LAM, PV_FCW, PV_FCB = 0, 48, 64, 68, 72, 76, 80, 146
NPV = 168
TM_SLOTS = [("z", 5632, 512), ("q01", 6144, 512), ("q2ba", 6656, 264), ("kv0", 6920, 512), ("kv1", 7432, 512), ("kv2", 7944, 512)]


class Model:
    def __init__(self, T, NSEQ, L, with_samples=True, debug=False):
        self.debug = debug
        self.T, self.NSEQ, self.L = T, NSEQ, L
        self.CW = 512 if T >= 512 else T
        assert T % self.CW == 0 and self.CW % 128 == 0
        self.NCH = T // self.CW
        self.NT = self.CW // 128
        self.with_samples = with_samples and NSEQ > 0
        self.TS = 4 * NSEQ
        nc = bass.Bass("TRN2", target_bir_lowering=False)
        self.nc = nc
        self.k = KB(nc, n_streams=32)
        self.Weff = [min(w, T) for w, _ in B_GROUPS]
        self.ring = [8, 8, 20]
        self.build()

    def declare(self):
        k, T, L, CW = self.k, self.T, self.L, self.CW
        I, O = "ExternalInput", "ExternalOutput"
        d = {}
        d["xp"] = k.dram("xp", [T, D], F32, I)
        d["win"] = k.dram("win", [L, D, N_INP], F32, I)
        d["wpa"] = k.dram("wpa", [L, 512, D], F32, I)
        d["wpb"] = k.dram("wpb", [L, 256, D], F32, I)
        d["wpc"] = k.dram("wpc", [L, 512, D], F32, I)
        d["wo"] = k.dram("wo", [L, D, D], F32, I)
        d["fup"] = k.dram("fup", [L, D, 2 * D_FF], F32, I)
        d["fdown"] = k.dram("fdown", [L, D_FF, D], F32, I)
        d["wbd"] = k.dram("wbd", [L, 128, 8, 128], F32, I)
        d["rvec"] = k.dram("rvec", [L, NRV], F32, I)
        d["rvec0"] = k.dram("rvec0", [2048], F32, I)
        d["pvec"] = k.dram("pvec", [L, 128, NPV], F32, I)
        d["csp"] = k.dram("csp", [T, 16], F32, I)
        d["consts"] = k.dram("consts", [128, 6, 128], F32, I)
        d["mw0"] = k.dram("mw0", [128, 256], F32, I)
        d["mw1"] = k.dram("mw1", [128, 640], F32, I)
        d["mw2"] = k.dram("mw2", [128, 2176], F32, I)
        d["yp"] = k.dram("yp", [T, D], F32, O)
        d["a_conv_p"] = k.dram("a_conv_p", [L, 12, 128, 3], F32, O)
        d["a_rec_p"] = k.dram("a_rec_p", [L, 4, 128, 128], F32, O)
        for g in range(3):
            d[f"b{g}_p"] = k.dram(f"b{g}_p", [L, self.Weff[g], 512], F32, O)
        d["c_conv_p"] = k.dram("c_conv_p", [L, 4, 128, 3], F32, O)
        d["c_h_p"] = k.dram("c_h_p", [L, 4, 128], F32, O)
        d["f_conv_p"] = k.dram("f_conv_p", [L, 22, 128, 2], F32, O)
        self.xs_scr = k.scratch("xs_scr", [T, D], F32)
        self.wscr = {}
        for nm, shp in (("win", [D, N_INP]), ("wpa", [512, D]), ("wpb", [256, D]), ("wpc", [512, D]), ("wo", [D, D]),
                        ("fup", [D, 2 * D_FF]), ("fdown", [D_FF, D])):
            for l_ in range(L):
                self.wscr[(nm, l_)] = k.scratch(f"wb_{nm}_{l_}", shp, BF16)
        if self.with_samples:
            NS, TS = self.NSEQ, self.TS
            assert NS == 16
            d["xs"] = k.dram("xs", [TS, D], F32, I)
            d["css"] = k.dram("css", [TS, 16], F32, I)
            d["consts_s"] = k.dram("consts_s", [128, 6, 128], F32, I)
            d["smask"] = k.dram("smask", [TS, 3, TS], F32, I)
            d["m0"] = k.dram("m0", [128, 4], F32, I)
            d["zsel"] = k.dram("zsel", [128, 127], F32, I)
            d["segm"] = k.dram("segm", [128, 16, TS], F32, I)
            d["sel"] = k.dram("sel", [128, 16], F32, I)
            d["a_conv_s_in"] = k.dram("a_conv_s_in", [L, 128, 12, NS, 3], F32, I)
            d["a_rec_s_in"] = k.dram("a_rec_s_in", [L, NS, 4, 128, 128], F32, I)
            d["c_conv_s_in"] = k.dram("c_conv_s_in", [L, 128, 4, NS, 3], F32, I)
            d["c_h_s_in"] = k.dram("c_h_s_in", [L, 128, 4, NS], F32, I)
            d["f_conv_s_in"] = k.dram("f_conv_s_in", [L, 128, 22, NS, 2], F32, I)
            for g, (W, dil) in enumerate(B_GROUPS):
                d[f"cb{g}"] = k.dram(f"cb{g}", [L, NS, W, 512], F32, I)
                d[f"b{g}_s"] = k.dram(f"b{g}_s", [L, TS, 512], F32, O)
            d["ys"] = k.dram("ys", [TS, D], F32, O)
            d["a_conv_s"] = k.dram("a_conv_s", [L, 12, 128, NS, 3], F32, O)
            d["a_rec_s"] = k.dram("a_rec_s", [L, NS, 4, 128, 128], F32, O)
            d["c_conv_s"] = k.dram("c_conv_s", [L, 4, 128, NS, 3], F32, O)
            d["c_h_s"] = k.dram("c_h_s", [L, 4, 128, NS], F32, O)
            d["f_conv_s"] = k.dram("f_conv_s", [L, 22, 128, NS, 2], F32, O)
            self.xs_scr_s = k.scratch("xs_scr_s", [TS, D], F32)
            self.qs_scr = k.scratch("qs_scr", [TS, 768], F32)
        if getattr(self, "debug", False):
            d["dbg_oT"] = k.dram("dbg_oT", [L, 128, 10, T], F32, O)
            d["dbg_sm"] = k.dram("dbg_sm", [T // 128, 128, 64], F32, O)
            d["dbg_gb"] = k.dram("dbg_gb", [T // 128, 128, 512], F32, O)
            d["dbg_vn"] = k.dram("dbg_vn", [64, 4, 128], F32, O)
            d["dbg_sms"] = k.dram("dbg_sms", [128, 64], F32, O)
            d["dbg_kd"] = k.dram("dbg_kd", [64, 4, 128], F32, O)
        self.d = d

    ARENA = 17920

    def alloc(self):
        k, CW, NT = self.k, self.CW, self.NT
        s = {}
        s["consts"] = k.sb("consts", [128, 6, 128], F32)
        s["ident_b"] = k.sb("ident_b", [128, 128], BF16)
        s["ones_b"] = k.sb("ones_b", [128, 128], BF16)
        s["zeros_b"] = k.sb("zeros_b", [128, 128], BF16)
        s["mw"] = [k.sb("mw0", [128, 256], BF16), k.sb("mw1", [128, 640], BF16), k.sb("mw2", [128, 2176], BF16)]
        s["lnbuf"] = k.sb("lnbuf", [128, 2048], F32)
        s["rsm"] = k.sb("rsm", [128, 136], F32)
        s["pvec"] = k.sb("pvec", [128, NPV], F32)
        s["lay"] = k.sb("lay", [128, 16], F32)
        s["wbd"] = k.sb("wbd", [128, 8, 128], BF16)
        s["xT"] = k.sb("xT", [128, 8, CW], BF16)
        s["xres"] = [k.sb(f"xres{i}", [128, D], F32) for i in range(NT)]
        s["wslot"] = [k.sb(f"wslot{i}", [128, 8, 512], BF16) for i in range(3)]
        s["oT"] = k.sb("oT", [128, 10, CW], BF16)
        s["zs"] = [k.sb(f"zs{i}", [128, 512], BF16) for i in range(NT)]
        s["ba"] = [k.sb(f"ba{i}", [128, 8], F32) for i in range(NT)]
        s["cs"] = [k.sb(f"cs{i}", [128, 16], F32) for i in range(NT)]
        s["qT"] = k.sb("qT", [128, 6, CW], BF16)
        s["kT"] = [k.sb(f"kT{g}", [128, 2, self.ring[g] * 128], BF16) for g in range(3)]
        s["vr"] = [k.sb(f"vr{g}", [128, self.ring[g], 256], BF16) for g in range(3)]
        s["S"] = k.sb("S", [128, 4, 128], F32)
        s["Sb"] = k.sb("Sb", [128, 4, 128], BF16)
        s["hlast"] = k.sb("hlast", [128, 4], F32)
        s["fh"] = k.sb("fh", [128, 22, 2], F32)
        s["ahalo"] = k.sb("ahalo", [128, 12, 3], F32)
        s["chalo"] = k.sb("chalo", [128, 4, 3], F32)
        s["psum"] = k.ps("psum", [128, 8, 512])
        if self.with_samples:
            s["consts_s"] = k.sb("consts_s", [128, 6, 128], F32)
            s["smask"] = k.sb("smask", [128, 3, 64], BF16)
            s["m0"] = k.sb("m0", [128, 4], BF16)
            s["zsel"] = k.sb("zsel", [128, 127], BF16)
            s["segm"] = k.sb("segm", [128, 16, 64], BF16)
            s["sel"] = k.sb("sel", [128, 16], F32)
            s["kTn"] = k.sb("kTn", [128, 6, 64], BF16)
            s["vns"] = k.sb("vns", [128, 3, 256], BF16)
        s["arena"] = k.sb("arena", [128, self.ARENA], F32)
        self.s = s
        print('SBUF bytes remaining after alloc:', self.nc.sbuf_bytes_remaining() if callable(getattr(self.nc, 'sbuf_bytes_remaining', None)) else self.nc.sbuf_bytes_remaining)
        self.f_v3 = None
        self.use_f32r = False
        self.bank_i = 0
        self.pinned = set()
        self.slot_i = 0
        self.tmp_i = {}
        self.phase_name = None
        self.phase_off = 0
        self.marks = []
        self.phase_bufs = {}
        self.phase_offs = {}
        self.arena_front = {}

    def phase(self, name):
        for bl in self.phase_bufs.values():
            for b in bl:
                for ev in ([b.w] if b.w else []) + list(b.r.values()):
                    key = id(ev[0])
                    if self.arena_front.get(key, (None, 0))[1] < ev[1]:
                        self.arena_front[key] = ev
        self.marks.append((name, dict(self.k.cnt)))
        self.phase_name = name
        self.phase_off = self.phase_offs.get(name, 0)
        for b in self.phase_bufs.get(name, []):
            for key, ev in self.arena_front.items():
                if b.r.get(key, (None, 0))[1] < ev[1]:
                    b.r[key] = ev

    def carve(self, name, shape, dtype):
        n = 1
        for d_ in shape[1:]:
            n *= d_
        nb = n * (4 if dtype == F32 else 2)
        ne = (nb + 3) // 4
        off = self.phase_off
        self.phase_off += (ne + 7) // 8 * 8
        self.phase_offs[self.phase_name] = self.phase_off
        assert self.phase_off <= self.ARENA, (self.phase_name, name, self.phase_off)
        ap = self.s["arena"].ap[:, off:off + ne]
        if dtype != F32:
            ap = ap.bitcast(dtype)[:, 0:n]
        if len(shape) > 2:
            names = " ".join(f"d{i}" for i in range(1, len(shape)))
            kw = {f"d{i}": shape[i] for i in range(1, len(shape))}
            ap = ap.rearrange(f"p ({names}) -> p {names}", **kw)
        b = Buf(f"ar_{self.phase_name}_{name}")
        for key, ev in self.arena_front.items():
            b.r[key] = ev
        self.phase_bufs.setdefault(self.phase_name, []).append(b)
        return V(ap, [b])

    def bank(self, pin=False):
        if pin:
            i = min(j for j in range(8) if j not in self.pinned)
            self.pinned.add(i)
        else:
            while self.bank_i % 8 in self.pinned:
                self.bank_i += 1
            i = self.bank_i % 8
            self.bank_i += 1
        p = self.s["psum"]
        v = V(p.ap[:, i, :], [p.sub(i).bufs[0]])
        v.bi = i
        return v

    def unpin(self, v):
        self.pinned.discard(v.bi)

    def bank2(self):
        i = (self.bank_i + 1) // 2 * 2 % 8
        n = 0
        while i in self.pinned or (i + 1) in self.pinned:
            i = (i + 2) % 8
            n += 1
            assert n < 8, "no free psum bank pair"
        self.bank_i = i + 2
        p = self.s["psum"]
        return V(p.ap[:, i:i + 2, :], [p.sub(i).bufs[0], p.sub(i + 1).bufs[0]])

    @staticmethod
    def run_gens(gens, weights=None):
        gens = list(gens)
        weights = list(weights) if weights else [1] * len(gens)
        live = list(range(len(gens)))
        while live:
            for gi in list(live):
                for _ in range(weights[gi]):
                    try:
                        next(gens[gi])
                    except StopIteration:
                        live.remove(gi)
                        break

    def wslot(self):
        v = self.s["wslot"][self.slot_i % 3]
        self.slot_i += 1
        return v

    def wload(self, src, kc, ncols):
        sl = self.wslot()
        dst = sl[:, 0:kc, 0:ncols]
        nm, l, rs, cs = src
        w = self.wscr[(nm, l)].sub((rs.start, rs.stop, cs.start, cs.stop))
        self.k.dma("sp", dst, V(w.ap[rs, cs].rearrange("(kc p) n -> p kc n", p=128), w.bufs))
        return dst

    def weight_blocks(self):
        blks = []
        R = slice(0, D)
        for name, off, ncols in TM_SLOTS:
            blks.append(("win", R, slice(off, off + ncols)))
        for si in (3, 4, 0, 1, 2):
            blks.append(("win", R, slice(si * 512, (si + 1) * 512)))
        for q in range(2):
            for br, (nk, wname) in enumerate(((4, "wpa"), (2, "wpb"), (4, "wpc"))):
                blks.append(("win", R, slice(O_GATE + br * 1024 + q * 512, O_GATE + br * 1024 + (q + 1) * 512)))
                blks.append((wname, slice(0, nk * 128), slice(q * 512, (q + 1) * 512)))
        for half in range(2):
            blks.append(("wo", R, slice(half * 512, (half + 1) * 512)))
        for grp in range(6):
            nb = 4 if grp < 5 else 2
            blks.append(("fup", R, slice(grp * 512, grp * 512 + nb * 128)))
            blks.append(("fup", R, slice(D_FF + grp * 512, D_FF + grp * 512 + nb * 128)))
        for half in range(2):
            for (k0, nk) in ((0, 8), (8, 8), (16, 6)):
                blks.append(("fdown", slice(k0 * 128, (k0 + nk) * 128), slice(half * 512, (half + 1) * 512)))
        return blks

    def cast_weights(self, l):
        d = self.d
        for nm, rs, cs in self.weight_blocks():
            w = self.wscr[(nm, l)].sub((rs.start, rs.stop, cs.start, cs.stop))
            self.k.dma("pool", V(w.ap[rs, cs], w.bufs), d[nm][l][rs, cs])

    def tmp(self, name, shape, dtype, n=2):
        key = (self.phase_name, name, tuple(shape), dtype)
        if key not in self.tmp_i:
            if self.phase_name is None:
                bufs = [self.k.sb(f"{name}_{j}", shape, dtype) for j in range(n)]
            else:
                bufs = [self.carve(f"{name}_{j}", shape, dtype) for j in range(n)]
            self.tmp_i[key] = [0, bufs]
        elif self.phase_name is not None:
            pass
        ent = self.tmp_i[key]
        v = ent[1][ent[0] % n]
        ent[0] += 1
        return v

    def const(self, i):
        return self.s["consts"][:, i, :]

    def layernorm(self, x, tp):
        k = self.k
        g_b, b_b = self.s["lnbuf"][:, 0:1024], self.s["lnbuf"][:, 1024:2048]
        st = self.tmp("lnst", [128, 2, 6], F32)
        mv = self.tmp("lnmv", [128, 4], F32)
        for h in range(2):
            k.bn_stats(st[0:tp, h, :], x[0:tp, h * 512:(h + 1) * 512])
        k.bn_aggr(mv[0:tp, 0:2], st[0:tp].re("p a b -> p (a b)"))
        k.ts("dve", mv[0:tp, 2:3], mv[0:tp, 1:2], LN_EPS, None, op0=ALU.add)
        k.act(mv[0:tp, 2:3], mv[0:tp, 2:3], AF.Sqrt)
        k.recip(mv[0:tp, 3:4], mv[0:tp, 2:3])
        k.ts("dve", x[0:tp], x[0:tp], mv[0:tp, 0:1], mv[0:tp, 3:4], op0=ALU.subtract, op1=ALU.mult)
        k.tt("pool", x[0:tp], x[0:tp], g_b[0:tp], ALU.mult)
        k.tt("pool", x[0:tp], x[0:tp], b_b[0:tp], ALU.add)

    def to_xT(self, x, tp, i):
        k = self.k
        for h in range(2):
            bk = self.bank()
            for j in range(4):
                kc = h * 4 + j
                k.tr(bk[:, j * 128:j * 128 + tp], x[0:tp, kc * 128:(kc + 1) * 128], self.const(0)[0:tp, 0:tp])
            k.copy("act" if h else "dve", self.s["xT"][:, h * 4:h * 4 + 4, i * 128:i * 128 + tp],
                   bk.re("p (a b) -> p a b", a=4)[:, :, 0:tp])

    def layer_setup(self, l):
        k, s, d = self.k, self.s, self.d
        k.dma("sp", s["rsm"], d["rvec"][l][4096:4232].partition_broadcast(128))
        k.dma("sp", s["pvec"], d["pvec"][l])
        k.dma("pool", s["wbd"], d["wbd"][l])
        rsm, lay = s["rsm"], s["lay"]
        k.act(lay[:, 0:4], rsm[:, 128:132], AF.Exp)
        k.ts("dve", lay[:, 0:4], lay[:, 0:4], -1.0, None, op0=ALU.mult)
        k.act(lay[:, 4:8], s["pvec"][:, PV_LAM:PV_LAM + 4], AF.Exp, scale=-1.0)
        k.act(lay[:, 4:8], lay[:, 4:8], AF.Ln, bias=1.0)
        k.ts("dve", lay[:, 4:8], lay[:, 4:8], -8.0, None, op0=ALU.mult)
        k.memset("pool", s["S"], 0.0)
        k.memset("pool", s["Sb"], 0.0)
        k.memset("pool", s["hlast"], 0.0)
        k.memset("pool", s["fh"], 0.0)
        k.memset("pool", s["ahalo"], 0.0)
        k.memset("pool", s["chalo"], 0.0)

    def load_ln(self, src):
        self.k.dma("sp", self.s["lnbuf"], src.partition_broadcast(128))

    def load_chunk(self, l, c):
        k, s, d, CW, NT = self.k, self.s, self.d, self.CW, self.NT
        if l == 0:
            self.load_ln(d["rvec0"])
        for i in range(NT):
            r0 = c * CW + i * 128
            x = s["xres"][i]
            if l == 0:
                k.dma("sp", x, d["xp"][r0:r0 + 128, :])
                self.layernorm(x, 128)
            else:
                k.dma("sp", x, self.xs_scr[r0:r0 + 128, :])
            k.dma("sp", s["cs"][i], d["csp"][r0:r0 + 128, :])
            self.to_xT(x, 128, i)

    def proj_fm(self, l, n, slots, dst_of, first, srcv=None):
        k, s, d = self.k, self.s, self.d
        for si in slots:
            w = self.wload(("win", l, slice(0, D), slice(si * 512, (si + 1) * 512)), 8, 512)
            for j in range(4):
                blk = si * 4 + j
                bk = self.bank()
                for kc in range(8):
                    k.mm(bk[:, 0:n], w[:, kc, j * 128:(j + 1) * 128], s["xT"][:, kc, 0:n], start=(kc == 0), stop=(kc == 7))
                k.copy("act" if blk % 2 else "dve", dst_of(blk), bk[:, 0:n] if srcv is None else srcv(bk[:, 0:n]))
                yield

    def rope(self, x, tp, nh, cs):
        k = self.k
        x3 = x.re("p (h e) -> p h e", e=64)
        x1, x2 = x3[0:tp, :, 0:8], x3[0:tp, :, 8:16]
        cosb = cs[0:tp, None, 0:8].bcast([tp, nh, 8])
        sinb = cs[0:tp, None, 8:16].bcast([tp, nh, 8])
        t = self.tmp("ropet", [128, 4, 12, 8], F32)
        t1, t2, t3, t4 = (t[0:tp, j, 0:nh, :] for j in range(4))
        k.tt("pool", t1, x1, cosb, ALU.mult)
        k.tt("pool", t2, x2, sinb, ALU.mult)
        k.tt("pool", t3, x2, cosb, ALU.mult)
        k.tt("pool", t4, x1, sinb, ALU.mult)
        k.tt("pool", x1, t1, t2, ALU.subtract)
        k.tt("pool", x2, t3, t4, ALU.add)

    def proj_tm(self, l, c):
        k, s, d, CW, NT, T = self.k, self.s, self.d, self.CW, self.NT, self.T
        pend = []
        for name, off, ncols in TM_SLOTS:
            w = self.wload(("win", l, slice(0, D), slice(off, off + ncols)), 8, ncols)
            for i in range(NT):
                a = c * NT + i
                bk = self.bank()
                for kc in range(8):
                    k.mm(bk[:, 0:ncols], s["xT"][:, kc, i * 128:(i + 1) * 128], w[:, kc, :], start=(kc == 0), stop=(kc == 7))
                while pend:
                    pend.pop(0)()
                if name == "z":
                    k.act(s["zs"][i], bk, AF.Silu)
                    yield
                    continue
                t = self.tmp("tmq", [128, 512], F32, n=3)
                k.copy("act", t[:, 0:ncols], bk[:, 0:ncols])
                if name == "q01":
                    self.rope(t, 128, 8, s["cs"][i])
                    pend.append(lambda t=t, i=i: self.tm_to_T(t, 4, s["qT"][:, 0:4, i * 128:(i + 1) * 128]))
                elif name == "q2ba":
                    k.copy("act", s["ba"][i], t[:, 256:264])
                    self.rope(t[:, 0:256], 128, 4, s["cs"][i])
                    pend.append(lambda t=t, i=i: self.tm_to_T(t, 2, s["qT"][:, 4:6, i * 128:(i + 1) * 128]))
                else:
                    g = int(name[2])
                    self.rope(t[:, 0:256], 128, 4, s["cs"][i])
                    W = self.Weff[g]
                    r = a * 128 - (T - W)
                    if r >= 0:
                        k.dma("sp", d[f"b{g}_p"][l][r:r + 128, :], t, is_output=True)
                    slot = a % self.ring[g]
                    pend.append(lambda t=t, g=g, slot=slot: self.tm_to_T(t, 2, s["kT"][g][:, :, slot * 128:(slot + 1) * 128]))
                    k.copy("act", s["vr"][g][:, slot, :], t[:, 256:512])
                yield
        while pend:
            pend.pop(0)()

    def tm_to_T(self, t, nblk, dst):
        k = self.k
        bk = self.bank()
        for j in range(nblk):
            k.tr(bk[:, j * 128:(j + 1) * 128], t[:, j * 128:(j + 1) * 128], self.const(0))
        k.copy("act", dst, bk.re("p (a b) -> p a b", a=4)[:, 0:nblk, :])

    def deltanet_tile(self, pre_cols, tp, sl, consts, ba, zs, o_dst, state, nlev, post=None):
        k, s = self.k, self.s
        pv, rsm, lay = s["pvec"], s["rsm"], s["lay"]
        ident, U, Lst, ones_f, same = consts[:, 0, :], consts[:, 1, :], consts[:, 3, :], consts[:, 4, :], consts[:, 5, :]
        qkvf = self.tmp("qkvf", [128, 12, 128], F32, n=1)
        for blk in range(12 if post is None else 0):
            acc = self.tmp("cacc", [128, 128], F32, n=3)
            accv = acc[:, 0:tp] if sl == tp else acc[:, 0:tp].re("p (s t) -> p s t", t=sl)
            k.ts("dve", accv, pre_cols(blk, 0), pv[:, PV_ACW + blk * 4:PV_ACW + blk * 4 + 1], None, op0=ALU.mult)
            for j in range(1, 4):
                k.stt(accv, pre_cols(blk, j), pv[:, PV_ACW + blk * 4 + j:PV_ACW + blk * 4 + j + 1], accv, ALU.mult, ALU.add)
            k.act(qkvf[:, blk, 0:tp], acc[:, 0:tp], AF.Silu)
            if blk % 4 == 3:
                yield
        yield
        qkvt = self.tmp("qkvt", [128, 12, 128], F32, n=1)
        for g3 in range(3):
            bk = self.bank()
            for j in range(4):
                k.tr(bk[0:tp, j * 128:(j + 1) * 128], qkvf[:, g3 * 4 + j, 0:tp] if post is None else post(g3 * 4 + j), ident)
            k.copy("act", qkvt[0:tp, g3 * 4:g3 * 4 + 4, :], bk.re("p (a b) -> p a b", a=4)[0:tp])
        yield
        sm = self.tmp("dsm", [128, 64], F32, n=2)
        sq = self.tmp("dsq", [128, 8, 128], F32, n=1)
        k.tt("dve", sq[0:tp], qkvt[0:tp, 0:8, :], qkvt[0:tp, 0:8, :], ALU.mult)
        k.reduce("dve", sm[0:tp, 0:8], sq[0:tp], ALU.add)
        k.ts("dve", sm[0:tp, 0:8], sm[0:tp, 0:8], NORM_EPS, None, op0=ALU.add)
        k.act(sm[0:tp, 0:8], sm[0:tp, 0:8], AF.Sqrt)
        k.recip(sm[0:tp, 8:16], sm[0:tp, 0:8])
        k.ts("dve", sm[0:tp, 8:12], sm[0:tp, 8:12], 128.0 ** -0.5, None, op0=ALU.mult)
        yield
        k.act(sm[0:tp, 16:20], ba[0:tp, 0:4], AF.Sigmoid)
        k.tt("pool", sm[0:tp, 20:24], ba[0:tp, 4:8], rsm[0:tp, 132:136], ALU.add)
        k.act(sm[0:tp, 20:24], sm[0:tp, 20:24], AF.Exp)
        k.act(sm[0:tp, 20:24], sm[0:tp, 20:24], AF.Ln, bias=1.0)
        k.tt("dve", sm[0:tp, 20:24], sm[0:tp, 20:24], lay[0:tp, 0:4], ALU.mult)
        g = sm[0:tp, 20:24]
        yield
        bk = self.bank()
        k.mm(bk[0:tp, 0:4], U[0:tp, 0:tp], g)
        k.mm(bk[0:tp, 4:8], same[0:tp, 0:tp], g)
        k.copy("dve", sm[0:tp, 24:32], bk[0:tp, 0:8])
        G, GL = sm[0:tp, 24:28], sm[0:tp, 28:32]
        ug = qkvf[:, 0:4, :]
        k.tt("dve", ug[0:tp, :, 0:tp], U[0:tp, None, 0:tp].bcast([tp, 4, tp]), g[:, :, None].bcast([tp, 4, tp]), ALU.mult)
        gbk = self.bank(pin=True)
        gb = gbk.re("p (h i) -> p h i", h=4)
        for h in range(4):
            k.mm(gb[:, h, 0:tp], ones_f[0:tp, :], ug[0:tp, h, 0:tp])
        yield
        X = qkvf[:, 4:8, :]
        k.tt("dve", X[0:tp, :, 0:tp], gb[0:tp, :, 0:tp], G[:, :, None].bcast([tp, 4, tp]), ALU.subtract)
        DT = self.tmp("dDT", [128, 4, 128], F32, n=1)
        D2 = self.tmp("dD2", [128, 4, 128], F32, n=1)
        k.ts("dve", DT[0:tp, :, 0:tp], X[0:tp, :, 0:tp], 0.0, None, op0=ALU.min)
        k.act(DT[0:tp, :, 0:tp], DT[0:tp, :, 0:tp], AF.Exp)
        k.ts("dve", D2[0:tp, :, 0:tp], X[0:tp, :, 0:tp], -1.0, 0.0, op0=ALU.mult, op1=ALU.min)
        k.act(D2[0:tp, :, 0:tp], D2[0:tp, :, 0:tp], AF.Exp)
        k.tt("pool", DT[0:tp, :, 0:tp], DT[0:tp, :, 0:tp], U[0:tp, None, 0:tp].bcast([tp, 4, tp]), ALU.mult)
        k.tt("pool", D2[0:tp, :, 0:tp], D2[0:tp, :, 0:tp], Lst[0:tp, None, 0:tp].bcast([tp, 4, tp]), ALU.mult)
        k.act(sm[0:tp, 32:36], G, AF.Exp)
        k.tt("pool", sm[0:tp, 36:40], GL, G, ALU.subtract)
        k.act(sm[0:tp, 36:40], sm[0:tp, 36:40], AF.Exp)
        eGb = qkvf[:, 8:12, :]
        k.act(eGb[:, :, 0:tp], gb[:, :, 0:tp], AF.Exp)
        if self.debug and self.cur_l == 0 and tp == 128:
            gbd = sq[:, 0:4, :].re("p a b -> p (a b)")
            k.copy("dve", gbd, gbk)
            k.dma("sp", self.d["dbg_gb"][self.dbg_tile], gbd, is_output=True)
        ncol = tp // sl
        eGL = self.tmp("deGL", [128, 4, 16], F32, n=2)
        k.act(eGL[:, :, 0:ncol], gb[:, :, sl - 1:tp:sl], AF.Exp)
        self.unpin(gbk)
        yield
        k.tt("pool", sm[0:tp, 40:44], sm[0:tp, 12:16], sm[0:tp, 16:20], ALU.mult)
        k.tt("pool", sm[0:tp, 40:44], sm[0:tp, 40:44], sm[0:tp, 32:36], ALU.mult)
        k.tt("pool", sm[0:tp, 44:48], sm[0:tp, 12:16], sm[0:tp, 36:40], ALU.mult)
        k.ts("dve", sm[0:tp, 48:52], sm[0:tp, 16:20], -1.0, None, op0=ALU.mult)

        def bc4(col):
            return sm[0:tp, col:col + 4, None].bcast([tp, 4, 128])

        knq = self.tmp("dknq", [128, 8, 128], BF16, n=1)
        kw = self.tmp("dkw", [128, 4, 128], BF16, n=1)
        kdec = self.tmp("dkdec", [128, 4, 128], BF16, n=1)
        vb = self.tmp("dvb", [128, 4, 128], BF16, n=1)
        k.tt("dve", knq[0:tp, 0:4, :], qkvt[0:tp, 4:8, :], bc4(12), ALU.mult)
        k.tt("dve", knq[0:tp, 4:8, :], qkvt[0:tp, 0:4, :], bc4(8), ALU.mult)
        k.tt("dve", kw[0:tp], qkvt[0:tp, 4:8, :], bc4(40), ALU.mult)
        k.tt("dve", kdec[0:tp], qkvt[0:tp, 4:8, :], bc4(44), ALU.mult)
        k.tt("dve", vb[0:tp], qkvt[0:tp, 8:12, :], bc4(16), ALU.mult)
        bkb = self.bank().bc(BF16)
        for j in range(8):
            k.tr(bkb[:, j * 128:j * 128 + tp], knq[0:tp, j, :], s["ident_b"][0:tp, 0:tp])
        knqT = self.tmp("dknqT", [128, 8, 128], BF16, n=1)
        k.copy("act", knqT[:, :, 0:tp], bkb.re("p (a b) -> p a b", a=8)[:, :, 0:tp])
        qdT = self.tmp("dqdT", [128, 4, 128], BF16, n=1)
        self.dn_scratch = knqT
        self.dn_sq = sq
        k.tt("dve", qdT[:, :, 0:tp], knqT[:, 4:8, 0:tp], eGb[:, :, 0:tp], ALU.mult)
        yield
        kk = self.bank().re("p (h i) -> p h i", h=4)
        qk = self.bank().re("p (h i) -> p h i", h=4)
        for h in range(4):
            k.mm(kk[0:tp, h, 0:tp], knqT[:, h, 0:tp], knqT[:, h, 0:tp])
        for h in range(4):
            k.mm(qk[0:tp, h, 0:tp], knqT[:, h, 0:tp], knqT[:, 4 + h, 0:tp])
        k.tt("pool", D2[0:tp, :, 0:tp], D2[0:tp, :, 0:tp], sm[0:tp, 48:52, None].bcast([tp, 4, tp]), ALU.mult)
        A = [qkvt[:, 0:4, :], qkvt[:, 4:8, :]]
        qb16 = qkvt[:, 8:12, :].bc(BF16).re("p a (b c) -> p (a b) c", c=128)
        BP = [self.tmp("dBP", [128, 4, 2, 128], F32, n=2) for _ in range(2)]
        rr = (lambda v_: v_.bc(dt.float32r)) if (tp == 128 and self.use_f32r) else (lambda v_: v_)
        k.tt("dve", rr(A[0][0:tp, :, 0:tp]), kk[0:tp, :, 0:tp], D2[0:tp, :, 0:tp], ALU.mult)
        QKm = qb16[:, 0:4, :]
        k.tt("dve", QKm[0:tp, :, 0:tp], qk[0:tp, :, 0:tp], DT[0:tp, :, 0:tp], ALU.mult)
        bkt = self.bank().re("p (a b) -> p a b", a=4)
        for h in range(4):
            k.tr(bkt[0:tp, h, 0:tp], A[0][0:tp, h, 0:tp], ident[0:tp, 0:tp])
        k.copy("act", rr(BP[0][0:tp, :, 0, 0:tp]), bkt[0:tp, :, 0:tp])
        k.copy("pool", rr(BP[0][0:tp, :, 1, 0:tp]), ident[0:tp, None, 0:tp].bcast([tp, 4, tp]))
        yield
        for n in range(nlev):
            a_n, bp_n = A[n % 2], BP[n % 2]
            a_x, bp_x = A[(n + 1) % 2], BP[(n + 1) % 2]
            last = n == nlev - 1
            p1 = self.bank2().re("p a (h t i) -> p (a h) t i", h=2, t=2)
            for h in range(4):
                if tp == 128 and self.use_f32r:
                    k.mm(p1[0:tp, h, :, 0:tp], a_n[0:tp, h, 0:tp].bc(dt.float32r), bp_n[0:tp, h, :, 0:tp].bc(dt.float32r))
                else:
                    k.mm(p1[0:tp, h, :, 0:tp], a_n[0:tp, h, 0:tp], bp_n[0:tp, h, :, 0:tp])
            if not last:
                p2 = self.bank().re("p (h i) -> p h i", h=4)
                for h in range(4):
                    k.mm(p2[0:tp, h, 0:tp], bp_n[0:tp, h, 0, 0:tp], a_n[0:tp, h, 0:tp])
                k.copy("act", rr(bp_x[0:tp, :, 0, 0:tp]), p1[0:tp, :, 0, 0:tp])
                k.copy("act", rr(a_x[0:tp, :, 0:tp]), p2[0:tp, :, 0:tp])
            k.tt("dve", rr(bp_x[0:tp, :, 1, 0:tp]), p1[0:tp, :, 1, 0:tp], bp_n[0:tp, :, 1, 0:tp], ALU.add)
            yield
        P = BP[(nlev + 1) % 2][:, :, 0, :].bc(BF16)[:, :, 0:128]
        k.copy("act", P[0:tp, :, 0:tp], BP[nlev % 2][0:tp, :, 1, 0:tp])
        yield
        wb = self.bank().re("p (h i) -> p h i", h=4)
        for h in range(4):
            k.mm(wb[:, h, 0:tp], kw[0:tp, h, :], P[0:tp, h, 0:tp])
        nwT = qb16[:, 4:8, :]
        k.ts("dve", nwT[:, :, 0:tp], wb[:, :, 0:tp], -1.0, None, op0=ALU.mult)
        yield
        ob = state(self, tp, sl, P, vb, nwT, qdT, QKm, kdec, eGL, consts)
        if self.debug and self.cur_l == 0 and tp == 128:
            k.dma("sp", self.d["dbg_sm"][self.dbg_tile], sm, is_output=True)
            self.dbg_tile += 1
        if self.debug and self.cur_l == 0 and tp == 64:
            k.dma("sp", self.d["dbg_sms"], sm, is_output=True)
        yield
        oe = sq[:, 4:8, :]
        k.copy("act", oe[0:tp], ob[0:tp])
        self.unpin(self.ob_bank)
        k.tt("dve", sq[0:tp, 0:4, :], oe[0:tp], oe[0:tp], ALU.mult)
        k.reduce("dve", sm[0:tp, 52:56], sq[0:tp, 0:4, :], ALU.add)
        k.ts("dve", sm[0:tp, 52:56], sm[0:tp, 52:56], 1.0 / 128.0, NORM_EPS, op0=ALU.mult, op1=ALU.add)
        k.act(sm[0:tp, 52:56], sm[0:tp, 52:56], AF.Sqrt)
        k.recip(sm[0:tp, 56:60], sm[0:tp, 52:56])
        k.tt("dve", oe[0:tp], oe[0:tp], bc4(56), ALU.mult)
        k.tt("pool", oe[0:tp], oe[0:tp], rsm[0:tp, None, 0:128].bcast([tp, 4, 128]), ALU.mult)
        oab = knq[:, 0:4, :]
        k.tt("dve", oab[0:tp], oe[0:tp], zs[0:tp].re("p (h e) -> p h e", h=4), ALU.mult)
        bkb = self.bank().bc(BF16)
        for h in range(4):
            k.tr(bkb[:, h * 128:h * 128 + tp], oab[0:tp, h, :], s["ident_b"][0:tp, 0:tp])
        k.copy("act", o_dst, bkb.re("p (a b) -> p a b", a=8)[:, 0:4, 0:tp])
        yield

    def deltanet_heads(self, tag, h0, nh, ba, zs, o_dst, post, nlev=7):
        k, s = self.k, self.s
        rsm, lay, consts = s["rsm"], s["lay"], self.cp
        ident, U, Lst, ones_f, same = consts[:, 0, :], consts[:, 1, :], consts[:, 3, :], consts[:, 4, :], consts[:, 5, :]
        tp = 128
        T_ = lambda nm, shape, dtp, n=1: self.tmp(nm + tag, shape, dtp, n=n)
        hs_ = slice(h0, h0 + nh)
        scr = T_("hscr", [128, 3 * nh, 128], F32)
        qkvt = T_("hqkvt", [128, 3, nh, 128], F32)
        for part in range(3):
            bk = self.bank()
            for j in range(nh):
                k.tr(bk[:, j * 128:(j + 1) * 128], post(part * 4 + h0 + j), ident)
            k.copy("act", qkvt[:, part, :, :], bk.re("p (a b) -> p a b", a=4)[:, 0:nh, :])
        yield
        sm = T_("hsm", [128, 64], F32)
        sq = T_("hsq", [128, 2, nh, 128], F32)
        k.tt("dve", sq, qkvt[:, 0:2], qkvt[:, 0:2], ALU.mult)
        k.reduce("dve", sm[:, 0:2 * nh], sq.re("p a h e -> p (a h) e"), ALU.add)
        k.ts("dve", sm[:, 0:2 * nh], sm[:, 0:2 * nh], NORM_EPS, None, op0=ALU.add)
        k.act(sm[:, 0:2 * nh], sm[:, 0:2 * nh], AF.Sqrt)
        k.recip(sm[:, 8:8 + 2 * nh], sm[:, 0:2 * nh])
        k.ts("dve", sm[:, 8:8 + nh], sm[:, 8:8 + nh], 128.0 ** -0.5, None, op0=ALU.mult)
        RNQ, RNK = 8, 8 + nh
        yield
        BETA, GG = 16, 20
        k.act(sm[:, BETA:BETA + nh], ba[:, h0:h0 + nh], AF.Sigmoid)
        k.tt("dve", sm[:, GG:GG + nh], ba[:, 4 + h0:4 + h0 + nh], rsm[:, 132 + h0:132 + h0 + nh], ALU.add)
        k.act(sm[:, GG:GG + nh], sm[:, GG:GG + nh], AF.Exp)
        k.act(sm[:, GG:GG + nh], sm[:, GG:GG + nh], AF.Ln, bias=1.0)
        k.tt("dve", sm[:, GG:GG + nh], sm[:, GG:GG + nh], lay[:, h0:h0 + nh], ALU.mult)
        g = sm[:, GG:GG + nh]
        yield
        bk = self.bank()
        k.mm(bk[:, 0:nh], U, g)
        k.mm(bk[:, 4:4 + nh], same, g)
        k.copy("dve", sm[:, 24:32], bk[:, 0:8])
        G, GL = sm[:, 24:24 + nh], sm[:, 28:28 + nh]
        ug, X, eGb = scr[:, 0:nh, :], scr[:, nh:2 * nh, :], scr[:, 2 * nh:3 * nh, :]
        k.tt("dve", ug, U[:, None, :].bcast([128, nh, 128]), g[:, :, None].bcast([128, nh, 128]), ALU.mult)
        gbk = self.bank(pin=True)
        gb = gbk.re("p (h i) -> p h i", h=4)[:, 0:nh, :]
        for h in range(nh):
            k.mm(gb[:, h, :], ones_f, ug[:, h, :])
        yield
        k.tt("dve", X, gb, G[:, :, None].bcast([128, nh, 128]), ALU.subtract)
        DT = T_("hDT", [128, nh, 128], F32)
        D2 = T_("hD2", [128, nh, 128], F32)
        k.ts("dve", DT, X, 0.0, None, op0=ALU.min)
        k.ts("dve", D2, X, -1.0, 0.0, op0=ALU.mult, op1=ALU.min)
        k.tt("dve", sm[:, 36:36 + nh], GL, G, ALU.subtract)
        k.act(DT, DT, AF.Exp)
        k.act(D2, D2, AF.Exp)
        k.act(sm[:, 32:32 + nh], G, AF.Exp)
        k.act(sm[:, 36:36 + nh], sm[:, 36:36 + nh], AF.Exp)
        k.act(eGb, gb, AF.Exp)
        eGL = T_("heGL", [128, 4], F32, n=2)
        k.act(eGL[:, 0:nh], gb[:, :, 127], AF.Exp)
        self.unpin(gbk)
        k.tt("dve", DT, DT, U[:, None, :].bcast([128, nh, 128]), ALU.mult)
        k.tt("dve", D2, D2, Lst[:, None, :].bcast([128, nh, 128]), ALU.mult)
        yield
        k.tt("dve", sm[:, 40:40 + nh], sm[:, RNK:RNK + nh], sm[:, BETA:BETA + nh], ALU.mult)
        k.tt("dve", sm[:, 40:40 + nh], sm[:, 40:40 + nh], sm[:, 32:32 + nh], ALU.mult)
        k.tt("dve", sm[:, 44:44 + nh], sm[:, RNK:RNK + nh], sm[:, 36:36 + nh], ALU.mult)
        k.ts("dve", sm[:, 48:48 + nh], sm[:, BETA:BETA + nh], -1.0, None, op0=ALU.mult)
        bcn = lambda col: sm[:, col:col + nh, None].bcast([128, nh, 128])
        knq = T_("hknq", [128, 2, nh, 128], BF16)
        kw = T_("hkw", [128, nh, 128], BF16)
        kdec = T_("hkdec", [128, nh, 128], BF16)
        vb = T_("hvb", [128, nh, 128], BF16)
        k.tt("dve", knq[:, 0], qkvt[:, 1], bcn(RNK), ALU.mult)
        k.tt("dve", knq[:, 1], qkvt[:, 0], bcn(RNQ), ALU.mult)
        k.tt("dve", kw, qkvt[:, 1], bcn(40), ALU.mult)
        k.tt("dve", kdec, qkvt[:, 1], bcn(44), ALU.mult)
        k.tt("dve", vb, qkvt[:, 2], bcn(BETA), ALU.mult)
        bkb = self.bank().bc(BF16).re("p (a b) -> p a b", a=8)
        for j in range(2 * nh):
            k.tr(bkb[:, j, :], knq[:, j // nh, j % nh, :], s["ident_b"])
        knqT = T_("hknqT", [128, 2, nh, 128], BF16)
        k.copy("act", knqT.re("p a h e -> p (a h) e"), bkb[:, 0:2 * nh, :])
        qdT = T_("hqdT", [128, nh, 128], BF16)
        k.tt("dve", qdT, knqT[:, 1], eGb, ALU.mult)
        yield
        kk = self.bank().re("p (h i) -> p h i", h=4)[:, 0:nh, :]
        qk = self.bank().re("p (h i) -> p h i", h=4)[:, 0:nh, :]
        for h in range(nh):
            k.mm(kk[:, h, :], knqT[:, 0, h, :], knqT[:, 0, h, :])
        for h in range(nh):
            k.mm(qk[:, h, :], knqT[:, 0, h, :], knqT[:, 1, h, :])
        k.tt("dve", D2, D2, sm[:, 48:48 + nh, None].bcast([128, nh, 128]), ALU.mult)
        A = [qkvt[:, 0], qkvt[:, 1]]
        qb16 = qkvt[:, 2].bc(BF16).re("p h (b c) -> p (h b) c", c=128)
        BP = [T_("hBP0", [128, nh, 2, 128], F32), T_("hBP1", [128, nh, 2, 128], F32)]
        k.tt("dve", A[0], kk, D2, ALU.mult)
        QKm = qb16[:, 0:nh, :]
        k.tt("dve", QKm, qk, DT, ALU.mult)
        bkt = self.bank().re("p (a b) -> p a b", a=4)[:, 0:nh, :]
        for h in range(nh):
            k.tr(bkt[:, h, :], A[0][:, h, :], ident)
        k.copy("act", BP[0][:, :, 0, :], bkt)
        k.copy("act", BP[0][:, :, 1, :], ident[:, None, :].bcast([128, nh, 128]))
        yield
        for n in range(nlev):
            a_n, bp_n = A[n % 2], BP[n % 2]
            a_x, bp_x = A[(n + 1) % 2], BP[(n + 1) % 2]
            last = n == nlev - 1
            if nh <= 2:
                p1 = self.bank().re("p (h t i) -> p h t i", h=2, t=2)[:, 0:nh]
            else:
                p1 = self.bank2().re("p a (h t i) -> p (a h) t i", h=2, t=2)
            for h in range(nh):
                k.mm(p1[:, h, :, :], a_n[:, h, :], bp_n[:, h, :, :])
            if not last:
                p2 = self.bank().re("p (h i) -> p h i", h=4)[:, 0:nh, :]
                for h in range(nh):
                    k.mm(p2[:, h, :], bp_n[:, h, 0, :], a_n[:, h, :])
                k.copy("act", bp_x[:, :, 0, :], p1[:, :, 0, :])
                k.copy("act", a_x, p2)
            k.tt("dve", bp_x[:, :, 1, :], p1[:, :, 1, :], bp_n[:, :, 1, :], ALU.add)
            yield
        P = BP[(nlev + 1) % 2][:, :, 0, :].bc(BF16)[:, :, 0:128]
        k.copy("act", P, BP[nlev % 2][:, :, 1, :])
        wb = self.bank().re("p (h i) -> p h i", h=4)[:, 0:nh, :]
        for h in range(nh):
            k.mm(wb[:, h, :], kw[:, h, :], P[:, h, :])
        nwT = qb16[:, nh:2 * nh, :]
        k.ts("dve", nwT, wb, -1.0, None, op0=ALU.mult)
        yield
        S, Sb = s["S"], s["Sb"]
        vn = self.bank().re("p (h e) -> p h e", h=4)[:, 0:nh, :]
        for h in range(nh):
            k.mm(vn[:, h, :], P[:, h, :], vb[:, h, :], start=True, stop=False)
            k.mm(vn[:, h, :], nwT[:, h, :], Sb[:, h0 + h, :], start=False, stop=True)
        vnb = knqT[:, 0]
        k.copy("act", vnb, vn)
        obk = self.bank(pin=True)
        ob = obk.re("p (h e) -> p h e", h=4)[:, 0:nh, :]
        for h in range(nh):
            k.mm(ob[:, h, :], qdT[:, h, :], Sb[:, h0 + h, :], start=True, stop=False)
            k.mm(ob[:, h, :], QKm[:, h, :], vnb[:, h, :], start=False, stop=True)
        dS = self.bank().re("p (h e) -> p h e", h=4)[:, 0:nh, :]
        for h in range(nh):
            k.mm(dS[:, h, :], kdec[:, h, :], vnb[:, h, :])
        Ssub, Sbsub = s["S"].sub(tag)[:, hs_, :], s["Sb"].sub(tag)[:, hs_, :]
        k.tt("dve", Ssub, Ssub, eGL[:, 0:nh, None].bcast([128, nh, 128]), ALU.mult)
        k.tt("dve", Ssub, Ssub, dS, ALU.add)
        k.copy("act", Sbsub, Ssub)
        yield
        oe = sq[:, 1]
        k.copy("act", oe, ob)
        self.unpin(obk)
        k.tt("dve", sq[:, 0], oe, oe, ALU.mult)
        k.reduce("dve", sm[:, 52:52 + nh], sq[:, 0], ALU.add)
        k.ts("dve", sm[:, 52:52 + nh], sm[:, 52:52 + nh], 1.0 / 128.0, NORM_EPS, op0=ALU.mult, op1=ALU.add)
        k.act(sm[:, 52:52 + nh], sm[:, 52:52 + nh], AF.Sqrt)
        k.recip(sm[:, 56:56 + nh], sm[:, 52:52 + nh])
        k.tt("dve", oe, oe, bcn(56), ALU.mult)
        k.tt("dve", oe, oe, rsm[:, None, 0:128].bcast([128, nh, 128]), ALU.mult)
        oab = knq[:, 0]
        k.tt("dve", oab, oe, zs[:, h0 * 128:(h0 + nh) * 128].re("p (h e) -> p h e", h=nh), ALU.mult)
        bkb = self.bank().bc(BF16).re("p (a b) -> p a b", a=8)
        for h in range(nh):
            k.tr(bkb[:, h, :], oab[:, h, :], s["ident_b"])
        k.copy("act", o_dst, bkb[:, 0:nh, :])
        yield

    @staticmethod
    def state_prompt(self, tp, sl, P, vb, nwT, qdT, QKm, kdec, eGL, consts):
        k, s = self.k, self.s
        S, Sb = s["S"], s["Sb"]
        vn = self.bank().re("p (h e) -> p h e", h=4)
        for h in range(4):
            k.mm(vn[0:tp, h, :], P[0:tp, h, 0:tp], vb[0:tp, h, :], start=True, stop=False)
            k.mm(vn[0:tp, h, :], nwT[:, h, 0:tp], Sb[:, h, :], start=False, stop=True)
        vnb = self.dn_scratch[:, 0:4, :]
        k.copy("act", vnb[0:tp], vn[0:tp])
        obk = self.bank(pin=True)
        self.ob_bank = obk
        ob = obk.re("p (h e) -> p h e", h=4)
        for h in range(4):
            k.mm(ob[0:tp, h, :], qdT[:, h, 0:tp], Sb[:, h, :], start=True, stop=False)
            k.mm(ob[0:tp, h, :], QKm[0:tp, h, 0:tp], vnb[0:tp, h, :], start=False, stop=True)
        dS = self.bank().re("p (h e) -> p h e", h=4)
        for h in range(4):
            k.mm(dS[:, h, :], kdec[0:tp, h, :], vnb[0:tp, h, :])
        k.tt("pool", S, S, eGL[:, :, 0:1].bcast([128, 4, 128]), ALU.mult)
        k.tt("dve", S, S, dS, ALU.add)
        k.copy("act", Sb, S)
        return ob

    def attention_chunk(self, c, LAG=2):
        k, s, CW, NT = self.k, self.s, self.CW, self.NT
        for pair in range(2):
            num, den = self.bank(pin=True), self.bank(pin=True)
            zr = s["mw"][0][:, 0:CW] if CW <= 256 else s["mw"][2][:, 0:CW]
            k.mm(num[:, 0:CW], s["zeros_b"], zr)
            k.mm(den[:, 0:CW], s["zeros_b"], zr)
            units = []
            for g, (W, dil) in enumerate(B_GROUPS):
                for hh in (2 * pair, 2 * pair + 1):
                    for a in range(max(0, c * NT - W // 128), c * NT + NT):
                        rho = a - c * NT
                        q_lo = max(0, 128 * rho)
                        q_hi = min(CW, 128 * rho + W + 128)
                        if q_hi - q_lo > 0:
                            units.append((g, hh, a, rho, q_lo, q_hi))
            pend = []
            cnt = 0
            for u in units + [None] * LAG:
                if u is not None:
                    g, hh, a, rho, q_lo, q_hi = u
                    n = q_hi - q_lo
                    prt = slice(64 * (hh % 2), 64 * (hh % 2) + 64)
                    slot = a % self.ring[g]
                    st = self.bank()
                    k.mm(st[:, 0:n], s["kT"][g][prt, pair, slot * 128:(slot + 1) * 128], s["qT"][prt, 2 * g + pair, q_lo:q_hi])
                    p = self.tmp("attp", [128, 512], BF16, n=LAG + 2)
                    k.act(p[:, 0:n], st[:, 0:n], AF.Exp, scale=0.125)
                    k.tt("pool" if cnt % 2 else "dve", p[:, 0:n], p[:, 0:n], s["mw"][g][:, q_lo - 128 * rho:q_hi - 128 * rho], ALU.mult)
                    cnt += 1
                    pend.append((p, n, prt, q_lo, q_hi, g, slot, hh))
                if len(pend) > LAG or (u is None and pend):
                    p, n, prt, q_lo, q_hi, g, slot, hh = pend.pop(0)
                    k.mm(num[prt, q_lo:q_hi], s["vr"][g][:, slot, hh * 64:(hh + 1) * 64], p[:, 0:n], start=False, stop=False)
                    k.mm(den[prt, q_lo:q_hi], s["ones_b"][:, 0:64], p[:, 0:n], start=False, stop=False)
                yield
            rd = self.tmp("attrd", [128, 512], F32, n=1)
            k.recip(rd[:, 0:CW], den[:, 0:CW])
            k.tt("dve", s["oT"][:, 4 + pair, :], num[:, 0:CW], rd[:, 0:CW], ALU.mult)
            self.unpin(num)
            self.unpin(den)
            yield

    def rglru(self, n, xwin, gview, hinit, o_dst, scan_fix=None, v3=None, hs=None):
        k, s = self.k, self.s
        pv, lay = s["pvec"], s["lay"]
        hs = [] if hs is None else hs
        for blk in range(4):
            def pc(off):
                return pv[:, off:off + 1]
            y = self.tmp("ry", [128, 512], F32, n=2)[:, 0:n]
            yv = y if v3 is None else v3(y)
            k.ts("dve", yv, xwin(blk, 0), pc(PV_CCW + blk * 4), pc(PV_CCB + blk), op0=ALU.mult, op1=ALU.add)
            for j in range(1, 4):
                k.stt(yv, xwin(blk, j), pc(PV_CCW + blk * 4 + j), yv, ALU.mult, ALU.add)
            yb = self.tmp("ryb", [128, 512], BF16, n=2)[:, 0:n]
            k.copy("act", yb, y)
            yield
            rp, ip = self.bank(), self.bank()
            k.mm(rp[:, 0:n], s["wbd"][:, blk, :], yb)
            k.mm(ip[:, 0:n], s["wbd"][:, 4 + blk, :], yb)
            r = self.tmp("rr", [128, 512], F32, n=2)[:, 0:n]
            ig = self.tmp("ri", [128, 512], F32, n=2)[:, 0:n]
            k.act(r, rp[:, 0:n], AF.Sigmoid, bias=pc(PV_CBR + blk))
            k.act(ig, ip[:, 0:n], AF.Sigmoid, bias=pc(PV_CBI + blk))
            yield
            a = self.tmp("ra", [128, 512], F32, n=2)[:, 0:n]
            k.act(a, r, AF.Exp, scale=lay[:, 4 + blk:5 + blk])
            k.tt("pool", r, a, a, ALU.mult)
            k.act(r, r, AF.Sqrt, scale=-1.0, bias=1.0)
            k.tt("pool", ig, ig, y, ALU.mult)
            k.tt("pool", ig, ig, r, ALU.mult)
            if scan_fix is not None:
                scan_fix(blk, a, ig)
            h = self.tmp("rh", [128, 512], F32, n=4)[:, 0:n]
            k.scan(h, a, ig, hinit(blk) if hinit is not None else 0.0, ALU.mult, ALU.add)
            hs.append(h)
            yield
            u = self.tmp("ru", [128, 512], F32, n=2)[:, 0:n]
            gv = self.tmp("rgv", [128, 512], F32, n=2)[:, 0:n]
            k.copy("pool", gv if v3 is None else v3(gv), gview(blk))
            k.tt("pool", u, gv, gv, ALU.mult)
            k.ts("dve", u, u, 0.044715, 1.0, op0=ALU.mult, op1=ALU.add)
            k.tt("pool", u, u, gv, ALU.mult)
            k.act(u, u, AF.Sigmoid, scale=1.5957691216057308)
            k.tt("pool", u, u, gv, ALU.mult)
            k.tt("dve", o_dst(blk), u, h, ALU.mult)
            yield

    def merge(self, l, n, tiles):
        k, s, d = self.k, self.s, self.d
        oT, xT = s["oT"], s["xT"]
        accs = [self.tmp(f"macc{j}", [128, 512], F32, n=1)[:, 0:n] for j in range(4)]
        for q in range(2):
            for br, (o0, nk, wname) in enumerate(((0, 4, "wpa"), (4, 2, "wpb"), (6, 4, "wpc"))):
                gw = self.wload(("win", l, slice(0, D), slice(O_GATE + br * 1024 + q * 512, O_GATE + br * 1024 + (q + 1) * 512)), 8, 512)
                pw = self.wload((wname, l, slice(0, nk * 128), slice(q * 512, (q + 1) * 512)), nk, 512)
                for j in range(4):
                    fb = q * 4 + j
                    bk = self.bank()
                    for kc in range(8):
                        k.mm(bk[:, 0:n], gw[:, kc, j * 128:(j + 1) * 128], xT[:, kc, 0:n], start=(kc == 0), stop=(kc == 7))
                    sg = self.tmp("msig", [128, 512], F32, n=3)[:, 0:n]
                    k.act(sg, bk[:, 0:n], AF.Sigmoid)
                    bk = self.bank()
                    for kc in range(nk):
                        k.mm(bk[:, 0:n], pw[:, kc, j * 128:(j + 1) * 128], oT[:, o0 + kc, 0:n], start=(kc == 0), stop=(kc == nk - 1))
                    if br == 0:
                        k.tt("dve", accs[j], bk[:, 0:n], sg, ALU.mult)
                    else:
                        k.tt("dve", sg, bk[:, 0:n], sg, ALU.mult)
                        if br == 1:
                            k.tt("pool", accs[j], accs[j], sg, ALU.add)
                        else:
                            k.tt("pool", s["mergedT"][:, fb, 0:n], accs[j], sg, ALU.add)
        for half in range(2):
            w = self.wload(("wo", l, slice(0, D), slice(half * 512, (half + 1) * 512)), 8, 512)
            for (x, tp, c0) in tiles:
                bk = self.bank()
                for kc in range(8):
                    k.mm(bk[0:tp, :], s["mergedT"][:, kc, c0:c0 + tp], w[:, kc, :], start=(kc == 0), stop=(kc == 7))
                xs = x[0:tp, half * 512:(half + 1) * 512]
                k.stt(xs, xs, ALPHA, bk[0:tp, :], ALU.mult, ALU.add)
        self.load_ln(d["rvec"][l][0:2048])
        for ti, (x, tp, c0) in enumerate(tiles):
            self.layernorm(x, tp)
            self.to_xT(x, tp, c0 // 128)

    def ffn(self, l, n, tiles, gwin, halo_update):
        k, s, d = self.k, self.s, self.d
        pv, xT, hT = s["pvec"], s["xT"], s["hT"]
        for grp in range(6):
            nb = 4 if grp < 5 else 2
            gwt = self.wload(("fup", l, slice(0, D), slice(grp * 512, grp * 512 + nb * 128)), 8, nb * 128)
            uwt = self.wload(("fup", l, slice(0, D), slice(D_FF + grp * 512, D_FF + grp * 512 + nb * 128)), 8, nb * 128)
            for j in range(nb):
                blk = grp * 4 + j
                gb_, ub = self.bank(), self.bank()
                for kc in range(8):
                    k.mm(gb_[:, 0:n], gwt[:, kc, j * 128:(j + 1) * 128], xT[:, kc, 0:n], start=(kc == 0), stop=(kc == 7))
                for kc in range(8):
                    k.mm(ub[:, 0:n], uwt[:, kc, j * 128:(j + 1) * 128], xT[:, kc, 0:n], start=(kc == 0), stop=(kc == 7))
                gp = self.tmp("fgp", [128, 3 * 512 // 2], F32, n=2)
                win = gwin(blk, gp, gb_)
                acc = self.tmp("facc", [128, 512], F32, n=2)[:, 0:n]
                accv = acc if self.f_v3 is None else self.f_v3(acc)
                if True:
                    k.ts("dve", accv, win(0), pv[:, PV_FCW + blk * 3:PV_FCW + blk * 3 + 1], pv[:, PV_FCB + blk:PV_FCB + blk + 1], op0=ALU.mult, op1=ALU.add)
                    for jj in (1, 2):
                        k.stt(accv, win(jj), pv[:, PV_FCW + blk * 3 + jj:PV_FCW + blk * 3 + jj + 1], accv, ALU.mult, ALU.add)
                else:
                    k.ts("pool", accv, win(0), pv[:, PV_FCW + blk * 3:PV_FCW + blk * 3 + 1], pv[:, PV_FCB + blk:PV_FCB + blk + 1], op0=ALU.mult, op1=ALU.add)
                    tq = self.tmp("fconvt", [128, 512], F32, n=1)[:, 0:n]
                    tqv = tq if self.f_v3 is None else self.f_v3(tq)
                    for jj in (1, 2):
                        k.ts("pool", tqv, win(jj), pv[:, PV_FCW + blk * 3 + jj:PV_FCW + blk * 3 + jj + 1], None, op0=ALU.mult)
                        k.tt("pool", accv, accv, tqv, ALU.add)
                halo_update(blk, gp)
                k.act(acc, acc, AF.Silu)
                k.tt("dve", hT[:, blk, 0:n], ub[:, 0:n], acc, ALU.mult)
        for half in range(2):
            banks = [self.bank() for _ in tiles]
            for kg, (k0, nk) in enumerate(((0, 8), (8, 8), (16, 6))):
                w = self.wload(("fdown", l, slice(k0 * 128, (k0 + nk) * 128), slice(half * 512, (half + 1) * 512)), nk, 512)
                for ti, (x, tp, c0) in enumerate(tiles):
                    for kc in range(nk):
                        k.mm(banks[ti][0:tp, :], hT[:, k0 + kc, c0:c0 + tp], w[:, kc, :], start=(k0 + kc == 0), stop=(k0 + kc == 21))
            for ti, (x, tp, c0) in enumerate(tiles):
                xs = x[0:tp, half * 512:(half + 1) * 512]
                k.stt(xs, xs, ALPHA, banks[ti][0:tp, :], ALU.mult, ALU.add)
        self.load_ln(d["rvec"][l][2048:4096])
        for (x, tp, c0) in tiles:
            self.layernorm(x, tp)

    def build(self):
        k = self.k
        self.declare()
        self.alloc()
        s, d, CW, NT, T, L = self.s, self.d, self.CW, self.NT, self.T, self.L
        k.dma("sp", s["consts"], d["consts"])
        for g in range(3):
            k.dma("pool", s["mw"][g], d[f"mw{g}"])
        k.copy("dve", s["ident_b"], s["consts"][:, 0, :])
        k.copy("dve", s["ones_b"], s["consts"][:, 4, :])
        k.memset("pool", s["zeros_b"], 0.0)
        self.cp = s["consts"]
        if self.with_samples:
            k.dma("sp", s["consts_s"], d["consts_s"])
            k.dma("sp", s["sel"], d["sel"])
            k.dma("pool", s["smask"][0:64], d["smask"])
            k.dma("pool", s["m0"], d["m0"])
            k.dma("pool", s["zsel"], d["zsel"])
            k.dma("pool", s["segm"], d["segm"])
        self.dbg_tile = 0
        self.cast_weights(0)
        for l in range(L):
            self.cur_l = l
            if l + 1 < L:
                self.cast_weights(l + 1)
            self.layer_setup(l)
            for c in range(self.NCH):
                self.cur_c = c
                self.prompt_chunk(l, c)
            self.prompt_layer_outputs(l)
            if self.with_samples:
                self.sample_chunk(l)
        k.finish()

    def prompt_chunk(self, l, c):
        k, s, d, CW, NT, T, L = self.k, self.s, self.d, self.CW, self.NT, self.T, self.L
        self.phase("TMC")
        self.load_chunk(l, c)
        cpre = self.tmp("cpre", [128, 8, 3 + CW], F32, n=1)
        hs = []

        def gen_c():
            k.copy("act", cpre[:, 0:4, 0:3], s["chalo"])
            yield from self.proj_fm(l, CW, range(3, 5), lambda blk: cpre[:, blk - 12, 3:3 + CW], c == 0)
            k.copy("act", s["chalo"], cpre[:, 0:4, CW:CW + 3])
            yield from self.rglru(CW, lambda blk, j: cpre[:, blk, j:j + CW], lambda blk: cpre[:, 4 + blk, 3:3 + CW],
                                  lambda blk: s["hlast"][:, blk:blk + 1], lambda blk: s["oT"][:, 6 + blk, :], hs=hs)
            for blk in range(4):
                k.copy("act", s["hlast"][:, blk:blk + 1], hs[blk][:, CW - 1:CW])

        self.run_gens([self.proj_tm(l, c), gen_c()])
        self.phase("AB")
        apre0 = self.tmp("apre", [128, 12, 3 + CW], F32, n=1)
        apre_all = apre0.subs(range(12))
        apre = lambda blk: apre0.sub(blk)
        pv = s["pvec"]
        k.copy("act", apre_all[:, :, 0:3], s["ahalo"])
        for _ in self.proj_fm(l, CW, range(0, 3), lambda blk: apre(blk)[:, blk, 3:3 + CW], c == 0):
            pass
        k.copy("act", s["ahalo"], apre_all[:, :, CW:CW + 3])

        def gen_a():
            qf = self.tmp("qkvf", [128, 12, 128], F32, n=1).re("p a b -> p (a b)")
            for blk in range(12):
                acc = qf[:, (blk % 2) * 512:(blk % 2) * 512 + CW]
                ab = apre(blk)
                k.ts("dve", acc, ab[:, blk, 0:CW], pv[:, PV_ACW + blk * 4:PV_ACW + blk * 4 + 1], None, op0=ALU.mult)
                for j in range(1, 4):
                    k.stt(acc, ab[:, blk, j:j + CW], pv[:, PV_ACW + blk * 4 + j:PV_ACW + blk * 4 + j + 1], acc, ALU.mult, ALU.add)
                k.act(ab[:, blk, 3:3 + CW], acc, AF.Silu)
                yield
            for i in range(NT):
                yield from self.deltanet_tile(None, 128, 128, self.cp, s["ba"][i], s["zs"][i], s["oT"][:, 0:4, i * 128:(i + 1) * 128],
                                              Model.state_prompt, 7, post=lambda blk, i=i: apre(blk)[:, blk, 3 + i * 128:3 + (i + 1) * 128])

        self.run_gens([gen_a(), self.attention_chunk(c)], weights=[1, 2 if c >= 2 else 1])
        tiles = [(s["xres"][i], 128, i * 128) for i in range(NT)]
        if getattr(self, "debug", False):
            self.phase("DBG")
            dbg = self.tmp("dbgo", [128, 10, CW], F32, n=1)
            k.copy("act", dbg, s["oT"])
            k.dma("sp", d["dbg_oT"][l][:, :, c * CW:(c + 1) * CW], dbg, is_output=True)
        self.phase("M")
        s["mergedT"] = self.tmp("mergedT", [128, 8, CW], BF16, n=1)
        self.merge(l, CW, tiles)
        self.phase("F")
        s["hT"] = self.tmp("hT", [128, 22, CW], BF16, n=1)

        def gwin(blk, gp, gbank):
            k.copy("act", gp[:, 0:2], s["fh"][:, blk, :])
            k.copy("act", gp[:, 2:2 + CW], gbank[:, 0:CW])
            return lambda j: gp[:, j:j + CW]

        def halo_update(blk, gp):
            k.copy("act", s["fh"][:, blk, :], gp[:, CW:CW + 2])

        self.ffn(l, CW, tiles, gwin, halo_update)
        for i in range(NT):
            r0 = c * CW + i * 128
            if l == L - 1:
                k.dma("sp", d["yp"][r0:r0 + 128, :], s["xres"][i], is_output=True)
            else:
                k.dma("sp", self.xs_scr[r0:r0 + 128, :], s["xres"][i])

    def prompt_layer_outputs(self, l):
        k, s, d, CW = self.k, self.s, self.d, self.CW
        with self.nc.allow_non_contiguous_dma(reason="small state outputs"):
            k.dma("sp", d["a_conv_p"][l].rearrange("b p j -> p b j"), s["ahalo"], is_output=True)
            k.dma("sp", d["c_conv_p"][l].rearrange("b p j -> p b j"), s["chalo"], is_output=True)
            k.dma("sp", d["c_h_p"][l].rearrange("b p -> p b"), s["hlast"], is_output=True)
            k.dma("sp", d["f_conv_p"][l].rearrange("b p j -> p b j"), s["fh"], is_output=True)
        k.dma("sp", d["a_rec_p"][l].rearrange("h k v -> k h v"), s["S"], is_output=True)


    def sample_chunk(self, l):
        k, s, d, L, NS, TS = self.k, self.s, self.d, self.L, self.NSEQ, self.TS
        x = s["xres"][0]
        v3 = lambda a: a.re("p (s t) -> p s t", t=4)
        self.phase("STM")
        if l == 0:
            self.load_ln(d["rvec0"])
            k.dma("sp", x[0:TS], d["xs"])
            self.layernorm(x, TS)
        else:
            k.dma("sp", x[0:TS], self.xs_scr_s)
        k.dma("sp", s["cs"][0][0:TS], d["css"])
        self.to_xT(x, TS, 0)
        for name, off, ncols in TM_SLOTS:
            w = self.wload(("win", l, slice(0, D), slice(off, off + ncols)), 8, ncols)
            bk = self.bank()
            for kc in range(8):
                k.mm(bk[0:TS, 0:ncols], s["xT"][:, kc, 0:TS], w[:, kc, :], start=(kc == 0), stop=(kc == 7))
            if name == "z":
                k.act(s["zs"][0][0:TS], bk[0:TS], AF.Silu)
                continue
            t = self.tmp("tmq", [128, 512], F32, n=3)
            k.copy("act", t[0:TS, 0:ncols], bk[0:TS, 0:ncols])
            if name == "q01":
                self.rope(t, TS, 8, s["cs"][0])
                k.dma("sp", self.qs_scr[:, 0:512], t[0:TS, :])
                self.tm_to_T_s(t, 4, s["qT"][:, 0:4, 0:TS])
            elif name == "q2ba":
                k.copy("pool", s["ba"][0][0:TS], t[0:TS, 256:264])
                self.rope(t[:, 0:256], TS, 4, s["cs"][0])
                k.dma("sp", self.qs_scr[:, 512:768], t[0:TS, 0:256])
                self.tm_to_T_s(t, 2, s["qT"][:, 4:6, 0:TS])
            else:
                g = int(name[2])
                self.rope(t[:, 0:256], TS, 4, s["cs"][0])
                k.dma("sp", d[f"b{g}_s"][l], t[0:TS, :], is_output=True)
                self.tm_to_T_s(t, 2, s["kTn"][:, 2 * g:2 * g + 2, :])
                k.copy("pool", s["vns"][0:TS, g, :], t[0:TS, 256:512])
        self.phase("SA")
        apre = self.tmp("apres", [128, 12, NS, 7], F32, n=1)
        st3 = self.tmp("ast3", [128, 12, NS, 3], F32, n=1)
        k.dma("sp", st3, d["a_conv_s_in"][l])
        k.copy("pool", apre[:, :, :, 0:3], st3)
        for _ in self.proj_fm(l, TS, range(0, 3), lambda blk: apre[:, blk, :, 3:7], True, srcv=v3):
            pass
        k.copy("pool", st3, apre[:, :, :, 4:7])
        k.dma("sp", d["a_conv_s"][l].rearrange("b p s j -> p b s j"), st3, is_output=True)
        self.cur_l = l
        for _ in self.deltanet_tile(lambda blk, j: apre[:, blk, :, j:j + 4], TS, 4, s["consts_s"], s["ba"][0], s["zs"][0],
                                    s["oT"][:, 0:4, 0:TS], Model.state_sample, 2):
            pass
        self.phase("SB")
        self.sample_attention(l)
        self.phase("SC")
        cpre = self.tmp("cpres", [128, 8, NS, 7], F32, n=1)
        ct3 = self.tmp("cst3", [128, 4, NS, 3], F32, n=1)
        h0 = self.tmp("ch0", [128, 4, NS], F32, n=1)
        k.dma("sp", ct3, d["c_conv_s_in"][l])
        k.dma("sp", h0, d["c_h_s_in"][l])
        k.copy("pool", cpre[:, 0:4, :, 0:3], ct3)
        for _ in self.proj_fm(l, TS, range(3, 5), lambda blk: cpre[:, blk - 12, :, 3:7], True, srcv=v3):
            pass
        k.copy("pool", ct3, cpre[:, 0:4, :, 4:7])
        k.dma("sp", d["c_conv_s"][l].rearrange("b p s j -> p b s j"), ct3, is_output=True)

        def scan_fix(blk, a, bx):
            a3, b3 = v3(a), v3(bx)
            tmpf = self.tmp("sfix", [128, NS], F32, n=2)
            k.tt("pool", tmpf, a3[:, :, 0], h0[:, blk, :], ALU.mult)
            k.tt("pool", b3[:, :, 0], b3[:, :, 0], tmpf, ALU.add)
            k.memset("pool", a3[:, :, 0], 0.0)

        hs = []
        for _ in self.rglru(TS, lambda blk, j: cpre[:, blk, :, j:j + 4], lambda blk: cpre[:, 4 + blk, :, 3:7], None,
                            lambda blk: s["oT"][:, 6 + blk, 0:TS], scan_fix=scan_fix, v3=v3, hs=hs):
            pass
        hout = self.tmp("chout", [128, 4, NS], F32, n=1)
        for blk in range(4):
            k.copy("pool", hout[:, blk, :], v3(hs[blk])[:, :, 3])
        k.dma("sp", d["c_h_s"][l].rearrange("b p s -> p b s"), hout, is_output=True)
        tiles = [(x, TS, 0)]
        self.phase("M")
        s["mergedT"] = self.tmp("mergedT", [128, 8, self.CW], BF16, n=1)
        self.merge(l, TS, tiles)
        self.phase("F")
        s["hT"] = self.tmp("hT", [128, 22, self.CW], BF16, n=1)
        fhs = self.tmp("fhs", [128, 22, NS, 2], F32, n=1)
        fho = self.tmp("fho", [128, 22, NS, 2], F32, n=1)
        k.dma("sp", fhs, d["f_conv_s_in"][l])

        def gwin(blk, gp, gbank):
            gp3 = gp[:, 0:NS * 6].re("p (s c) -> p s c", c=6)
            k.copy("pool", gp3[:, :, 0:2], fhs[:, blk, :, :])
            k.copy("act", gp3[:, :, 2:6], v3(gbank[:, 0:TS]))
            self._gp3 = gp3
            return lambda j: gp3[:, :, j:j + 4]

        def halo_update(blk, gp):
            k.copy("pool", fho[:, blk, :, :], self._gp3[:, :, 4:6])

        self.f_v3 = v3
        self.ffn(l, TS, tiles, gwin, halo_update)
        self.f_v3 = None
        k.dma("sp", d["f_conv_s"][l].rearrange("b p s j -> p b s j"), fho, is_output=True)
        if l == L - 1:
            k.dma("sp", d["ys"], x[0:TS], is_output=True)
        else:
            k.dma("sp", self.xs_scr_s, x[0:TS])

    def tm_to_T_s(self, t, nblk, dst):
        k, TS = self.k, self.TS
        bk = self.bank()
        for j in range(nblk):
            k.tr(bk[:, j * 128:j * 128 + TS], t[0:TS, j * 128:(j + 1) * 128], self.const(0)[0:TS, 0:TS])
        k.copy("act", dst, bk.re("p (a b) -> p a b", a=4)[:, 0:nblk, 0:TS])

    @staticmethod
    def state_sample(self, tp, sl, P, vb, nwT, qdT, QKm, kdec, eGL, consts):
        k, s, d, l, NS = self.k, self.s, self.d, self.cur_l, self.NSEQ
        segb = s["segm"][:, None, :, :].bcast([128, 4, NS, tp])
        msk = self.tmp("dmsk", [128, 4, NS, 64], BF16, n=1)
        k.tt("dve", msk, nwT[:, :, None, 0:tp].bcast([128, 4, NS, tp]), segb, ALU.mult)
        zrhs = s["mw"][2][:, 0:512]
        vnk = self.bank(pin=True)
        vn = vnk.re("p (h e) -> p h e", h=4)
        k.mm(vnk[0:tp, :], s["zeros_b"][:, 0:tp], zrhs)
        for h in range(4):
            k.mm(vn[0:tp, h, :], P[0:tp, h, 0:tp], vb[0:tp, h, :], start=False, stop=False)
        src = lambda q: d["a_rec_s_in"][l][q].rearrange("h k v -> k h v")
        for q in range(NS):
            s32 = self.tmp("ds32", [128, 4, 128], F32, n=2)
            sb = self.tmp("dsb", [128, 4, 128], BF16, n=2)
            k.dma("sp", s32, src(q))
            k.copy("act" if q % 2 else "pool", sb, s32)
            for h in range(4):
                k.mm(vn[0:tp, h, :], msk[:, h, q, :], sb[:, h, :], start=False, stop=False)
        vnb = self.dn_scratch[:, 0:4, :]
        k.copy("act", vnb[0:tp], vn[0:tp])
        if self.debug and l == 0:
            dv = self.dn_sq[:, 0:4, :]
            k.copy("act", dv[0:tp], vn[0:tp])
            k.dma("sp", d["dbg_vn"], dv[0:tp], is_output=True)
            dv2 = self.dn_sq[:, 4:8, :]
            k.copy("act", dv2[0:tp], kdec[0:tp])
            k.dma("sp", d["dbg_kd"], dv2[0:tp], is_output=True)
        self.unpin(vnk)
        k.tt("dve", msk, qdT[:, :, None, 0:tp].bcast([128, 4, NS, tp]), segb, ALU.mult)
        obk = self.bank(pin=True)
        self.ob_bank = obk
        ob = obk.re("p (h e) -> p h e", h=4)
        k.mm(obk[0:tp, :], s["zeros_b"][:, 0:tp], zrhs)
        for h in range(4):
            k.mm(ob[0:tp, h, :], QKm[0:tp, h, 0:tp], vnb[0:tp, h, :], start=False, stop=False)
        for q in range(NS):
            s32 = self.tmp("ds32", [128, 4, 128], F32, n=2)
            sb = self.tmp("dsb", [128, 4, 128], BF16, n=2)
            k.dma("sp", s32, src(q))
            k.copy("act" if q % 2 else "pool", sb, s32)
            for h in range(4):
                k.mm(ob[0:tp, h, :], msk[:, h, q, :], sb[:, h, :], start=False, stop=False)
            kdm = self.tmp("dkdm", [128, 4, 128], BF16, n=2)
            k.ts("dve", kdm[0:tp], kdec[0:tp], s["sel"][0:tp, q:q + 1], None, op0=ALU.mult)
            dS = self.bank().re("p (h e) -> p h e", h=4)
            for h in range(4):
                k.mm(dS[:, h, :], kdm[0:tp, h, :], vnb[0:tp, h, :])
            sn = self.dn_sq[:, 4 * (q % 2):4 * (q % 2) + 4, :]
            k.tt("pool", sn, s32, eGL[:, :, q:q + 1].bcast([128, 4, 128]), ALU.mult)
            k.tt("dve", sn, sn, dS, ALU.add)
            k.dma("sp", d["a_rec_s"][l][q].rearrange("h k v -> k h v"), sn, is_output=True)
        return ob

    def sample_attention(self, l):
        k, s, d, NS, TS = self.k, self.s, self.d, self.NSEQ, self.TS
        acck = self.bank(pin=True)
        k.mm(acck[0:TS, 0:260], s["zeros_b"][:, 0:TS], s["mw"][2][:, 0:260])
        for q in range(NS):
            for g, (W, dil) in enumerate(B_GROUPS):
                kv = self.tmp("skv", [128, 4, 512], F32, n=3)
                qb = self.tmp("sqb", [128, 4, 256], F32, n=3)
                if g == 0:
                    k.dma("sp", kv[:, 0, :], d["cb0"][l][q])
                    kk_ = kv[:, 0:1, 0:256].bcast([128, 4, 256])
                    vv_ = kv[:, 0:1, 256:512].bcast([128, 4, 256])
                else:
                    k.dma("sp", kv, d[f"cb{g}"][l][q].rearrange("(m j) c -> m j c", j=dil)[:, 0:4, :])
                    kk_, vv_ = kv[:, :, 0:256], kv[:, :, 256:512]
                k.dma("sp", qb, self.qs_scr[4 * q:4 * q + 4, g * 256:(g + 1) * 256].pbcast(128))
                prod = self.tmp("sprod", [128, 4, 4, 64], F32, n=2)
                sp4 = lambda a: a.re("p t (h e) -> p t h e", e=64)
                k.tt("dve", prod, sp4(kk_), sp4(qb), ALU.mult)
                sc = self.tmp("ssc", [128, 16], F32, n=3)
                k.reduce("dve", sc, prod.re("p t h e -> p (t h) e"), ALU.add)
                pb = self.tmp("spb", [128, 16], BF16, n=3)
                k.act(pb, sc, AF.Exp, scale=0.125)
                if g == 0:
                    pb3 = pb.re("p (t h) -> p t h", h=4)
                    k.tt("pool", pb3, pb3, s["m0"][:, :, None].bcast([128, 4, 4]), ALU.mult)
                Wt = self.tmp("sW", [128, 4, 4, 64], BF16, n=3)
                k.tt("pool!" if (3 * q + g) % 2 else "dve", Wt, sp4(vv_), pb.re("p (t h) -> p t h", h=4)[:, :, :, None].bcast([128, 4, 4, 64]), ALU.mult)
                for t in range(4):
                    tok = 4 * q + t
                    E = s["zsel"][:, 63 - tok:127 - tok]
                    k.mm(acck[0:TS, 0:256], E, Wt[:, t, :, :].re("p h e -> p (h e)"), start=False, stop=False)
                    k.mm(acck[0:TS, 256:260], E, pb[:, 4 * t:4 * t + 4], start=False, stop=False)
        for g in range(3):
            for hh in range(4):
                pair, ph = hh // 2, hh % 2
                prt = slice(64 * ph, 64 * ph + 64)
                st = self.bank()
                k.mm(st[0:TS, 0:TS], s["kTn"][prt, 2 * g + pair, :], s["qT"][prt, 2 * g + pair, 0:TS])
                pn = self.tmp("spn", [128, 64], BF16, n=2)
                k.act(pn[0:TS], st[0:TS, 0:TS], AF.Exp, scale=0.125)
                k.tt("dve", pn[0:TS], pn[0:TS], s["smask"][0:TS, g, :], ALU.mult)
                k.mm(acck[0:TS, hh * 64:(hh + 1) * 64], pn[0:TS], s["vns"][0:TS, g, hh * 64:(hh + 1) * 64], start=False, stop=False)
                k.mm(acck[0:TS, 256 + hh:257 + hh], pn[0:TS], s["ones_b"][0:TS, 0:1], start=False, stop=False)
        rd = self.tmp("srd", [128, 4], F32, n=1)
        k.recip(rd[0:TS], acck[0:TS, 256:260])
        ob = self.tmp("sob", [128, 4, 64], F32, n=1)
        k.tt("dve", ob[0:TS], acck[0:TS, 0:256].re("p (h e) -> p h e", e=64), rd[0:TS, :, None].bcast([TS, 4, 64]), ALU.mult)
        self.unpin(acck)
        ob2 = ob.re("p h e -> p (h e)")
        bk = self.bank()
        for j in range(2):
            k.tr(bk[:, j * 128:j * 128 + TS], ob2[0:TS, j * 128:(j + 1) * 128], self.const(0)[0:TS, 0:TS])
        k.copy("act", s["oT"][:, 4:6, 0:TS], bk.re("p (a b) -> p a b", a=4)[:, 0:2, 0:TS])


def _rope_table(pos):
    half = 8
    inv = (500000.0 ** (-np.arange(half, dtype=np.float32) / half)).astype(np.float32)
    ang = pos.astype(np.float32)[:, None] * inv[None, :]
    return np.concatenate([np.cos(ang), np.sin(ang)], axis=1).astype(np.float32)


def _mask_w(W, dil):
    kk = np.arange(128)[:, None]
    u = np.arange(W + 128)[None, :]
    dd = u - kk
    return ((dd >= 0) & (dd <= W) & (dd % dil == 0)).astype(np.float32)


def _consts():
    i = np.arange(128)
    c = np.zeros((128, 6, 128), np.float32)
    c[:, 0] = np.eye(128)
    c[:, 1] = (i[:, None] <= i[None, :])
    c[:, 2] = (i[:, None] < i[None, :])
    c[:, 3] = (i[:, None] > i[None, :])
    c[:, 4] = 1.0
    c[:, 5] = 1.0
    return c


def prep_shared(inp, L):
    f = lambda a: np.ascontiguousarray(np.asarray(a, dtype=np.float32))
    w_in = f(inp["w_in"])[:L]
    sp = np.cumsum([0, 1536, 4, 4, 512, 2304, 512, 512, 3072])
    qkv_a, b_a, a_a, z_a, qkv_b, x_c, g_c, gates = [w_in[:, :, sp[i]:sp[i + 1]] for i in range(8)]
    qb, kb, vb = qkv_b[..., 0:768], qkv_b[..., 768:1536], qkv_b[..., 1536:2304]
    gsl = lambda a, g: a[..., g * 256:(g + 1) * 256]
    win = np.concatenate([qkv_a, x_c, g_c, gates, z_a, gsl(qb, 0), gsl(qb, 1), gsl(qb, 2), b_a, a_a,
                          gsl(kb, 0), gsl(vb, 0), gsl(kb, 1), gsl(vb, 1), gsl(kb, 2), gsl(vb, 2)], axis=-1)
    assert win.shape[-1] == N_INP
    out = {"win": f(win)}
    out["wpa"], out["wpb"], out["wpc"], out["wo"] = f(inp["w_pa"])[:L], f(inp["w_pb"])[:L], f(inp["w_pc"])[:L], f(inp["w_o"])[:L]
    out["fup"], out["fdown"] = f(inp["f_up"])[:L], f(inp["f_down"])[:L]
    wbd = np.zeros((L, 128, 8, 128), np.float32)
    for ri, nm in enumerate(("c_w_r", "c_w_i")):
        w = f(inp[nm])[:L]
        for blk in range(4):
            for hb in range(2):
                wbd[:, hb * 64:(hb + 1) * 64, ri * 4 + blk, hb * 64:(hb + 1) * 64] = w[:, blk * 2 + hb]
    out["wbd"] = wbd
    out["rvec"] = f(np.concatenate([inp["ln1_g"][:L], inp["ln1_b"][:L], inp["ln2_g"][:L], inp["ln2_b"][:L],
                                    inp["a_norm_w"][:L], inp["a_A_log"][:L], inp["a_dt_bias"][:L]], axis=1))
    out["rvec0"] = f(np.concatenate([inp["ln_in_g"], inp["ln_in_b"]]))
    pm = lambda a, nb: np.moveaxis(f(a)[:L].reshape(L, -1, nb, 128), 3, 1)

    def pp(a, nb):
        a = f(a)[:L]
        J = a.shape[1]
        return a.reshape(L, J, nb, 128).transpose(0, 3, 2, 1).reshape(L, 128, nb * J)

    def p1(a, nb):
        return f(a)[:L].reshape(L, nb, 128).transpose(0, 2, 1)

    out["pvec"] = f(np.concatenate([pp(inp["a_conv_w"], 12), pp(inp["c_conv_w"], 4), p1(inp["c_conv_b"], 4), p1(inp["c_b_r"], 4),
                                    p1(inp["c_b_i"], 4), p1(inp["c_lam"], 4), pp(inp["f_conv_w"], 22), p1(inp["f_conv_b"], 22)], axis=2))
    assert out["pvec"].shape[2] == NPV
    out["consts"] = _consts()
    for g, (W, dil) in enumerate(B_GROUPS):
        out[f"mw{g}"] = _mask_w(W, dil)
    return out


_PROG_CACHE = {}


def run_model(inp, T, NSEQ, L, n_cores, xp_list, with_samples=False, debug=False):
    key = (T, NSEQ, L, with_samples, debug)
    if key not in _PROG_CACHE:
        _PROG_CACHE[key] = Model(T, NSEQ, L, with_samples, debug)
    m = _PROG_CACHE[key]
    sh = prep_shared(inp, L)
    sh["csp"] = _rope_table(np.arange(T))
    in_maps = []
    for c in range(n_cores):
        im = dict(sh)
        im["xp"] = np.ascontiguousarray(xp_list[c], dtype=np.float32)
        in_maps.append(im)
    res = run_bass_kernel_spmd(m.nc, in_maps, core_ids=list(range(n_cores)))
    return m, res.results


def assemble_prompt(m, results, cores):
    L, T = m.L, m.T
    R = [results[c] for c in cores]
    st = lambda nm: np.stack([r[nm] for r in R], axis=1)
    yp = np.stack([r["yp"] for r in R], axis=0)
    a_conv = st("a_conv_p").transpose(0, 1, 4, 2, 3).reshape(L, len(R), 3, 1536)
    a_rec = st("a_rec_p")
    bs = [st(f"b{g}_p").reshape(L, len(R), m.Weff[g], 2, 4, 64) for g in range(3)]
    c_conv = st("c_conv_p").transpose(0, 1, 4, 2, 3).reshape(L, len(R), 3, 512)
    c_h = st("c_h_p").reshape(L, len(R), 512)
    f_conv = st("f_conv_p").transpose(0, 1, 4, 2, 3).reshape(L, len(R), 2, D_FF)
    return [yp, a_conv, a_rec, bs[0], bs[1], bs[2], c_conv, c_h, f_conv]


PAST_LEN = 2048


def sample_consts():
    i = np.arange(128)
    valid = (i[:, None] < 64) & (i[None, :] < 64)
    same = ((i[:, None] // 4) == (i[None, :] // 4)) & valid
    c = np.zeros((128, 6, 128), np.float32)
    c[:, 0] = np.eye(128)
    c[:, 1] = same & (i[:, None] <= i[None, :])
    c[:, 2] = same & (i[:, None] < i[None, :])
    c[:, 3] = same & (i[:, None] > i[None, :])
    c[:, 4] = 1.0
    c[:, 5] = same
    out = {"consts_s": c}
    j = np.arange(64)
    sm = np.zeros((64, 3, 64), np.float32)
    ss = (j[:, None] // 4) == (j[None, :] // 4)
    sm[:, 0] = ss & ((j[:, None] % 4) <= (j[None, :] % 4))
    sm[:, 1] = ss & ((j[:, None] % 4) == (j[None, :] % 4))
    sm[:, 2] = sm[:, 1]
    out["smask"] = sm
    out["m0"] = (i[:, None] >= np.arange(4)[None, :]).astype(np.float32)
    z = np.zeros((128, 127), np.float32)
    z[:, 63] = 1.0
    out["zsel"] = z
    out["segm"] = np.broadcast_to(((j[None, :] // 4) == np.arange(16)[:, None]).astype(np.float32)[None], (128, 16, 64)).copy()
    out["sel"] = (((i[:, None] // 4) == np.arange(16)[None, :]) & (i[:, None] < 64)).astype(np.float32)
    out["css"] = _rope_table(PAST_LEN + (j % 4))
    return out


def prep_samples(inp, L, seqs):
    f = lambda a: np.ascontiguousarray(a, dtype=np.float32)
    NS = len(seqs)
    o = {}
    o["xs"] = f(np.asarray(inp["x_sample"])[seqs].reshape(NS * 4, D))

    def fm(a, nb):
        a = np.asarray(a)[:L][:, seqs]
        J = a.shape[2]
        return f(a.reshape(L, NS, J, nb, 128).transpose(0, 4, 3, 1, 2))

    o["a_conv_s_in"] = fm(inp["state_a_conv"], 12)
    o["a_rec_s_in"] = f(np.asarray(inp["state_a_rec"])[:L][:, seqs])
    o["c_conv_s_in"] = fm(inp["state_c_conv"], 4)
    o["c_h_s_in"] = f(np.asarray(inp["state_c_h"])[:L][:, seqs].reshape(L, NS, 4, 128).transpose(0, 3, 2, 1))
    o["f_conv_s_in"] = fm(inp["state_f_conv"], 22)
    for g, nm in enumerate(("cache_b_w128", "cache_b_w512", "cache_b_w2048")):
        a = np.asarray(inp[nm])[:L][:, seqs]
        o[f"cb{g}"] = f(a.reshape(L, NS, a.shape[2], 512))
    return o


def assemble_samples(m, results, cores):
    L = m.L
    R = [results[c] for c in cores]
    cat = lambda nm, ax: np.concatenate([r[nm] for r in R], axis=ax)
    NS = 16 * len(R)
    ys = cat("ys", 0).reshape(NS, 4, D)
    a_conv = cat("a_conv_s", 3).transpose(0, 3, 4, 1, 2).reshape(L, NS, 3, 1536)
    a_rec = cat("a_rec_s", 1)
    bs = [cat(f"b{g}_s", 1).reshape(L, NS, 4, 2, 4, 64) for g in range(3)]
    c_conv = cat("c_conv_s", 3).transpose(0, 3, 4, 1, 2).reshape(L, NS, 3, 512)
    c_h = cat("c_h_s", 3).transpose(0, 3, 1, 2).reshape(L, NS, 512)
    f_conv = cat("f_conv_s", 3).transpose(0, 3, 4, 1, 2).reshape(L, NS, 2, D_FF)
    return [ys, a_conv, a_rec, bs[0], bs[1], bs[2], c_conv, c_h, f_conv]


def run_full(inp, T, L, n_cores, prompt_of_core, seqs_of_core, debug=False):
    key = (T, 16, L, True, debug)
    if key not in _PROG_CACHE:
        _PROG_CACHE[key] = Model(T, 16, L, True, debug)
    m = _PROG_CACHE[key]
    sh = prep_shared(inp, L)
    sh["csp"] = _rope_table(np.arange(T))
    sh.update(sample_consts())
    xp = np.asarray(inp["x_prompt"])
    in_maps = []
    for c in range(n_cores):
        im = dict(sh)
        im["xp"] = np.ascontiguousarray(xp[prompt_of_core[c]], dtype=np.float32)
        im.update(prep_samples(inp, L, seqs_of_core[c]))
        in_maps.append(im)
    res = run_bass_kernel_spmd(m.nc, in_maps, core_ids=list(range(n_cores)))
    return m, res.results


def kernel(**inputs):
    n = 8
    prompt_of_core = [c % 4 for c in range(n)]
    seqs_of_core = [list(range(16 * c, 16 * c + 16)) for c in range(n)]
    m, results = run_full(inputs, 4096, DEPTH, n, prompt_of_core, seqs_of_core)
    po = assemble_prompt(m, results, [0, 1, 2, 3])
    so = assemble_samples(m, results, list(range(n)))
    outs = [po[0], so[0]] + po[1:] + so[1:]
    return tuple(np.ascontiguousarray(o, dtype=np.float32) for o in outs)
```
